# Optimizing a Trainium2 kernel written in Bass

```python
import jax, jax.numpy as jnp
from jax import lax
import numpy as np

D_MODEL = 1024
BATCH = 4
SEQ = 4096
DEPTH = 2

N_A_LAYERS = DEPTH // 2
N_B_LAYERS = DEPTH - N_A_LAYERS
CHUNK = 128
A_EXPAND = 2
A_WIDTH = A_EXPAND * D_MODEL
A_GROUPS = 16
A_GROUP_DIM = A_WIDTH // A_GROUPS
HEAD_DIM = 64
N_Q_HEADS = D_MODEL // HEAD_DIM
N_KV_HEADS = max(1, N_Q_HEADS // 8)
Q_PER_KV = N_Q_HEADS // N_KV_HEADS
B_WIDTH = N_Q_HEADS * HEAD_DIM
KV_WIDTH = N_KV_HEADS * HEAD_DIM
WINDOW = 128
ROPE_THETA = 10000.0
EPS = 1e-5

kernel_name = "yoco_gmlp_swa_sink_hybrid"


def rms_norm(x, g):
    xf = x.astype(jnp.float32)
    y = xf * lax.rsqrt(jnp.mean(xf * xf, axis=-1, keepdims=True) + EPS) * g.astype(jnp.float32)
    return y.astype(x.dtype)


def layer_norm(x, g, b):
    xf = x.astype(jnp.float32)
    mu = jnp.mean(xf, axis=-1, keepdims=True)
    xc = xf - mu
    var = jnp.mean(xc * xc, axis=-1, keepdims=True)
    y = xc * lax.rsqrt(var + EPS) * g.astype(jnp.float32) + b.astype(jnp.float32)
    return y.astype(x.dtype)


def rotary(x, pos):
    dh = x.shape[-1]
    inv_freq = ROPE_THETA ** (-jnp.arange(0, dh, 2, dtype=jnp.float32) / dh)
    ang = pos[:, None] * inv_freq[None, :]
    cos = jnp.cos(ang)[None, :, None, :].astype(x.dtype)
    sin = jnp.sin(ang)[None, :, None, :].astype(x.dtype)
    x1, x2 = jnp.split(x, 2, axis=-1)
    return jnp.concatenate([x1 * cos - x2 * sin, x2 * cos + x1 * sin], axis=-1)


def band(t):
    b, s, h, d = t.shape
    blk = t.reshape(b, s // CHUNK, CHUNK, h, d)
    prev = jnp.concatenate([jnp.zeros_like(blk[:, :1]), blk[:, :-1]], axis=1)
    return jnp.concatenate([prev, blk], axis=2)


def gmlp_mixer(h, w_in, ln_g, ln_b, ws, bs, w_out):
    b, s, _ = h.shape
    nc = s // CHUNK
    z = h @ w_in
    u, v, g = jnp.split(z, 3, axis=-1)
    v = layer_norm(v, ln_g, ln_b)
    v = v.reshape(b, nc, CHUNK, A_GROUPS, A_GROUP_DIM)
    causal = jnp.tril(jnp.ones((CHUNK, CHUNK), dtype=bool))
    wsm = jnp.where(causal[None], ws, jnp.zeros_like(ws)).astype(v.dtype)
    sv = jnp.einsum('gts,bcsgd->bctgd', wsm, v) + bs.T[:, :, None].astype(v.dtype)
    sv = sv.reshape(b, s, A_WIDTH)
    y = u * sv * jax.nn.silu(g)
    return y @ w_out


def swa_mixer(h, k_band, v_band, pos, w_in, b_q, sinks, w_out):
    b, s, _ = h.shape
    nb = s // CHUNK
    z = h @ w_in
    q, g = jnp.split(z, 2, axis=-1)
    q = (q + b_q).reshape(b, s, N_Q_HEADS, HEAD_DIM)
    q = rotary(q, pos).reshape(b, nb, CHUNK, N_KV_HEADS, Q_PER_KV, HEAD_DIM)
    scores = jnp.einsum('bnqhrd,bnkhd->bnhrqk', q, k_band).astype(jnp.float32) * (HEAD_DIM ** -0.5)
    qi = jnp.arange(CHUNK)[:, None]
    kj = jnp.arange(2 * CHUNK)[None, :]
    rel = kj - CHUNK - qi
    in_window = (rel <= 0) & (rel > -WINDOW)
    key_valid = (jnp.arange(nb)[:, None] * CHUNK + jnp.arange(2 * CHUNK)[None, :] - CHUNK) >= 0
    mask = in_window[None] & key_valid[:, None, :]
    scores = jnp.where(mask[None, :, None, None], scores, -jnp.inf)
    sink = sinks.astype(jnp.float32).reshape(N_KV_HEADS, Q_PER_KV)[None, None, :, :, None, None]
    m = jnp.maximum(jnp.max(scores, axis=-1, keepdims=True), sink)
    p = jnp.exp(scores - m)
    denom = jnp.sum(p, axis=-1, keepdims=True) + jnp.exp(sink - m)
    p = (p / denom).astype(v_band.dtype)
    o = jnp.einsum('bnhrqk,bnkhd->bnqhrd', p, v_band).reshape(b, s, B_WIDTH)
    y = o * jax.nn.silu(g)
    return y @ w_out


def setup_inputs(seed: int = 0) -> dict:
    key = jax.random.key(seed)
    ks = jax.random.split(key, 20)
    f32 = jnp.float32
    nrm = lambda k, shp, sc: jax.random.normal(k, shp, f32) * sc
    return {
        "x": nrm(ks[0], (BATCH, SEQ, D_MODEL), 1.0),
        "a_norm_g": 1.0 + nrm(ks[1], (N_A_LAYERS, D_MODEL), 0.02),
        "a_w_in": nrm(ks[2], (N_A_LAYERS, D_MODEL, 3 * A_WIDTH), D_MODEL ** -0.5),
        "a_ln_g": 1.0 + nrm(ks[3], (N_A_LAYERS, A_WIDTH), 0.02),
        "a_ln_b": nrm(ks[4], (N_A_LAYERS, A_WIDTH), 0.02),
        "a_ws": nrm(ks[5], (N_A_LAYERS, A_GROUPS, CHUNK, CHUNK), 0.5 * CHUNK ** -0.5),
        "a_bs": 1.0 + nrm(ks[6], (N_A_LAYERS, A_GROUPS, CHUNK), 0.02),
        "a_w_out": nrm(ks[7], (N_A_LAYERS, A_WIDTH, D_MODEL), 0.5 * A_WIDTH ** -0.5),
        "kv_norm_g": 1.0 + nrm(ks[8], (D_MODEL,), 0.02),
        "w_kv": nrm(ks[9], (D_MODEL, 2 * KV_WIDTH), D_MODEL ** -0.5),
        "b_kv": nrm(ks[10], (2 * KV_WIDTH,), 0.02),
        "b_norm_g": 1.0 + nrm(ks[11], (N_B_LAYERS, D_MODEL), 0.02),
        "b_w_in": nrm(ks[12], (N_B_LAYERS, D_MODEL, 2 * B_WIDTH), D_MODEL ** -0.5),
        "b_bq": nrm(ks[13], (N_B_LAYERS, B_WIDTH), 0.02),
        "b_sinks": nrm(ks[14], (N_B_LAYERS, N_Q_HEADS), 1.0),
        "b_w_out": nrm(ks[15], (N_B_LAYERS, B_WIDTH, D_MODEL), B_WIDTH ** -0.5),
        "final_norm_g": 1.0 + nrm(ks[16], (D_MODEL,), 0.02),
    }


def reference(x, a_norm_g, a_w_in, a_ln_g, a_ln_b, a_ws, a_bs, a_w_out, kv_norm_g, w_kv, b_kv,
              b_norm_g, b_w_in, b_bq, b_sinks, b_w_out, final_norm_g):
    b, s, _ = x.shape
    pos = jnp.arange(s, dtype=jnp.float32)
    h = x
    k_band = None
    v_band = None
    for l in range(DEPTH):
        if l < N_A_LAYERS:
            i = l
            h = h + gmlp_mixer(rms_norm(h, a_norm_g[i]), a_w_in[i], a_ln_g[i], a_ln_b[i],
                               a_ws[i], a_bs[i], a_w_out[i])
        else:
            if l == N_A_LAYERS:
                kv = rms_norm(h, kv_norm_g) @ w_kv + b_kv
                k, v = jnp.split(kv, 2, axis=-1)
                k = rotary(k.reshape(b, s, N_KV_HEADS, HEAD_DIM), pos)
                v = v.reshape(b, s, N_KV_HEADS, HEAD_DIM)
                k_band = band(k)
                v_band = band(v)
            i = l - N_A_LAYERS
            h = h + swa_mixer(rms_norm(h, b_norm_g[i]), k_band, v_band, pos,
                              b_w_in[i], b_bq[i], b_sinks[i], b_w_out[i])
    return rms_norm(h, final_norm_g)
```

```python
from contextlib import ExitStack
import numpy as np
import concourse.bass as bass
import concourse.mybir as mybir
from concourse.bass_utils import run_bass_kernel_spmd

F32 = mybir.dt.float32
BF16 = mybir.dt.bfloat16
ALU = mybir.AluOpType
AF = mybir.ActivationFunctionType

ENGS = ["sync", "scalar", "vector", "gpsimd", "tensor"]
NCORES = 8
NCH = 17
D = 1024
AW = 2048
EPS = 1e-5
AV_ORDER = 1
AV_NORM = "pool"
AV_PARTS = 4
ROT_ENG = "gpsimd"
AV_SVBANK = "v"
ST_RING = 3
TRB = 5
TRB_HNB = 3
KZB = 6
KVB = 4
YTB = 7
WO_BANKS = (6, 7)
AV_VMM_ACT = 1
AV_VMM_BN = 1


class Res:
    __slots__ = ("name", "w", "r")

    def __init__(self, name):
        self.name = name
        self.w = None
        self.r = []


class Prog:
    def __init__(self):
        self.ops = {e: [] for e in ENGS}
        self.seen = {e: {} for e in ENGS}
        self.dcnt = {}
        self.nres = 0

    def res(self, name=None):
        self.nres += 1
        return Res(name or f"r{self.nres}")

    def _need(self, eng, deps, tok, raw):
        if tok is None:
            return
        key, seq, peng = tok
        if peng == eng and not raw:
            return
        if self.seen[eng].get(key, -1) >= seq:
            return
        self.seen[eng][key] = seq
        deps[key] = max(deps.get(key, -1), seq)

    def _deps(self, eng, reads, writes):
        deps = {}
        for r in reads:
            self._need(eng, deps, r.w, True)
        for w in writes:
            self._need(eng, deps, w.w, False)
            for t in w.r:
                self._need(eng, deps, t, False)
        return deps

    def _commit(self, tok, reads, writes):
        for r in reads:
            r.r.append(tok)
        for w in writes:
            w.w = tok
            w.r = []

    def op(self, eng, fn, reads=(), writes=(), signal=True):
        deps = self._deps(eng, reads, writes)
        seq = len(self.ops[eng])
        tok = ("E_" + eng, seq, eng)
        self._commit(tok, reads, writes)
        self.ops[eng].append(dict(fn=fn, deps=deps, sig_ok=signal, dma=None))

    def dma(self, eng, fn, sem, reads=(), writes=()):
        deps = self._deps(eng, reads, writes)
        key = "D_" + sem
        n = self.dcnt.get(key, 0) + 1
        self.dcnt[key] = n
        tok = (key, n, "dma:" + key)
        self._commit(tok, reads, writes)
        self.ops[eng].append(dict(fn=fn, deps=deps, sig_ok=False, dma=key))

    def fence(self):
        for e in ENGS:
            deps = {}
            for pe in ENGS:
                if pe != e:
                    last = [i for i, o in enumerate(self.ops[pe]) if o["sig_ok"]]
                    if last:
                        self._need(e, deps, ("E_" + pe, last[-1], pe), True)
            for k, n in self.dcnt.items():
                self._need(e, deps, (k, n, "dma:" + k), True)
            if deps:
                self.ops[e].append(dict(fn=None, deps=deps, sig_ok=False, dma=None))

    def emit(self, nc):
        needed = {e: set() for e in ENGS}
        sig_idx = {}
        for e in ENGS:
            idx = [i for i, o in enumerate(self.ops[e]) if o["sig_ok"]]
            sig_idx[e] = idx
        import bisect
        for e in ENGS:
            for o in self.ops[e]:
                nd = {}
                for k, seq in o["deps"].items():
                    if k.startswith("E_"):
                        pe = k[2:]
                        idx = sig_idx[pe]
                        p = bisect.bisect_left(idx, seq)
                        assert p < len(idx), ("no signalable op after", pe, seq)
                        s2 = idx[p]
                        needed[pe].add(s2)
                        nd[k] = s2
                    else:
                        nd[k] = seq
                o["deps"] = nd
        cnt_at = {}
        for e in ENGS:
            c = 0
            m = {}
            for i in range(len(self.ops[e])):
                if i in needed[e]:
                    c += 1
                    m[i] = c
            cnt_at[e] = m
        keys = ["E_" + e for e in ENGS if needed[e]] + sorted(self.dcnt.keys())
        with ExitStack() as es:
            sems = {k: es.enter_context(nc.semaphore(k)) for k in keys}
            block = es.enter_context(nc.Block())

            def mk(ename):
                def body(eng):
                    for i, o in enumerate(self.ops[ename]):
                        for k, v in o["deps"].items():
                            if k.startswith("E_"):
                                eng.wait_ge(sems[k], cnt_at[k[2:]][v])
                            else:
                                eng.wait_ge(sems[k], 16 * v)
                        if o["fn"] is None:
                            continue
                        ins = o["fn"](eng)
                        if o["dma"] is not None:
                            ins.then_inc(sems[o["dma"]], 16)
                        elif i in needed[ename]:
                            ins.then_inc(sems["E_" + ename], 1)
                return body

            for e in ENGS:
                if self.ops[e]:
                    getattr(block, e)(mk(e))
        return len(keys)


class Region:
    def __init__(self, t, nbytes):
        self.t = t
        self.nbytes = nbytes
        self.off = 0

    def reset(self):
        self.off = 0

    def take(self, dtype, shape):
        esz = 4 if dtype == F32 else 2
        n = 1
        for s in shape[1:]:
            n *= s
        nb = (n * esz + 31) // 32 * 32
        assert self.off + nb <= self.nbytes, (self.off, nb, self.nbytes)
        a = self.t[0:shape[0], self.off // 2:(self.off + n * esz) // 2]
        self.off += nb
        if dtype == F32:
            a = a.bitcast(F32)
        if len(shape) == 3:
            a = a.rearrange("p (a b) -> p a b", a=shape[1])
        elif len(shape) == 4:
            a = a.rearrange("p (a b c) -> p a b c", a=shape[1], b=shape[2])
        return a


def build(stage="full"):
    nc = bass.Bass("TRN2", target_bir_lowering=False)
    P = Prog()

    def din(name, shape):
        return nc.dram_tensor(name, list(shape), F32, kind="ExternalInput").ap()

    x = din("x", [NCH, 128, D])
    a_norm_g = din("a_norm_g", [D])
    a_w_in = din("a_w_in", [D, 3 * AW])
    a_ln_g = din("a_ln_g", [AW])
    a_ln_b = din("a_ln_b", [AW])
    a_ws = din("a_ws", [16, 128, 128])
    a_bs = din("a_bs", [AW])
    a_w_out = din("a_w_out", [AW, D])
    kv_norm_g = din("kv_norm_g", [D])
    w_kv = din("w_kv", [D, 256])
    b_kv = din("b_kv", [256])
    b_norm_g = din("b_norm_g", [D])
    b_w_in = din("b_w_in", [D, 2048])
    b_bq = din("b_bq", [D])
    b_sinks = din("b_sinks", [16])
    b_w_out = din("b_w_out", [D, D])
    final_norm_g = din("final_norm_g", [D])
    ident_d = din("ident", [128, 128])
    maskc_d = din("maskc", [128, 128])
    negc_d = din("negc", [128, 4, 128])
    negp_d = din("negp", [128, 4, 128])
    negp0_d = din("negp0", [128, 4, 128])
    cos_d = din("cos_t", [128, NCH, 32])
    sin_d = din("sin_t", [128, NCH, 32])
    out = nc.dram_tensor("out", [16, 128, D], F32, kind="ExternalOutput").ap()

    with ExitStack() as es:
        def sbt(name, nbytes):
            return Region(es.enter_context(nc.sbuf_tensor(name, [128, nbytes // 2], BF16)), nbytes)

        K = 1024
        SLOT = sbt("slot", NCH * 4 * K)
        RB = sbt("rb", 34 * K)
        RC = sbt("rc", 32 * K)
        RD = sbt("rd", 16 * K)
        RE = sbt("re", 4 * K)
        RF = sbt("rf", 22 * K)
        RW = sbt("rw", 24 * K)
        RM = sbt("rm", 5 * K + 512)
        PS = es.enter_context(nc.psum_tensor("ps", [128, 8 * 512], F32))
        bank_res = [P.res(f"bank{i}") for i in range(8)]

        def bank(i, n=1):
            return PS[:, i * 512:(i + n) * 512]

        def bank_bf(i):
            return PS[:, i * 512:(i + 1) * 512].bitcast(BF16)

        def mm(outap, lhsT, rhs, start, stop, reads, writes, signal):
            P.op("tensor", lambda e: e.matmul(outap, lhsT, rhs, start=start, stop=stop),
                 reads=reads, writes=writes, signal=signal)

        def tr(outap, inap, idap, reads, writes, signal):
            P.op("tensor", lambda e: e.transpose(outap, inap, idap),
                 reads=reads, writes=writes, signal=signal)

        def act(outap, inap, func, reads, writes, scale=1.0, bias=0.0, accum=None):
            if accum is None:
                P.op("scalar", lambda e: e.activation(outap, inap, func, bias=bias, scale=scale),
                     reads=reads, writes=writes)
            else:
                P.op("scalar", lambda e: e.activation(outap, inap, func, bias=bias, scale=scale, accum_out=accum),
                     reads=reads, writes=writes)

        def tt(eng, outap, a, b, op, reads, writes):
            P.op(eng, lambda e: e.tensor_tensor(outap, a, b, op), reads=reads, writes=writes)

        def stt(outap, a, sc, b, op0, op1, reads, writes):
            P.op("vector", lambda e: e.scalar_tensor_tensor(outap, a, sc, b, op0, op1),
                 reads=reads, writes=writes)

        def ts(eng, outap, a, s1, s2, op0, op1, reads, writes):
            if op1 is None:
                P.op(eng, lambda e: e.tensor_scalar(outap, a, s1, None, op0), reads=reads, writes=writes)
            else:
                P.op(eng, lambda e: e.tensor_scalar(outap, a, s1, s2, op0, op1), reads=reads, writes=writes)

        def cp(eng, outap, inap, reads, writes):
            if eng == "scalar":
                P.op("scalar", lambda e: e.copy(outap, inap), reads=reads, writes=writes)
            else:
                P.op(eng, lambda e: e.tensor_copy(outap, inap), reads=reads, writes=writes)

        def dma(eng, outap, inap, sem, reads=(), writes=(), slow=False):
            if slow:
                P.dma(eng, lambda e: e.dma_start(out=outap, in_=inap, allow_slow_non_contiguous=True),
                      sem, reads=reads, writes=writes)
            else:
                P.dma(eng, lambda e: e.dma_start(out=outap, in_=inap), sem, reads=reads, writes=writes)

        def rstd_from(outap, ssap, n, reads, writes):
            ts("vector", outap, ssap, 1.0 / n, EPS, ALU.mult, ALU.add, reads, writes)
            P.op("gpsimd", lambda e: e.tensor_tensor(outap, outap, negh, ALU.pow),
                 reads=list(writes) + [r_negh], writes=writes)

        ident = RM.take(BF16, [128, 128]); r_ident = P.res()
        maskc = RM.take(BF16, [128, 128]); r_maskc = P.res()
        ones_bf = RM.take(BF16, [128, 128]); r_ones = P.res()
        negc = RM.take(BF16, [128, 4, 128]); r_negc = P.res()
        negp = RM.take(BF16, [128, 4, 128]); r_negp = P.res()
        negp0 = RM.take(BF16, [128, 4, 128]); r_negp0 = P.res()
        lg_pp = RM.take(F32, [128, 16]); r_lg = P.res()
        lb_pp = RM.take(F32, [128, 16]); r_lb = P.res()
        sinkexp = RM.take(F32, [128, 16]); r_sink = P.res()
        stat = RM.take(F32, [128, 128])
        negh = RM.take(F32, [128, 1]); r_negh = P.res()
        vaug = [RM.take(BF16, [128, 2, 65]) for _ in range(3)]
        r_vaug = [P.res() for _ in range(3)]
        wkv = RE.take(BF16, [128, 8, 256]); r_wkv = P.res()

        dma("gpsimd", ident, ident_d, "c0", writes=[r_ident])
        dma("gpsimd", maskc, maskc_d, "c1", writes=[r_maskc])
        dma("sync", sinkexp, b_sinks.partition_broadcast(128), "c6", writes=[r_sink])
        act(sinkexp, sinkexp, AF.Exp, [r_sink], [r_sink])
        ts("vector", sinkexp, sinkexp, 2.0, None, ALU.mult, None, [r_sink], [r_sink])
        P.op("vector", lambda e: e.memset(ones_bf, 1.0), writes=[r_ones])
        P.op("vector", lambda e: e.memset(negh, -0.5), writes=[r_negh])
        for i in range(3):
            P.op("vector", (lambda v: (lambda e: e.memset(v, 1.0)))(vaug[i]), writes=[r_vaug[i]])

        ga_pp = RF.take(F32, [128, 8]); r_ga = P.res()
        Cc = RF.take(F32, [128, 16, 128]); r_C = P.res()
        wsT = RF.take(BF16, [128, 16, 128]); r_wsT = P.res()
        ga_bc = RF.take(F32, [128, 8, 128]); r_gabc = P.res()
        dma("sync", ga_bc, a_norm_g.partition_broadcast(128).rearrange("p (k q) -> p k q", k=8), "c7",
            writes=[r_gabc])
        wug0 = RF.take(BF16, [128, 8, 2, 128])
        RE.reset()
        wug1 = RE.take(BF16, [128, 8, 2, 128])

        RW.reset()
        xs = [RW.take(F32, [128, D]) for _ in range(2)]; r_xs = [P.res() for _ in range(2)]
        sq = RW.take(BF16, [128, D]); r_sq = P.res()
        hn = RW.take(BF16, [128, D]); r_hn = P.res()
        vn = [RW.take(BF16, [128, AW]) for _ in range(2)]; r_vn = [P.res() for _ in range(2)]
        ident_f = RW.take(F32, [128, 128]); r_identf = P.res()
        RD.reset()
        vraw = [RD.take(F32, [128, AW]) for _ in range(2)]
        r_vraw = [[P.res() for _ in range(4)] for _ in range(2)]
        ws_st = vraw[0].rearrange("p (g t) -> p g t", g=16); r_wsst_l = r_vraw[0]
        bs_bc = vraw[1].rearrange("p (g t) -> p g t", g=16); r_bs_l = r_vraw[1]
        dma("sync", ws_st, a_ws.rearrange("g t s -> t g s"), "c8", writes=r_wsst_l)
        dma("sync", bs_bc, a_bs.partition_broadcast(128).rearrange("p (g t) -> p g t", g=16), "c9", writes=r_bs_l)
        dma("sync", ident_f, ident_d, "c10", writes=[r_identf])
        tt("vector", ga_bc, ga_bc, ident_f.unsqueeze(1).to_broadcast([128, 8, 128]), ALU.mult,
           [r_gabc, r_identf], [r_gabc])
        P.op("vector", lambda e: e.reduce_sum(ga_pp, ga_bc, mybir.AxisListType.X), reads=[r_gabc], writes=[r_ga])
        dma("sync", lg_pp, a_ln_g.rearrange("(g d) -> d g", g=16), "c4", writes=[r_lg], slow=True)
        dma("sync", lb_pp, a_ln_b.rearrange("(g d) -> d g", g=16), "c5", writes=[r_lb], slow=True)

        Wv = RC.take(BF16, [128, 8, 2048])
        r_Wv = [P.res() for _ in range(4)]
        w_in_r = a_w_in.rearrange("(kc p) n -> p kc n", p=128)
        for s in range(4):
            dma("gpsimd", Wv[:, :, s * 512:(s + 1) * 512], w_in_r[:, :, AW + s * 512:AW + (s + 1) * 512],
                f"wv{s}", writes=[r_Wv[s]])

        dma("gpsimd", negc, negc_d, "c2", writes=[r_negc])
        dma("gpsimd", negp, negp_d, "c3", writes=[r_negp])
        dma("gpsimd", negp0, negp0_d, "c11", writes=[r_negp0])

        for gq in range(4):
            b = gq % 2
            for gi in range(4):
                g = gq * 4 + gi
                tr(bank(b)[:, gi * 128:(gi + 1) * 128], ws_st[:, g, :], ident_f,
                   r_wsst_l + [r_identf], [bank_res[b]], gi == 3)
            tt("vector", wsT[:, gq * 4:(gq + 1) * 4, :],
               bank(b).rearrange("p (a b) -> p a b", a=4),
               maskc.unsqueeze(1).to_broadcast([128, 4, 128]), ALU.mult,
               [bank_res[b], r_maskc], [r_wsT])
        for gq in range(4):
            b = 2 + gq % 2
            mm(bank(b), ones_bf, wsT[:, gq * 4:(gq + 1) * 4, :].rearrange("p a b -> p (a b)"), True, True,
               [r_ones, r_wsT], [bank_res[b]], True)
            for gi in range(4):
                g = gq * 4 + gi
                stt(Cc[:, g, :], bank(b)[:, gi * 128:(gi + 1) * 128], lb_pp[:, g:g + 1], bs_bc[:, g, :],
                    ALU.mult, ALU.add, [bank_res[b], r_lb] + r_bs_l, [r_C])
        if stage == "setup":
            dma("sync", out[0][:, 0:128], ident_f, "o0", reads=[r_identf])
            P.fence()
            P.emit(nc)
            return nc

        hnT = RB.take(BF16, [128, 8, NCH * 128])
        r_hnT = [P.res() for _ in range(NCH)]
        r_slot = [P.res() for _ in range(NCH)]

        def S_view(j):
            return SLOT.t[:, j * 2048:(j + 1) * 2048].rearrange("p (g t) -> p g t", g=16)

        def h1_view(j):
            return SLOT.t[:, j * 2048:(j + 1) * 2048].bitcast(F32)

        ssA = [stat[:, 0:1], stat[:, 1:2]]
        rstdA = [stat[:, 2:3], stat[:, 3:4]]
        r_ssA = [P.res(), P.res()]
        bnst = [stat[:, 8:32], stat[:, 32:56]]
        mv = [stat[:, 56:58], stat[:, 58:60]]
        rstdv = [stat[:, 60:61], stat[:, 61:62]]
        nmr = [stat[:, 62:63], stat[:, 63:64]]
        r_bn = [P.res(), P.res()]

        def av_front(j):
            p = j % 2
            xb = xs[p]; rxb = r_xs[p]
            dma("sync", xb, x[j], f"x{p}", writes=[rxb])
            act(sq, xb, AF.Square, [rxb], [r_sq, r_ssA[p]], accum=ssA[p])
            ts("vector", rstdA[p], ssA[p], 1.0 / D, EPS, ALU.mult, ALU.add, [r_ssA[p]], [r_ssA[p]])
            P.op("gpsimd", lambda e: e.tensor_tensor(rstdA[p], rstdA[p], negh, ALU.pow),
                 reads=[r_ssA[p], r_negh], writes=[r_ssA[p]])

        def av_front_a2(j):
            p = j % 2
            xb = xs[p]; rxb = r_xs[p]
            act(hn, xb, AF.Identity, [rxb, r_ssA[p]], [r_hn], scale=rstdA[p])

        def av_front_b(j):
            tb = bank_bf(4)
            for kc in range(8):
                tr(tb[:, kc * 128:(kc + 1) * 128], hn[:, kc * 128:(kc + 1) * 128], ident,
                   [r_hn, r_ident], [bank_res[4]], kc == 7)
            cp("scalar", hnT[:, :, j * 128:(j + 1) * 128], tb.rearrange("p (a b) -> p a b", a=8),
               [bank_res[4]], [r_hnT[j]])

        def av_vmm(j, slices=range(4)):
            p = j % 2
            for s in slices:
                if j == 0:
                    scale_wv(s)
                for kc in range(8):
                    mm(bank(s), hnT[:, kc, j * 128:(j + 1) * 128], Wv[:, kc, s * 512:(s + 1) * 512],
                       kc == 0, kc == 7, [r_hnT[j], r_Wv[s]], [bank_res[s]], kc == 7)
                if AV_VMM_ACT:
                    cp("scalar", vraw[p][:, s * 512:(s + 1) * 512], bank(s), [bank_res[s]], [r_vraw[p][s]])
                if AV_VMM_BN:
                    P.op("vector", (lambda o, i: (lambda e: e.bn_stats(o, i)))(bnst[p][:, s * 6:(s + 1) * 6],
                                                                               vraw[p][:, s * 512:(s + 1) * 512]),
                         reads=[r_vraw[p][s]], writes=[r_bn[p]])

        def av_mid(j):
            p = j % 2
            P.op("vector", lambda e: e.bn_aggr(mv[p], bnst[p]), reads=[r_bn[p]], writes=[r_bn[p]])
            ts("vector", rstdv[p], mv[p][:, 1:2], EPS, None, ALU.add, None, [r_bn[p]], [r_bn[p]])
            P.op("gpsimd", lambda e: e.tensor_tensor(rstdv[p], rstdv[p], negh, ALU.pow),
                 reads=[r_bn[p], r_negh], writes=[r_bn[p]])
            stt(nmr[p], mv[p][:, 0:1], -1.0, rstdv[p], ALU.mult, ALU.mult, [r_bn[p]], [r_bn[p]])
            if AV_NORM == "pool":
                ts("gpsimd", vn[p], vraw[p], rstdv[p], nmr[p], ALU.mult, ALU.add, r_vraw[p] + [r_bn[p]], [r_vn[p]])
            elif AV_NORM == "dve":
                ts("vector", vn[p], vraw[p], rstdv[p], nmr[p], ALU.mult, ALU.add, r_vraw[p] + [r_bn[p]], [r_vn[p]])
            else:
                act(vn[p], vraw[p], AF.Identity, r_vraw[p] + [r_bn[p]], [r_vn[p]], scale=rstdv[p], bias=nmr[p])

        sv_rot = [0]

        def av_back(j, gqs=range(4)):
            p = j % 2
            Sj = S_view(j)
            for gq in gqs:
                b = (5, 6, 7, 3)[gq] if AV_SVBANK == "v" else 5 + sv_rot[0] % 3
                sv_rot[0] += 1
                for gi in range(4):
                    g = gq * 4 + gi
                    mm(bank(b)[:, gi * 128:(gi + 1) * 128], vn[p][:, g * 128:(g + 1) * 128], wsT[:, g, :],
                       True, True, [r_vn[p], r_wsT], [bank_res[b]], gi == 3)
                for gi in range(4):
                    g = gq * 4 + gi
                    stt(Sj[:, g, :], bank(b)[:, gi * 128:(gi + 1) * 128], lg_pp[:, g:g + 1], Cc[:, g, :],
                        ALU.mult, ALU.add, [bank_res[b], r_lg, r_C], [r_slot[j]])

        r_wug = [P.res() for _ in range(4)]
        r_wug2 = [P.res() for _ in range(4)]
        wug_pre = [wug0, wug1]

        def load_wug_pre(g):
            dma("gpsimd", wug_pre[g][:, :, 0, :], w_in_r[:, :, g * 128:(g + 1) * 128], f"wugu{g}",
                writes=[r_wug[g]])
            dma("gpsimd", wug_pre[g][:, :, 1, :], w_in_r[:, :, 2 * AW + g * 128:2 * AW + (g + 1) * 128],
                f"wugg{g}", writes=[r_wug2[g]])

        def scale_wv(s_):
            for kc in range(8):
                act(Wv[:, kc, s_ * 512:(s_ + 1) * 512], Wv[:, kc, s_ * 512:(s_ + 1) * 512], AF.Identity,
                    [r_Wv[s_], r_ga], [r_Wv[s_]], scale=ga_pp[:, kc:kc + 1])

        def scale_wug(buf, k, kcs=range(8)):
            for kc in kcs:
                ts("vector", buf[:, kc, :, :], buf[:, kc, :, :], ga_pp[:, kc:kc + 1], None, ALU.mult, None,
                   [r_wug[k], r_wug2[k], r_ga], [r_wug[k], r_wug2[k]])

        if AV_ORDER == 1:
            av_front(0)
            av_front_a2(0)
            av_front_b(0)
            av_front(1)
            for j in range(NCH):
                if j + 2 < NCH:
                    av_front(j + 2)
                if j + 1 < NCH:
                    av_front_a2(j + 1)
                av_vmm(j)
                if j + 1 < NCH:
                    av_front_b(j + 1)
                if j >= 1:
                    av_back(j - 1)
                av_mid(j)
                if j == 3:
                    load_wug_pre(0)
                    load_wug_pre(1)
                if 6 <= j < 14:
                    q_ = j - 6
                    scale_wug(wug_pre[q_ // 4], q_ // 4, range((q_ % 4) * 2, (q_ % 4) * 2 + 2))
            av_back(NCH - 1)
        else:
            for j in range(NCH):
                av_front(j)
                av_front_a2(j)
                av_front_b(j)
                if AV_PARTS >= 2:
                    av_vmm(j)
                if AV_PARTS >= 3:
                    av_mid(j)
                if AV_PARTS >= 4:
                    av_back(j)
        if stage == "av":
            for j in range(1, NCH):
                dma("sync", out[j - 1], h1_view(j), "o0", reads=[r_slot[j]])
            P.fence()
            P.emit(nc)
            return nc
        AUG_PB = [6, 4, 0, 2]
        for (j0_, nj_), pb_ in zip([(0, 4), (4, 4)], AUG_PB[:2]):
            n_ = nj_ * 128
            rh_ = [r_hnT[j] for j in range(j0_, j0_ + nj_)]
            for kc in range(8):
                mm(bank(pb_)[:, 0:n_], wug0[:, kc, 0, :], hnT[:, kc, j0_ * 128:j0_ * 128 + n_],
                   kc == 0, kc == 7, rh_ + [r_wug[0]], [bank_res[pb_]], kc == 7)
            for kc in range(8):
                mm(bank(pb_ + 1)[:, 0:n_], wug0[:, kc, 1, :], hnT[:, kc, j0_ * 128:j0_ * 128 + n_],
                   kc == 0, kc == 7, rh_ + [r_wug2[0]], [bank_res[pb_ + 1]], kc == 7)
        P.fence()

        RC.reset()
        Wout = RC.take(BF16, [128, 16, D]); r_Wout = [P.res() for _ in range(2)]
        w_out_r = a_w_out.rearrange("(g p) n -> p g n", p=128)
        RW.reset()
        sgs = [RW.take(F32, [128, 512]) for _ in range(3)]; r_sgs = [P.res() for _ in range(3)]
        tus = [RW.take(F32, [128, 512]) for _ in range(3)]; r_tus = [P.res() for _ in range(3)]
        xs_o = [RW.take(F32, [128, D]) for _ in range(2)]; r_xso = [P.res() for _ in range(2)]
        RD.reset()
        wug = [wug0, wug1] + [RD.take(BF16, [128, 8, 2, 128]) for _ in range(2)]
        batches = [(0, 4), (4, 4), (8, 3), (11, 3), (14, 3)]
        S4 = SLOT.t[:, :].rearrange("p (j g t) -> p j g t", j=NCH, g=16)
        w_in_4 = a_w_in.rearrange("(kc p) (th n) -> p kc th n", p=128, th=3)

        def load_wug(g):
            dma("gpsimd", wug[g % 4][:, :, 0, :], w_in_r[:, :, g * 128:(g + 1) * 128], f"wugu{g % 4}",
                writes=[r_wug[g % 4]])
            dma("gpsimd", wug[g % 4][:, :, 1, :], w_in_r[:, :, 2 * AW + g * 128:2 * AW + (g + 1) * 128],
                f"wugg{g % 4}", writes=[r_wug2[g % 4]])

        it = 0
        for g in range(16):
            if g + 2 < 16:
                load_wug(g + 2)
            if 2 <= g + 1 < 16:
                scale_wug(wug[(g + 1) % 4], (g + 1) % 4)
            if g == 1:
                for hf in range(2):
                    dma("gpsimd", Wout[:, hf * 8:(hf + 1) * 8, :], w_out_r[:, hf * 8:(hf + 1) * 8, :],
                        f"wout{hf}", writes=[r_Wout[hf]])
            if g == 14:
                for jj in range(2):
                    dma("sync", xs_o[jj], x[jj], f"x{jj}", writes=[r_xso[jj]])
            wb = wug[g % 4]; rwb = r_wug[g % 4]; rwb2 = r_wug2[g % 4]
            for (j0, nj) in batches:
                n = nj * 128
                pb = AUG_PB[it % 4]
                pre_issued = it < 2
                sg = sgs[it % 3]; rsg = r_sgs[it % 3]
                tu = tus[it % 3]; rtu = r_tus[it % 3]
                it += 1
                rh = [r_hnT[j] for j in range(j0, j0 + nj)]
                rs = [r_slot[j] for j in range(j0, j0 + nj)]
                for kc in range(8):
                    if pre_issued:
                        break
                    mm(bank(pb)[:, 0:n], wb[:, kc, 0, :], hnT[:, kc, j0 * 128:j0 * 128 + n],
                       kc == 0, kc == 7, rh + [rwb], [bank_res[pb]], kc == 7)
                for kc in range(8):
                    if pre_issued:
                        break
                    mm(bank(pb + 1)[:, 0:n], wb[:, kc, 1, :], hnT[:, kc, j0 * 128:j0 * 128 + n],
                       kc == 0, kc == 7, rh + [rwb2], [bank_res[pb + 1]], kc == 7)
                act(sg[:, 0:n], bank(pb + 1)[:, 0:n], AF.Silu, [bank_res[pb + 1]], [rsg])
                tt("vector", tu[:, 0:n], bank(pb)[:, 0:n], sg[:, 0:n], ALU.mult, [bank_res[pb], rsg], [rtu])
                Sv = S4[:, j0:j0 + nj, g, :]
                tt("gpsimd", Sv, Sv, tu[:, 0:n].rearrange("p (j t) -> p j t", j=nj), ALU.mult,
                   rs + [rtu], rs)
        if stage == "aug":
            for j in range(1, NCH):
                dma("sync", out[j - 1], h1_view(j), "o0", reads=[r_slot[j]])
            P.fence()
            P.emit(nc)
            return nc
        P.fence()

        RF.reset()
        kvg_pp = RF.take(F32, [128, 8]); r_kvg = P.res()
        bg_pp = RF.take(F32, [128, 8]); r_bg = P.res()
        fg_bc = RF.take(F32, [128, D]); r_fg = P.res()
        bq_bc = RF.take(F32, [128, D]); r_bq = P.res()
        bkv_bc = RF.take(F32, [128, 256]); r_bkv = P.res()
        cos_t = RF.take(F32, [128, NCH, 32]); r_cos = P.res()
        sin_t = RF.take(F32, [128, NCH, 32]); r_sin = P.res()
        dma("sync", kvg_pp, kv_norm_g.rearrange("(kc p) -> p kc", p=128), "c4", writes=[r_kvg], slow=True)
        dma("sync", bg_pp, b_norm_g.rearrange("(kc p) -> p kc", p=128), "c5", writes=[r_bg], slow=True)
        dma("sync", fg_bc, final_norm_g.partition_broadcast(128), "c6", writes=[r_fg])
        dma("sync", bq_bc, b_bq.partition_broadcast(128), "c7", writes=[r_bq])
        dma("sync", bkv_bc, b_kv.partition_broadcast(128), "c8", writes=[r_bkv])
        dma("sync", cos_t, cos_d, "c9", writes=[r_cos])
        dma("sync", sin_t, sin_d, "c10", writes=[r_sin])

        RE.reset()
        wkv = RE.take(BF16, [128, 8, 256])
        dma("gpsimd", wkv, w_kv.rearrange("(kc p) n -> p kc n", p=128), "wkv", writes=[r_wkv])
        RB.reset()
        bwin = RB.take(BF16, [128, 8, 2048]); r_bwin = [P.res() for _ in range(4)]
        RD.reset()
        bwout = RD.take(BF16, [128, 8, D]); r_bwout = P.res()
        b_w_in_r = b_w_in.rearrange("(kc p) n -> p kc n", p=128)
        for s in range(4):
            dma("gpsimd", bwin[:, :, s * 512:(s + 1) * 512], b_w_in_r[:, :, s * 512:(s + 1) * 512],
                f"wv{s}", writes=[r_bwin[s]])
        dma("gpsimd", bwout, b_w_out.rearrange("(kc p) n -> p kc n", p=128), "wout0", writes=[r_bwout])
        def fold_gain(step):
            kc = step % 8
            if step < 8:
                ts("vector", wkv[:, kc, :], wkv[:, kc, :], kvg_pp[:, kc:kc + 1], None, ALU.mult, None,
                   [r_wkv, r_kvg], [r_wkv])
            else:
                ts("vector", bwin[:, kc, :], bwin[:, kc, :], bg_pp[:, kc:kc + 1], None, ALU.mult, None,
                   list(r_bwin) + [r_bg], list(r_bwin))
        for j in range(NCH):
            xb = xs_o[j % 2]; rxb = r_xso[j % 2]
            if j >= 2:
                dma("sync", xb, x[j], f"x{j % 2}", writes=[rxb])
            Sj = S_view(j)
            pb = 4 * (j % 2)
            for hf in range(2):
                b = pb + hf
                for g in range(16):
                    mm(bank(b), Sj[:, g, :], Wout[:, g, hf * 512:(hf + 1) * 512], g == 0, g == 15,
                       [r_slot[j], r_Wout[g // 8]], [bank_res[b]], g == 15)
            h1 = h1_view(j)
            for hf in range(2):
                tt("vector", h1[:, hf * 512:(hf + 1) * 512], bank(pb + hf), xb[:, hf * 512:(hf + 1) * 512],
                   ALU.add, [bank_res[pb + hf], rxb], [r_slot[j]])
            if j >= 1:
                fold_gain(j - 1)

        if stage == "h1":
            for j in range(1, NCH):
                dma("sync", out[j - 1], h1_view(j), "o0", reads=[r_slot[j]])
            P.fence()
            P.emit(nc)
            return nc
        P.fence()

        RC.reset(); RW.reset()
        hnkv = RC.take(BF16, [128, D]); r_hnkv = P.res()
        hnb = RC.take(BF16, [128, D]); r_hnb = P.res()
        hnkvT = RC.take(BF16, [128, 8, 128]); r_hnkvT = P.res()
        hnbT = RC.take(BF16, [128, 8, 128]); r_hnbT = P.res()
        qf = RC.take(F32, [128, D]); r_qf = P.res()
        qrot = RC.take(BF16, [128, D]); r_qrot = P.res()
        qT = RC.take(BF16, [128, 8, 128]); r_qT = P.res()
        sgB = [RC.take(F32, [128, D]) for _ in range(2)]; r_sgB = [P.res(), P.res()]
        PT = RC.take(BF16, [128, 2, 16, 128]); r_PT = [[P.res(), P.res()], [P.res(), P.res()]]
        on = qf; r_on = r_qf
        kvf = RW.take(F32, [128, 256]); r_kvf = P.res()
        kz = RW.take(BF16, [128, 2, 2, 128]); r_kz = P.res()
        kTz = [RW.take(BF16, [128, 2, 2, 128]) for _ in range(3)]; r_kTz = [P.res() for _ in range(3)]
        rt = [RW.take(F32, [128, 16, 32]) for _ in range(2)]; r_rt = [P.res() for _ in range(2)]
        rtk = [RW.take(F32, [128, 2, 32]) for _ in range(2)]; r_rtk = [P.res() for _ in range(2)]
        yb = RW.take(BF16, [128, D]); r_yb = P.res()
        yT = RW.take(BF16, [128, 8, 128]); r_yT = P.res()
        sqB = RW.take(BF16, [128, D]); r_sqB = P.res()
        rt2 = [RW.take(F32, [128, 16, 32]) for _ in range(2)]; r_rt2 = [P.res() for _ in range(2)]
        r_qrot_hi = P.res()
        ot = RW.take(F32, [128, D]); r_ot = P.res()
        ssB = [stat[:, 64:65], stat[:, 65:66]]
        rstdB = [stat[:, 66:67], stat[:, 67:68]]
        r_ssB = [P.res(), P.res()]
        ss2 = stat[:, 68:69]
        rstd2 = stat[:, 69:70]
        r_ss2 = P.res()
        den = stat[:, 72:88]; r_den = [P.res(), P.res()]
        r_on2 = [P.res(), P.res()]
        r_yb2 = [[P.res(), P.res()], [P.res(), P.res()]]
        ybs = [yb, hnb]
        P.op("vector", lambda e: e.memset(kz.rearrange("p a b c -> p (a b c)"), 0.0), writes=[r_kz])

        def rotary(eng, src, dst_lo, dst_hi, nh, j, rsrc, rdst, rt, r_rt):
            cb = cos_t[:, j, :].unsqueeze(1).to_broadcast([128, nh, 32])
            sb_ = sin_t[:, j, :].unsqueeze(1).to_broadcast([128, nh, 32])
            x1 = src[:, :, 0:32]
            x2 = src[:, :, 32:64]
            a, b = (rt[i][:, 0:nh, :] for i in range(2))
            tt(eng, a, x1, cb, ALU.mult, rsrc + [r_cos], [r_rt[0]])
            tt(eng, b, x2, sb_, ALU.mult, rsrc + [r_sin], [r_rt[1]])
            tt(eng, dst_lo, a, b, ALU.subtract, [r_rt[0], r_rt[1]], rdst)
            tt(eng, a, x2, cb, ALU.mult, rsrc + [r_cos], [r_rt[0]])
            tt(eng, b, x1, sb_, ALU.mult, rsrc + [r_sin], [r_rt[1]])
            tt(eng, dst_hi, a, b, ALU.add, [r_rt[0], r_rt[1]], rdst)

        def rotary_q(j, src, dst_lo, dst_hi, rsrc):
            nh = 16
            cb = cos_t[:, j, :].unsqueeze(1).to_broadcast([128, nh, 32])
            sb_ = sin_t[:, j, :].unsqueeze(1).to_broadcast([128, nh, 32])
            x1 = src[:, :, 0:32]
            x2 = src[:, :, 32:64]
            a, b = rt[0], rt[1]
            c, d_ = rt2[0], rt2[1]
            tt("gpsimd", a, x1, cb, ALU.mult, rsrc + [r_cos], [r_rt[0]])
            tt("vector", c, x2, cb, ALU.mult, rsrc + [r_cos], [r_rt2[0]])
            tt("gpsimd", b, x2, sb_, ALU.mult, rsrc + [r_sin], [r_rt[1]])
            tt("vector", d_, x1, sb_, ALU.mult, rsrc + [r_sin], [r_rt2[1]])
            tt("gpsimd", dst_lo, a, b, ALU.subtract, [r_rt[0], r_rt[1]], [r_qrot])
            tt("vector", dst_hi, c, d_, ALU.add, [r_rt2[0], r_rt2[1]], [r_qrot_hi])

        def b_a(j):
            p = j % 2
            h1 = h1_view(j)
            act(sqB, h1, AF.Square, [r_slot[j]], [r_sqB, r_ssB[p]], accum=ssB[p])
            ts("vector", rstdB[p], ssB[p], 1.0 / D, EPS, ALU.mult, ALU.add, [r_ssB[p]], [r_ssB[p]])
            P.op("gpsimd", lambda e: e.tensor_tensor(rstdB[p], rstdB[p], negh, ALU.pow),
                 reads=[r_ssB[p], r_negh], writes=[r_ssB[p]])

        def b_a2(j):
            p = j % 2
            h1 = h1_view(j)
            ts("vector", hnkv, h1, rstdB[p], None, ALU.mult, None, [r_slot[j], r_ssB[p]], [r_hnkv])

        def b_b_part(j, part):
            which, half = divmod(part, 4)
            if which == 1:
                return
            src, rsrc, bk, dst, rdst = ((hnkv, r_hnkv, 4, hnkvT, r_hnkvT), (hnb, r_hnb, TRB_HNB, hnbT, r_hnbT))[which]
            tb = bank_bf(bk)
            for kc in range(half * 2, half * 2 + 2):
                tr(tb[:, kc * 128:(kc + 1) * 128], src[:, kc * 128:(kc + 1) * 128], ident,
                   [rsrc, r_ident], [bank_res[bk]], kc == 7)
            if half == 3:
                cp("vector", dst, tb.rearrange("p (a b) -> p a b", a=8), [bank_res[bk]], [rdst])

        def b_b(j):
            for part in range(8):
                b_b_part(j, part)

        def b_c_kv(j):
            for kc in range(8):
                mm(bank(KVB)[:, 0:256], hnkvT[:, kc, :], wkv[:, kc, :], kc == 0, kc == 7,
                   [r_hnkvT, r_wkv], [bank_res[KVB]], kc == 7)
            tt("vector", kvf, bank(KVB)[:, 0:256], bkv_bc, ALU.add, [bank_res[KVB], r_bkv], [r_kvf])
            ksrc = kvf[:, 0:128].rearrange("p (h d) -> p h d", h=2)
            rotary("vector", ksrc, kz[:, :, 0, 0:32], kz[:, :, 0, 32:64], 2, j, [r_kvf], [r_kz], rtk, r_rtk)
            cp("vector", kz[:, :, 1, 64:128], kz[:, :, 0, 0:64], [r_kz], [r_kz])
            va = vaug[j % 3]; rva = r_vaug[j % 3]
            cp("vector", va[:, :, 0:64], kvf[:, 128:256].rearrange("p (h d) -> p h d", h=2), [r_kvf], [rva])

        def b_c_q(j):
            if j >= 1:
                for s in range(4):
                    for kc in range(8):
                        mm(bank(s), hnkvT[:, kc, :], bwin[:, kc, s * 512:(s + 1) * 512], kc == 0, kc == 7,
                           [r_hnkvT, r_bwin[s]], [bank_res[s]], kc == 7)
                sg = sgB[j % 2]; rsg = r_sgB[j % 2]
                for s in range(2):
                    tt("vector", qf[:, s * 512:(s + 1) * 512], bank(s), bq_bc[:, s * 512:(s + 1) * 512], ALU.add,
                       [bank_res[s], r_bq], [r_qf, r_on2[s]])
                for s in range(2):
                    act(sg[:, s * 512:(s + 1) * 512], bank(2 + s), AF.Tanh, [bank_res[2 + s]], [rsg], scale=0.5)
                for s in range(2):
                    stt(sg[:, s * 512:(s + 1) * 512], sg[:, s * 512:(s + 1) * 512], 1.0, bank(2 + s),
                        ALU.add, ALU.mult, [rsg, bank_res[2 + s]], [rsg])
                q3 = qf.rearrange("p (h d) -> p h d", h=16)
                qr3 = qrot.rearrange("p (h d) -> p h d", h=16)
                rotary_q(j, q3, qr3[:, :, 0:32], qr3[:, :, 32:64], [r_qf])

        def b_d(j):
            tb = bank_bf(KZB)
            for hk in range(2):
                for par in range(2):
                    c0 = (hk * 2 + par) * 128
                    tr(tb[:, c0:c0 + 128], kz[:, hk, par, :], ident, [r_kz, r_ident], [bank_res[KZB]],
                       hk == 1 and par == 1)
            cp("scalar", kTz[j % 3], tb[:, 0:512].rearrange("p (a b c) -> p a b c", a=2, b=2),
               [bank_res[KZB]], [r_kTz[j % 3]])
            if j >= 1:
                tb = bank_bf(TRB)
                for kc in range(8):
                    tr(tb[:, kc * 128:(kc + 1) * 128], qrot[:, kc * 128:(kc + 1) * 128], ident,
                       [r_qrot, r_qrot_hi, r_ident], [bank_res[TRB]], kc == 7)
                cp("vector", qT, tb.rearrange("p (a b) -> p a b", a=8), [bank_res[TRB]], [r_qT])

        st_rot = [0]

        def b_f(j, inter=None):
            npair = 0
            for hk in range(2):
                for kb in range(2):
                    jk = j - 1 + kb
                    kTk = kTz[jk % 3]; rkTk = r_kTz[jk % 3]
                    ng = negc if kb == 1 else (negp0 if j == 1 else negp)
                    rng_ = r_negc if kb == 1 else (r_negp0 if j == 1 else r_negp)
                    for par in range(2):
                        b = (8 - ST_RING) + st_rot[0] % ST_RING
                        st_rot[0] += 1
                        ob = bank(b).rearrange("p (a b) -> p a b", a=4)
                        mm(ob, ident, ng, True, False, [r_ident, rng_], [bank_res[b]], False)
                        mm(ob, kTk[:, hk, par, :], qT[:, hk * 4:(hk + 1) * 4, :], False, True,
                           [rkTk, r_qT], [bank_res[b]], True)
                        pv = PT[:, kb, hk * 8 + par:hk * 8 + 8:2, :]
                        act(pv, ob, AF.Exp, [bank_res[b]], [r_PT[kb][hk]], scale=0.125)
                        if inter is not None:
                            inter(npair)
                        npair += 1

        def b_g(j):
            o4 = PS[:, 0:2048].rearrange("p (h c) -> p h c", h=16)
            for h in range(16):
                hk = h // 8
                b = h // 4
                for kb in range(2):
                    jk = j - 1 + kb
                    mm(o4[:, h, 0:65], PT[:, kb, h, :], vaug[jk % 3][:, hk, :], kb == 0, kb == 1,
                       [r_PT[kb][hk], r_vaug[jk % 3]], [bank_res[b]], (kb == 1 and h % 4 == 3))
                if h % 8 == 7:
                    hh = h // 8
                    ro = [bank_res[2 * hh], bank_res[2 * hh + 1]]
                    hs = slice(hh * 8, hh * 8 + 8)
                    cs = slice(hh * 512, hh * 512 + 512)
                    dn = den[:, hs]
                    stt(dn, o4[:, hs, 64], 2.0, sinkexp[:, hs], ALU.mult, ALU.add, ro + [r_sink], [r_den[hh]])
                    P.op("vector", (lambda d_: (lambda e: e.reciprocal(d_, d_)))(dn), reads=[r_den[hh]], writes=[r_den[hh]])
                    tt("vector", on[:, cs].rearrange("p (h d) -> p h d", h=8), o4[:, hs, 0:64],
                       dn.unsqueeze(2).to_broadcast([128, 8, 64]), ALU.mult, ro + [r_den[hh]], [r_on2[hh], r_qf])
                    tt("gpsimd", ybs[j % 2][:, cs], on[:, cs], sgB[j % 2][:, cs], ALU.mult,
                       [r_on2[hh], r_sgB[j % 2]], [r_yb2[j % 2][hh]])

        def b_h(j):
            tb = bank_bf(YTB)
            for kc in range(8):
                tr(tb[:, kc * 128:(kc + 1) * 128], ybs[j % 2][:, kc * 128:(kc + 1) * 128], ident,
                   [r_yb2[j % 2][kc // 4], r_ident], [bank_res[YTB]], kc == 7)
            cp("scalar", yT, tb.rearrange("p (a b) -> p a b", a=8), [bank_res[YTB]], [r_yT])

        def b_i(j):
            h1 = h1_view(j)
            for hf in range(2):
                b = WO_BANKS[hf]
                for kc in range(8):
                    mm(bank(b), yT[:, kc, :], bwout[:, kc, hf * 512:(hf + 1) * 512], kc == 0, kc == 7,
                       [r_yT, r_bwout], [bank_res[b]], kc == 7)
            for hf in range(2):
                b = WO_BANKS[hf]
                tt("vector", h1[:, hf * 512:(hf + 1) * 512], bank(b), h1[:, hf * 512:(hf + 1) * 512], ALU.add,
                   [bank_res[b], r_slot[j]], [r_slot[j]])
            h2 = h1
            r_h2 = r_slot[j]
            act(sqB, h2, AF.Square, [r_h2], [r_sqB, r_ss2], accum=ss2)
            ts("vector", rstd2, ss2, 1.0 / D, EPS, ALU.mult, ALU.add, [r_ss2], [r_ss2])
            P.op("gpsimd", lambda e: e.tensor_tensor(rstd2, rstd2, negh, ALU.pow),
                 reads=[r_ss2, r_negh], writes=[r_ss2])
            stt(ot, h2, rstd2, fg_bc, ALU.mult, ALU.mult, [r_h2, r_ss2, r_fg], [r_ot])
            dma("sync", out[j - 1], ot, "o0", reads=[r_ot])

        b_a(0)
        b_a2(0)
        b_b(0)
        for i in range(NCH + 3):
            if i + 1 < NCH:
                b_a(i + 1)
            if i < NCH:
                b_c_kv(i)
            if i + 1 < NCH:
                b_a2(i + 1)
            if 1 <= i - 3 < NCH:
                b_i(i - 3)
            if i < NCH:
                b_c_q(i)
            if 1 <= i - 1 < NCH:
                if i + 1 < NCH:
                    b_f(i - 1, inter=(lambda jj: (lambda k: b_b_part(jj, k)))(i + 1))
                else:
                    b_f(i - 1)
                b_g(i - 1)
            elif i + 1 < NCH:
                b_b(i + 1)
            if 1 <= i - 2 < NCH:
                b_h(i - 2)
            if i < NCH:
                b_d(i)
        P.fence()
        P.emit(nc)
    return nc


def _host_inputs(inputs):
    x = np.ascontiguousarray(np.asarray(inputs["x"], dtype=np.float32))
    sq = lambda k: np.ascontiguousarray(np.asarray(inputs[k], dtype=np.float32))
    shared = {
        "a_norm_g": sq("a_norm_g")[0], "a_w_in": sq("a_w_in")[0], "a_ln_g": sq("a_ln_g")[0],
        "a_ln_b": sq("a_ln_b")[0], "a_ws": sq("a_ws")[0], "a_bs": sq("a_bs")[0].reshape(-1),
        "a_w_out": sq("a_w_out")[0], "kv_norm_g": sq("kv_norm_g"), "w_kv": sq("w_kv"), "b_kv": sq("b_kv"),
        "b_norm_g": sq("b_norm_g")[0], "b_w_in": sq("b_w_in")[0], "b_bq": sq("b_bq")[0],
        "b_sinks": sq("b_sinks")[0], "b_w_out": sq("b_w_out")[0], "final_norm_g": sq("final_norm_g"),
    }
    shared = {k: np.ascontiguousarray(v) for k, v in shared.items()}
    k_i = np.arange(128)[:, None]
    t_i = np.arange(128)[None, :]
    shared["ident"] = np.eye(128, dtype=np.float32)
    shared["maskc"] = (k_i <= t_i).astype(np.float32)
    NEG = np.float32(-30000.0)
    negc = np.where(k_i <= t_i, np.float32(0), NEG).astype(np.float32)
    negp = np.where(k_i > t_i, np.float32(0), NEG).astype(np.float32)
    rep4 = lambda m: np.ascontiguousarray(np.repeat(m[:, None, :], 4, axis=1))
    shared["negc"] = rep4(negc)
    shared["negp"] = rep4(negp)
    inv_freq = (10000.0 ** (-np.arange(0, 64, 2, dtype=np.float32) / 64)).astype(np.float32)
    in_maps = []
    for c in range(NCORES):
        b, hf = divmod(c, 2)
        xc = np.zeros((NCH, 128, D), np.float32)
        xc[1:] = x[b, hf * 2048:(hf + 1) * 2048].reshape(16, 128, D)
        if hf == 1:
            xc[0] = x[b, 2048 - 128:2048]
        pos = (hf * 2048 - 128 + np.arange(NCH * 128)).astype(np.float32)
        ang = pos[:, None] * inv_freq[None, :]
        cos_t = np.cos(ang).astype(np.float32).reshape(NCH, 128, 32).transpose(1, 0, 2)
        sin_t = np.sin(ang).astype(np.float32).reshape(NCH, 128, 32).transpose(1, 0, 2)
        m = dict(shared)
        m["x"] = xc
        m["cos_t"] = np.ascontiguousarray(cos_t)
        m["sin_t"] = np.ascontiguousarray(sin_t)
        m["negp0"] = shared["negp"] if hf == 1 else np.full((128, 4, 128), NEG, np.float32)
        in_maps.append(m)
    return in_maps


def run(inputs, stage="full"):
    in_maps = _host_inputs(inputs)
    nc = build(stage)
    res = run_bass_kernel_spmd(nc, in_maps, core_ids=list(range(NCORES)))
    outs = [np.asarray(r["out"]).reshape(2048, D) for r in res.results]
    full = np.stack([np.concatenate(outs[2 * b:2 * b + 2], axis=0) for b in range(4)], axis=0)
    return full.astype(np.float32)


def kernel(**inputs):
    return run(inputs, "full")
```

```python
from contextlib import ExitStack
import numpy as np
import concourse.bass as bass
import concourse.mybir as mybir
from concourse.bass_utils import run_bass_kernel_spmd

F32 = mybir.dt.float32
BF16 = mybir.dt.bfloat16
ALU = mybir.AluOpType
AF = mybir.ActivationFunctionType

ENGS = ["sync", "scalar", "vector", "gpsimd", "tensor"]
NCORES = 8
NCH = 17
D = 1024
AW = 2048
EPS = 1e-5
AV_ORDER = 1
AV_NORM = "pool"
AV_PARTS = 4
ROT_ENG = "gpsimd"
AV_SVBANK = "v"
ST_RING = 3
TRB = 5
TRB_HNB = 3
KZB = 6
KVB = 4
YTB = 7
WO_BANKS = (6, 7)
AV_VMM_ACT = 1
AV_VMM_BN = 1


class Res:
    __slots__ = ("name", "w", "r")

    def __init__(self, name):
        self.name = name
        self.w = None
        self.r = []


class Prog:
    def __init__(self):
        self.ops = {e: [] for e in ENGS}
        self.seen = {e: {} for e in ENGS}
        self.dcnt = {}
        self.nres = 0

    def res(self, name=None):
        self.nres += 1
        return Res(name or f"r{self.nres}")

    def _need(self, eng, deps, tok, raw):
        if tok is None:
            return
        key, seq, peng = tok
        if peng == eng and not raw:
            return
        if self.seen[eng].get(key, -1) >= seq:
            return
        self.seen[eng][key] = seq
        deps[key] = max(deps.get(key, -1), seq)

    def _deps(self, eng, reads, writes):
        deps = {}
        for r in reads:
            self._need(eng, deps, r.w, True)
        for w in writes:
            self._need(eng, deps, w.w, False)
            for t in w.r:
                self._need(eng, deps, t, False)
        return deps

    def _commit(self, tok, reads, writes):
        for r in reads:
            r.r.append(tok)
        for w in writes:
            w.w = tok
            w.r = []

    def op(self, eng, fn, reads=(), writes=(), signal=True):
        deps = self._deps(eng, reads, writes)
        seq = len(self.ops[eng])
        tok = ("E_" + eng, seq, eng)
        self._commit(tok, reads, writes)
        self.ops[eng].append(dict(fn=fn, deps=deps, sig_ok=signal, dma=None))

    def dma(self, eng, fn, sem, reads=(), writes=()):
        deps = self._deps(eng, reads, writes)
        key = "D_" + sem
        n = self.dcnt.get(key, 0) + 1
        self.dcnt[key] = n
        tok = (key, n, "dma:" + key)
        self._commit(tok, reads, writes)
        self.ops[eng].append(dict(fn=fn, deps=deps, sig_ok=False, dma=key))

    def fence(self):
        for e in ENGS:
            deps = {}
            for pe in ENGS:
                if pe != e:
                    last = [i for i, o in enumerate(self.ops[pe]) if o["sig_ok"]]
                    if last:
                        self._need(e, deps, ("E_" + pe, last[-1], pe), True)
            for k, n in self.dcnt.items():
                self._need(e, deps, (k, n, "dma:" + k), True)
            if deps:
                self.ops[e].append(dict(fn=None, deps=deps, sig_ok=False, dma=None))

    def emit(self, nc):
        needed = {e: set() for e in ENGS}
        sig_idx = {}
        for e in ENGS:
            idx = [i for i, o in enumerate(self.ops[e]) if o["sig_ok"]]
            sig_idx[e] = idx
        import bisect
        for e in ENGS:
            for o in self.ops[e]:
                nd = {}
                for k, seq in o["deps"].items():
                    if k.startswith("E_"):
                        pe = k[2:]
                        idx = sig_idx[pe]
                        p = bisect.bisect_left(idx, seq)
                        assert p < len(idx), ("no signalable op after", pe, seq)
                        s2 = idx[p]
                        needed[pe].add(s2)
                        nd[k] = s2
                    else:
                        nd[k] = seq
                o["deps"] = nd
        cnt_at = {}
        for e in ENGS:
            c = 0
            m = {}
            for i in range(len(self.ops[e])):
                if i in needed[e]:
                    c += 1
                    m[i] = c
            cnt_at[e] = m
        keys = ["E_" + e for e in ENGS if needed[e]] + sorted(self.dcnt.keys())
        with ExitStack() as es:
            sems = {k: es.enter_context(nc.semaphore(k)) for k in keys}
            block = es.enter_context(nc.Block())

            def mk(ename):
                def body(eng):
                    for i, o in enumerate(self.ops[ename]):
                        for k, v in o["deps"].items():
                            if k.startswith("E_"):
                                eng.wait_ge(sems[k], cnt_at[k[2:]][v])
                            else:
                                eng.wait_ge(sems[k], 16 * v)
                        if o["fn"] is None:
                            continue
                        ins = o["fn"](eng)
                        if o["dma"] is not None:
                            ins.then_inc(sems[o["dma"]], 16)
                        elif i in needed[ename]:
                            ins.then_inc(sems["E_" + ename], 1)
                return body

            for e in ENGS:
                if self.ops[e]:
                    getattr(block, e)(mk(e))
        return len(keys)


class Region:
    def __init__(self, t, nbytes):
        self.t = t
        self.nbytes = nbytes
        self.off = 0

    def reset(self):
        self.off = 0

    def take(self, dtype, shape):
        esz = 4 if dtype == F32 else 2
        n = 1
        for s in shape[1:]:
            n *= s
        nb = (n * esz + 31) // 32 * 32
        assert self.off + nb <= self.nbytes, (self.off, nb, self.nbytes)
        a = self.t[0:shape[0], self.off // 2:(self.off + n * esz) // 2]
        self.off += nb
        if dtype == F32:
            a = a.bitcast(F32)
        if len(shape) == 3:
            a = a.rearrange("p (a b) -> p a b", a=shape[1])
        elif len(shape) == 4:
            a = a.rearrange("p (a b c) -> p a b c", a=shape[1], b=shape[2])
        return a


def build(stage="full"):
    nc = bass.Bass("TRN2", target_bir_lowering=False)
    P = Prog()

    def din(name, shape):
        return nc.dram_tensor(name, list(shape), F32, kind="ExternalInput").ap()

    x = din("x", [NCH, 128, D])
    a_norm_g = din("a_norm_g", [D])
    a_w_in = din("a_w_in", [D, 3 * AW])
    a_ln_g = din("a_ln_g", [AW])
    a_ln_b = din("a_ln_b", [AW])
    a_ws = din("a_ws", [16, 128, 128])
    a_bs = din("a_bs", [AW])
    a_w_out = din("a_w_out", [AW, D])
    kv_norm_g = din("kv_norm_g", [D])
    w_kv = din("w_kv", [D, 256])
    b_kv = din("b_kv", [256])
    b_norm_g = din("b_norm_g", [D])
    b_w_in = din("b_w_in", [D, 2048])
    b_bq = din("b_bq", [D])
    b_sinks = din("b_sinks", [16])
    b_w_out = din("b_w_out", [D, D])
    final_norm_g = din("final_norm_g", [D])
    ident_d = din("ident", [128, 128])
    maskc_d = din("maskc", [128, 128])
    negc_d = din("negc", [128, 4, 128])
    negp_d = din("negp", [128, 4, 128])
    negp0_d = din("negp0", [128, 4, 128])
    cos_d = din("cos_t", [128, NCH, 32])
    sin_d = din("sin_t", [128, NCH, 32])
    out = nc.dram_tensor("out", [16, 128, D], F32, kind="ExternalOutput").ap()

    with ExitStack() as es:
        def sbt(name, nbytes):
            return Region(es.enter_context(nc.sbuf_tensor(name, [128, nbytes // 2], BF16)), nbytes)

        K = 1024
        SLOT = sbt("slot", NCH * 4 * K)
        RB = sbt("rb", 34 * K)
        RC = sbt("rc", 32 * K)
        RD = sbt("rd", 16 * K)
        RE = sbt("re", 4 * K)
        RF = sbt("rf", 22 * K)
        RW = sbt("rw", 24 * K)
        RM = sbt("rm", 5 * K + 512)
        PS = es.enter_context(nc.psum_tensor("ps", [128, 8 * 512], F32))
        bank_res = [P.res(f"bank{i}") for i in range(8)]

        def bank(i, n=1):
            return PS[:, i * 512:(i + n) * 512]

        def bank_bf(i):
            return PS[:, i * 512:(i + 1) * 512].bitcast(BF16)

        def mm(outap, lhsT, rhs, start, stop, reads, writes, signal):
            P.op("tensor", lambda e: e.matmul(outap, lhsT, rhs, start=start, stop=stop),
                 reads=reads, writes=writes, signal=signal)

        def tr(outap, inap, idap, reads, writes, signal):
            P.op("tensor", lambda e: e.transpose(outap, inap, idap),
                 reads=reads, writes=writes, signal=signal)

        def act(outap, inap, func, reads, writes, scale=1.0, bias=0.0, accum=None):
            if accum is None:
                P.op("scalar", lambda e: e.activation(outap, inap, func, bias=bias, scale=scale),
                     reads=reads, writes=writes)
            else:
                P.op("scalar", lambda e: e.activation(outap, inap, func, bias=bias, scale=scale, accum_out=accum),
                     reads=reads, writes=writes)

        def tt(eng, outap, a, b, op, reads, writes):
            P.op(eng, lambda e: e.tensor_tensor(outap, a, b, op), reads=reads, writes=writes)

        def stt(outap, a, sc, b, op0, op1, reads, writes):
            P.op("vector", lambda e: e.scalar_tensor_tensor(outap, a, sc, b, op0, op1),
                 reads=reads, writes=writes)

        def ts(eng, outap, a, s1, s2, op0, op1, reads, writes):
            if op1 is None:
                P.op(eng, lambda e: e.tensor_scalar(outap, a, s1, None, op0), reads=reads, writes=writes)
            else:
                P.op(eng, lambda e: e.tensor_scalar(outap, a, s1, s2, op0, op1), reads=reads, writes=writes)

        def cp(eng, outap, inap, reads, writes):
            if eng == "scalar":
                P.op("scalar", lambda e: e.copy(outap, inap), reads=reads, writes=writes)
            else:
                P.op(eng, lambda e: e.tensor_copy(outap, inap), reads=reads, writes=writes)

        def dma(eng, outap, inap, sem, reads=(), writes=(), slow=False):
            if slow:
                P.dma(eng, lambda e: e.dma_start(out=outap, in_=inap, allow_slow_non_contiguous=True),
                      sem, reads=reads, writes=writes)
            else:
                P.dma(eng, lambda e: e.dma_start(out=outap, in_=inap), sem, reads=reads, writes=writes)

        def rstd_from(outap, ssap, n, reads, writes):
            ts("vector", outap, ssap, 1.0 / n, EPS, ALU.mult, ALU.add, reads, writes)
            P.op("gpsimd", lambda e: e.tensor_tensor(outap, outap, negh, ALU.pow),
                 reads=list(writes) + [r_negh], writes=writes)

        ident = RM.take(BF16, [128, 128]); r_ident = P.res()
        maskc = RM.take(BF16, [128, 128]); r_maskc = P.res()
        ones_bf = RM.take(BF16, [128, 128]); r_ones = P.res()
        negc = RM.take(BF16, [128, 4, 128]); r_negc = P.res()
        negp = RM.take(BF16, [128, 4, 128]); r_negp = P.res()
        negp0 = RM.take(BF16, [128, 4, 128]); r_negp0 = P.res()
        lg_pp = RM.take(F32, [128, 16]); r_lg = P.res()
        lb_pp = RM.take(F32, [128, 16]); r_lb = P.res()
        sinkexp = RM.take(F32, [128, 16]); r_sink = P.res()
        stat = RM.take(F32, [128, 128])
        negh = RM.take(F32, [128, 1]); r_negh = P.res()
        vaug = [RM.take(BF16, [128, 2, 65]) for _ in range(3)]
        r_vaug = [P.res() for _ in range(3)]
        wkv = RE.take(BF16, [128, 8, 256]); r_wkv = P.res()

        dma("gpsimd", ident, ident_d, "c0", writes=[r_ident])
        dma("gpsimd", maskc, maskc_d, "c1", writes=[r_maskc])
        dma("sync", lg_pp, a_ln_g.rearrange("(g d) -> d g", g=16), "c4", writes=[r_lg], slow=True)
        dma("sync", lb_pp, a_ln_b.rearrange("(g d) -> d g", g=16), "c5", writes=[r_lb], slow=True)
        dma("sync", sinkexp, b_sinks.partition_broadcast(128), "c6", writes=[r_sink])
        act(sinkexp, sinkexp, AF.Exp, [r_sink], [r_sink])
        ts("vector", sinkexp, sinkexp, 2.0, None, ALU.mult, None, [r_sink], [r_sink])
        P.op("vector", lambda e: e.memset(ones_bf, 1.0), writes=[r_ones])
        P.op("vector", lambda e: e.memset(negh, -0.5), writes=[r_negh])
        for i in range(3):
            P.op("vector", (lambda v: (lambda e: e.memset(v, 1.0)))(vaug[i]), writes=[r_vaug[i]])

        ga_bc = RF.take(F32, [128, D]); r_ga = P.res()
        Cc = RF.take(F32, [128, 16, 128]); r_C = P.res()
        wsT = RF.take(BF16, [128, 16, 128]); r_wsT = P.res()
        dma("sync", ga_bc, a_norm_g.partition_broadcast(128), "c7", writes=[r_ga])
        wug0 = RF.take(BF16, [128, 8, 2, 128])
        RE.reset()
        wug1 = RE.take(BF16, [128, 8, 2, 128])

        RW.reset()
        xs = [RW.take(F32, [128, D]) for _ in range(2)]; r_xs = [P.res() for _ in range(2)]
        sq = RW.take(BF16, [128, D]); r_sq = P.res()
        hn = RW.take(BF16, [128, D]); r_hn = P.res()
        vn = [RW.take(BF16, [128, AW]) for _ in range(2)]; r_vn = [P.res() for _ in range(2)]
        ident_f = RW.take(F32, [128, 128]); r_identf = P.res()
        svtmp = [RW.take(F32, [128, 128]) for _ in range(4)]; r_svtmp = [P.res() for _ in range(4)]
        RD.reset()
        vraw = [RD.take(F32, [128, AW]) for _ in range(2)]
        r_vraw = [[P.res() for _ in range(4)] for _ in range(2)]
        ws_st = vraw[0].rearrange("p (g t) -> p g t", g=16); r_wsst_l = r_vraw[0]
        bs_bc = vraw[1].rearrange("p (g t) -> p g t", g=16); r_bs_l = r_vraw[1]
        dma("sync", ws_st, a_ws.rearrange("g t s -> t g s"), "c8", writes=r_wsst_l)
        dma("sync", bs_bc, a_bs.partition_broadcast(128).rearrange("p (g t) -> p g t", g=16), "c9", writes=r_bs_l)
        dma("sync", ident_f, ident_d, "c10", writes=[r_identf])

        Wv = RC.take(BF16, [128, 8, 2048])
        r_Wv = [P.res() for _ in range(4)]
        w_in_r = a_w_in.rearrange("(kc p) n -> p kc n", p=128)
        for s in range(4):
            dma("gpsimd", Wv[:, :, s * 512:(s + 1) * 512], w_in_r[:, :, AW + s * 512:AW + (s + 1) * 512],
                f"wv{s}", writes=[r_Wv[s]])

        dma("gpsimd", negc, negc_d, "c2", writes=[r_negc])
        dma("gpsimd", negp, negp_d, "c3", writes=[r_negp])
        dma("gpsimd", negp0, negp0_d, "c11", writes=[r_negp0])

        for gq in range(4):
            b = gq % 2
            for gi in range(4):
                g = gq * 4 + gi
                tr(bank(b)[:, gi * 128:(gi + 1) * 128], ws_st[:, g, :], ident_f,
                   r_wsst_l + [r_identf], [bank_res[b]], gi == 3)
            tt("vector", wsT[:, gq * 4:(gq + 1) * 4, :],
               bank(b).rearrange("p (a b) -> p a b", a=4),
               maskc.unsqueeze(1).to_broadcast([128, 4, 128]), ALU.mult,
               [bank_res[b], r_maskc], [r_wsT])
        for gq in range(4):
            b = 2 + gq % 2
            mm(bank(b), ones_bf, wsT[:, gq * 4:(gq + 1) * 4, :].rearrange("p a b -> p (a b)"), True, True,
               [r_ones, r_wsT], [bank_res[b]], True)
            for gi in range(4):
                g = gq * 4 + gi
                stt(Cc[:, g, :], bank(b)[:, gi * 128:(gi + 1) * 128], lb_pp[:, g:g + 1], bs_bc[:, g, :],
                    ALU.mult, ALU.add, [bank_res[b], r_lb] + r_bs_l, [r_C])
        if stage == "setup":
            dma("sync", out[0][:, 0:128], ident_f, "o0", reads=[r_identf])
            P.fence()
            P.emit(nc)
            return nc

        hnT = RB.take(BF16, [128, 8, NCH * 128])
        r_hnT = [P.res() for _ in range(NCH)]
        r_slot = [P.res() for _ in range(NCH)]

        def S_view(j):
            return SLOT.t[:, j * 2048:(j + 1) * 2048].rearrange("p (g t) -> p g t", g=16)

        def h1_view(j):
            return SLOT.t[:, j * 2048:(j + 1) * 2048].bitcast(F32)

        ssA = [stat[:, 0:1], stat[:, 1:2]]
        rstdA = [stat[:, 2:3], stat[:, 3:4]]
        r_ssA = [P.res(), P.res()]
        bnst = [stat[:, 8:32], stat[:, 32:56]]
        mv = [stat[:, 56:58], stat[:, 58:60]]
        rstdv = [stat[:, 60:61], stat[:, 61:62]]
        nmr = [stat[:, 62:63], stat[:, 63:64]]
        r_bn = [P.res(), P.res()]

        def av_front(j):
            p = j % 2
            xb = xs[p]; rxb = r_xs[p]
            dma("sync", xb, x[j], f"x{p}", writes=[rxb])
            act(sq, xb, AF.Square, [rxb], [r_sq, r_ssA[p]], accum=ssA[p])
            ts("vector", rstdA[p], ssA[p], 1.0 / D, EPS, ALU.mult, ALU.add, [r_ssA[p]], [r_ssA[p]])
            P.op("gpsimd", lambda e: e.tensor_tensor(rstdA[p], rstdA[p], negh, ALU.pow),
                 reads=[r_ssA[p], r_negh], writes=[r_ssA[p]])

        def av_front_a2(j):
            p = j % 2
            xb = xs[p]; rxb = r_xs[p]
            stt(hn, xb, rstdA[p], ga_bc, ALU.mult, ALU.mult, [rxb, r_ssA[p], r_ga], [r_hn])

        def av_front_b(j):
            tb = bank_bf(4)
            for kc in range(8):
                tr(tb[:, kc * 128:(kc + 1) * 128], hn[:, kc * 128:(kc + 1) * 128], ident,
                   [r_hn, r_ident], [bank_res[4]], kc == 7)
            cp("scalar", hnT[:, :, j * 128:(j + 1) * 128], tb.rearrange("p (a b) -> p a b", a=8),
               [bank_res[4]], [r_hnT[j]])

        def av_vmm(j, slices=range(4)):
            p = j % 2
            for s in slices:
                for kc in range(8):
                    mm(bank(s), hnT[:, kc, j * 128:(j + 1) * 128], Wv[:, kc, s * 512:(s + 1) * 512],
                       kc == 0, kc == 7, [r_hnT[j], r_Wv[s]], [bank_res[s]], kc == 7)
                if AV_VMM_ACT:
                    cp("scalar", vraw[p][:, s * 512:(s + 1) * 512], bank(s), [bank_res[s]], [r_vraw[p][s]])
                if AV_VMM_BN:
                    P.op("vector", (lambda o, i: (lambda e: e.bn_stats(o, i)))(bnst[p][:, s * 6:(s + 1) * 6],
                                                                               vraw[p][:, s * 512:(s + 1) * 512]),
                         reads=[r_vraw[p][s]], writes=[r_bn[p]])

        def av_mid(j):
            p = j % 2
            P.op("vector", lambda e: e.bn_aggr(mv[p], bnst[p]), reads=[r_bn[p]], writes=[r_bn[p]])
            ts("vector", rstdv[p], mv[p][:, 1:2], EPS, None, ALU.add, None, [r_bn[p]], [r_bn[p]])
            P.op("gpsimd", lambda e: e.tensor_tensor(rstdv[p], rstdv[p], negh, ALU.pow),
                 reads=[r_bn[p], r_negh], writes=[r_bn[p]])
            stt(nmr[p], mv[p][:, 0:1], -1.0, rstdv[p], ALU.mult, ALU.mult, [r_bn[p]], [r_bn[p]])
            if AV_NORM == "pool":
                ts("gpsimd", vn[p], vraw[p], rstdv[p], nmr[p], ALU.mult, ALU.add, r_vraw[p] + [r_bn[p]], [r_vn[p]])
            elif AV_NORM == "dve":
                ts("vector", vn[p], vraw[p], rstdv[p], nmr[p], ALU.mult, ALU.add, r_vraw[p] + [r_bn[p]], [r_vn[p]])
            else:
                act(vn[p], vraw[p], AF.Identity, r_vraw[p] + [r_bn[p]], [r_vn[p]], scale=rstdv[p], bias=nmr[p])

        sv_rot = [0]

        sv_tmp_rot = [0]

        def av_back(j, gqs=range(4)):
            p = j % 2
            Sj = S_view(j)
            for gq in gqs:
                b = (5, 6, 7, 3)[gq] if AV_SVBANK == "v" else 5 + sv_rot[0] % 3
                sv_rot[0] += 1
                for gi in range(4):
                    g = gq * 4 + gi
                    mm(bank(b)[:, gi * 128:(gi + 1) * 128], vn[p][:, g * 128:(g + 1) * 128], wsT[:, g, :],
                       True, True, [r_vn[p], r_wsT], [bank_res[b]], gi == 3)
                for gi in range(4):
                    g = gq * 4 + gi
                    if gq < 2:
                        stt(Sj[:, g, :], bank(b)[:, gi * 128:(gi + 1) * 128], lg_pp[:, g:g + 1], Cc[:, g, :],
                            ALU.mult, ALU.add, [bank_res[b], r_lg, r_C], [r_slot[j]])
                    else:
                        k = sv_tmp_rot[0] % 4
                        sv_tmp_rot[0] += 1
                        act(svtmp[k], bank(b)[:, gi * 128:(gi + 1) * 128], AF.Identity, [bank_res[b], r_lg],
                            [r_svtmp[k]], scale=lg_pp[:, g:g + 1])
                        tt("gpsimd", Sj[:, g, :], svtmp[k], Cc[:, g, :], ALU.add, [r_svtmp[k], r_C], [r_slot[j]])

        r_wug = [P.res() for _ in range(4)]
        r_wug2 = [P.res() for _ in range(4)]
        wug_pre = [wug0, wug1]

        def load_wug_pre(g):
            dma("gpsimd", wug_pre[g][:, :, 0, :], w_in_r[:, :, g * 128:(g + 1) * 128], f"wugu{g}",
                writes=[r_wug[g]])
            dma("gpsimd", wug_pre[g][:, :, 1, :], w_in_r[:, :, 2 * AW + g * 128:2 * AW + (g + 1) * 128],
                f"wugg{g}", writes=[r_wug2[g]])

        if AV_ORDER == 1:
            av_front(0)
            av_front_a2(0)
            av_front_b(0)
            av_front(1)
            for j in range(NCH):
                if j + 2 < NCH:
                    av_front(j + 2)
                if j + 1 < NCH:
                    av_front_a2(j + 1)
                av_vmm(j)
                if j + 1 < NCH:
                    av_front_b(j + 1)
                if j >= 1:
                    av_back(j - 1)
                av_mid(j)
                if j == NCH - 4:
                    load_wug_pre(0)
                    load_wug_pre(1)
            av_back(NCH - 1)
        else:
            for j in range(NCH):
                av_front(j)
                av_front_a2(j)
                av_front_b(j)
                if AV_PARTS >= 2:
                    av_vmm(j)
                if AV_PARTS >= 3:
                    av_mid(j)
                if AV_PARTS >= 4:
                    av_back(j)
        if stage == "av":
            for j in range(1, NCH):
                dma("sync", out[j - 1], h1_view(j), "o0", reads=[r_slot[j]])
            P.fence()
            P.emit(nc)
            return nc
        AUG_PB = [6, 4, 0, 2]
        for (j0_, nj_), pb_ in zip([(0, 4), (4, 4)], AUG_PB[:2]):
            n_ = nj_ * 128
            rh_ = [r_hnT[j] for j in range(j0_, j0_ + nj_)]
            for kc in range(8):
                mm(bank(pb_)[:, 0:n_], wug0[:, kc, 0, :], hnT[:, kc, j0_ * 128:j0_ * 128 + n_],
                   kc == 0, kc == 7, rh_ + [r_wug[0]], [bank_res[pb_]], kc == 7)
            for kc in range(8):
                mm(bank(pb_ + 1)[:, 0:n_], wug0[:, kc, 1, :], hnT[:, kc, j0_ * 128:j0_ * 128 + n_],
                   kc == 0, kc == 7, rh_ + [r_wug2[0]], [bank_res[pb_ + 1]], kc == 7)
        P.fence()

        RC.reset()
        Wout = RC.take(BF16, [128, 16, D]); r_Wout = [P.res() for _ in range(2)]
        w_out_r = a_w_out.rearrange("(g p) n -> p g n", p=128)
        RW.reset()
        sgs = [RW.take(F32, [128, 512]) for _ in range(3)]; r_sgs = [P.res() for _ in range(3)]
        tus = [RW.take(F32, [128, 512]) for _ in range(3)]; r_tus = [P.res() for _ in range(3)]
        xs_o = [RW.take(F32, [128, D]) for _ in range(2)]; r_xso = [P.res() for _ in range(2)]
        RD.reset()
        wug = [wug0, wug1] + [RD.take(BF16, [128, 8, 2, 128]) for _ in range(2)]
        batches = [(0, 4), (4, 4), (8, 3), (11, 3), (14, 3)]
        S4 = SLOT.t[:, :].rearrange("p (j g t) -> p j g t", j=NCH, g=16)
        w_in_4 = a_w_in.rearrange("(kc p) (th n) -> p kc th n", p=128, th=3)

        def load_wug(g):
            dma("gpsimd", wug[g % 4][:, :, 0, :], w_in_r[:, :, g * 128:(g + 1) * 128], f"wugu{g % 4}",
                writes=[r_wug[g % 4]])
            dma("gpsimd", wug[g % 4][:, :, 1, :], w_in_r[:, :, 2 * AW + g * 128:2 * AW + (g + 1) * 128],
                f"wugg{g % 4}", writes=[r_wug2[g % 4]])

        it = 0
        for g in range(16):
            if g + 2 < 16:
                load_wug(g + 2)
            if g == 1:
                for hf in range(2):
                    dma("gpsimd", Wout[:, hf * 8:(hf + 1) * 8, :], w_out_r[:, hf * 8:(hf + 1) * 8, :],
                        f"wout{hf}", writes=[r_Wout[hf]])
            if g == 14:
                for jj in range(2):
                    dma("sync", xs_o[jj], x[jj], f"x{jj}", writes=[r_xso[jj]])
            wb = wug[g % 4]; rwb = r_wug[g % 4]; rwb2 = r_wug2[g % 4]
            for (j0, nj) in batches:
                n = nj * 128
                pb = AUG_PB[it % 4]
                pre_issued = it < 2
                sg = sgs[it % 3]; rsg = r_sgs[it % 3]
                tu = tus[it % 3]; rtu = r_tus[it % 3]
                it += 1
                rh = [r_hnT[j] for j in range(j0, j0 + nj)]
                rs = [r_slot[j] for j in range(j0, j0 + nj)]
                for kc in range(8):
                    if pre_issued:
                        break
                    mm(bank(pb)[:, 0:n], wb[:, kc, 0, :], hnT[:, kc, j0 * 128:j0 * 128 + n],
                       kc == 0, kc == 7, rh + [rwb], [bank_res[pb]], kc == 7)
                for kc in range(8):
                    if pre_issued:
                        break
                    mm(bank(pb + 1)[:, 0:n], wb[:, kc, 1, :], hnT[:, kc, j0 * 128:j0 * 128 + n],
                       kc == 0, kc == 7, rh + [rwb2], [bank_res[pb + 1]], kc == 7)
                act(sg[:, 0:n], bank(pb + 1)[:, 0:n], AF.Silu, [bank_res[pb + 1]], [rsg])
                tt("vector", tu[:, 0:n], bank(pb)[:, 0:n], sg[:, 0:n], ALU.mult, [bank_res[pb], rsg], [rtu])
                Sv = S4[:, j0:j0 + nj, g, :]
                tt("gpsimd", Sv, Sv, tu[:, 0:n].rearrange("p (j t) -> p j t", j=nj), ALU.mult,
                   rs + [rtu], rs)
        if stage == "aug":
            for j in range(1, NCH):
                dma("sync", out[j - 1], h1_view(j), "o0", reads=[r_slot[j]])
            P.fence()
            P.emit(nc)
            return nc
        P.fence()

        RF.reset()
        kvg_pp = RF.take(F32, [128, 8]); r_kvg = P.res()
        bg_pp = RF.take(F32, [128, 8]); r_bg = P.res()
        fg_bc = RF.take(F32, [128, D]); r_fg = P.res()
        bq_bc = RF.take(F32, [128, D]); r_bq = P.res()
        bkv_bc = RF.take(F32, [128, 256]); r_bkv = P.res()
        cos_t = RF.take(F32, [128, NCH, 32]); r_cos = P.res()
        sin_t = RF.take(F32, [128, NCH, 32]); r_sin = P.res()
        dma("sync", kvg_pp, kv_norm_g.rearrange("(kc p) -> p kc", p=128), "c4", writes=[r_kvg], slow=True)
        dma("sync", bg_pp, b_norm_g.rearrange("(kc p) -> p kc", p=128), "c5", writes=[r_bg], slow=True)
        dma("sync", fg_bc, final_norm_g.partition_broadcast(128), "c6", writes=[r_fg])
        dma("sync", bq_bc, b_bq.partition_broadcast(128), "c7", writes=[r_bq])
        dma("sync", bkv_bc, b_kv.partition_broadcast(128), "c8", writes=[r_bkv])
        dma("sync", cos_t, cos_d, "c9", writes=[r_cos])
        dma("sync", sin_t, sin_d, "c10", writes=[r_sin])

        RE.reset()
        wkv = RE.take(BF16, [128, 8, 256])
        dma("gpsimd", wkv, w_kv.rearrange("(kc p) n -> p kc n", p=128), "wkv", writes=[r_wkv])
        RB.reset()
        bwin = RB.take(BF16, [128, 8, 2048]); r_bwin = [P.res() for _ in range(4)]
        RD.reset()
        bwout = RD.take(BF16, [128, 8, D]); r_bwout = P.res()
        b_w_in_r = b_w_in.rearrange("(kc p) n -> p kc n", p=128)
        for s in range(4):
            dma("gpsimd", bwin[:, :, s * 512:(s + 1) * 512], b_w_in_r[:, :, s * 512:(s + 1) * 512],
                f"wv{s}", writes=[r_bwin[s]])
        dma("gpsimd", bwout, b_w_out.rearrange("(kc p) n -> p kc n", p=128), "wout0", writes=[r_bwout])
        def fold_gain(step):
            kc = step % 8
            if step < 8:
                ts("vector", wkv[:, kc, :], wkv[:, kc, :], kvg_pp[:, kc:kc + 1], None, ALU.mult, None,
                   [r_wkv, r_kvg], [r_wkv])
            else:
                ts("vector", bwin[:, kc, :], bwin[:, kc, :], bg_pp[:, kc:kc + 1], None, ALU.mult, None,
                   list(r_bwin) + [r_bg], list(r_bwin))
        for j in range(NCH):
            xb = xs_o[j % 2]; rxb = r_xso[j % 2]
            if j >= 2:
                dma("sync", xb, x[j], f"x{j % 2}", writes=[rxb])
            Sj = S_view(j)
            pb = 4 * (j % 2)
            for hf in range(2):
                b = pb + hf
                for g in range(16):
                    mm(bank(b), Sj[:, g, :], Wout[:, g, hf * 512:(hf + 1) * 512], g == 0, g == 15,
                       [r_slot[j], r_Wout[g // 8]], [bank_res[b]], g == 15)
            h1 = h1_view(j)
            for hf in range(2):
                tt("vector", h1[:, hf * 512:(hf + 1) * 512], bank(pb + hf), xb[:, hf * 512:(hf + 1) * 512],
                   ALU.add, [bank_res[pb + hf], rxb], [r_slot[j]])
            if j >= 1:
                fold_gain(j - 1)

        if stage == "h1":
            for j in range(1, NCH):
                dma("sync", out[j - 1], h1_view(j), "o0", reads=[r_slot[j]])
            P.fence()
            P.emit(nc)
            return nc
        P.fence()

        RC.reset(); RW.reset()
        hnkv = RC.take(BF16, [128, D]); r_hnkv = P.res()
        hnb = RC.take(BF16, [128, D]); r_hnb = P.res()
        hnkvT = RC.take(BF16, [128, 8, 128]); r_hnkvT = P.res()
        hnbT = RC.take(BF16, [128, 8, 128]); r_hnbT = P.res()
        qf = RC.take(F32, [128, D]); r_qf = P.res()
        qrot = RC.take(BF16, [128, D]); r_qrot = P.res()
        qT = RC.take(BF16, [128, 8, 128]); r_qT = P.res()
        sgB = [RC.take(F32, [128, D]) for _ in range(2)]; r_sgB = [P.res(), P.res()]
        PT = RC.take(BF16, [128, 2, 16, 128]); r_PT = [[P.res(), P.res()], [P.res(), P.res()]]
        on = qf; r_on = r_qf
        kvf = RW.take(F32, [128, 256]); r_kvf = P.res()
        kz = RW.take(BF16, [128, 2, 2, 128]); r_kz = P.res()
        kTz = [RW.take(BF16, [128, 2, 2, 128]) for _ in range(3)]; r_kTz = [P.res() for _ in range(3)]
        rt = [RW.take(F32, [128, 16, 32]) for _ in range(2)]; r_rt = [P.res() for _ in range(2)]
        rtk = [RW.take(F32, [128, 2, 32]) for _ in range(2)]; r_rtk = [P.res() for _ in range(2)]
        yb = RW.take(BF16, [128, D]); r_yb = P.res()
        yT = RW.take(BF16, [128, 8, 128]); r_yT = P.res()
        sqB = RW.take(BF16, [128, D]); r_sqB = P.res()
        rt2 = [RW.take(F32, [128, 16, 32]) for _ in range(2)]; r_rt2 = [P.res() for _ in range(2)]
        r_qrot_hi = P.res()
        ot = RW.take(F32, [128, D]); r_ot = P.res()
        ssB = [stat[:, 64:65], stat[:, 65:66]]
        rstdB = [stat[:, 66:67], stat[:, 67:68]]
        r_ssB = [P.res(), P.res()]
        ss2 = stat[:, 68:69]
        rstd2 = stat[:, 69:70]
        r_ss2 = P.res()
        den = stat[:, 72:88]; r_den = [P.res(), P.res()]
        r_on2 = [P.res(), P.res()]
        r_yb2 = [[P.res(), P.res()], [P.res(), P.res()]]
        ybs = [yb, hnb]
        P.op("vector", lambda e: e.memset(kz.rearrange("p a b c -> p (a b c)"), 0.0), writes=[r_kz])

        def rotary(eng, src, dst_lo, dst_hi, nh, j, rsrc, rdst, rt, r_rt):
            cb = cos_t[:, j, :].unsqueeze(1).to_broadcast([128, nh, 32])
            sb_ = sin_t[:, j, :].unsqueeze(1).to_broadcast([128, nh, 32])
            x1 = src[:, :, 0:32]
            x2 = src[:, :, 32:64]
            a, b = (rt[i][:, 0:nh, :] for i in range(2))
            tt(eng, a, x1, cb, ALU.mult, rsrc + [r_cos], [r_rt[0]])
            tt(eng, b, x2, sb_, ALU.mult, rsrc + [r_sin], [r_rt[1]])
            tt(eng, dst_lo, a, b, ALU.subtract, [r_rt[0], r_rt[1]], rdst)
            tt(eng, a, x2, cb, ALU.mult, rsrc + [r_cos], [r_rt[0]])
            tt(eng, b, x1, sb_, ALU.mult, rsrc + [r_sin], [r_rt[1]])
            tt(eng, dst_hi, a, b, ALU.add, [r_rt[0], r_rt[1]], rdst)

        def rotary_q(j, src, dst_lo, dst_hi, rsrc):
            nh = 16
            cb = cos_t[:, j, :].unsqueeze(1).to_broadcast([128, nh, 32])
            sb_ = sin_t[:, j, :].unsqueeze(1).to_broadcast([128, nh, 32])
            x1 = src[:, :, 0:32]
            x2 = src[:, :, 32:64]
            a, b = rt[0], rt[1]
            c, d_ = rt2[0], rt2[1]
            tt("gpsimd", a, x1, cb, ALU.mult, rsrc + [r_cos], [r_rt[0]])
            tt("vector", c, x2, cb, ALU.mult, rsrc + [r_cos], [r_rt2[0]])
            tt("gpsimd", b, x2, sb_, ALU.mult, rsrc + [r_sin], [r_rt[1]])
            tt("vector", d_, x1, sb_, ALU.mult, rsrc + [r_sin], [r_rt2[1]])
            tt("gpsimd", dst_lo, a, b, ALU.subtract, [r_rt[0], r_rt[1]], [r_qrot])
            tt("vector", dst_hi, c, d_, ALU.add, [r_rt2[0], r_rt2[1]], [r_qrot_hi])

        def b_a(j):
            p = j % 2
            h1 = h1_view(j)
            act(sqB, h1, AF.Square, [r_slot[j]], [r_sqB, r_ssB[p]], accum=ssB[p])
            ts("vector", rstdB[p], ssB[p], 1.0 / D, EPS, ALU.mult, ALU.add, [r_ssB[p]], [r_ssB[p]])
            P.op("gpsimd", lambda e: e.tensor_tensor(rstdB[p], rstdB[p], negh, ALU.pow),
                 reads=[r_ssB[p], r_negh], writes=[r_ssB[p]])

        def b_a2(j):
            p = j % 2
            h1 = h1_view(j)
            ts("vector", hnkv, h1, rstdB[p], None, ALU.mult, None, [r_slot[j], r_ssB[p]], [r_hnkv])

        def b_b_part(j, part):
            which, half = divmod(part, 4)
            if which == 1:
                return
            src, rsrc, bk, dst, rdst = ((hnkv, r_hnkv, 4, hnkvT, r_hnkvT), (hnb, r_hnb, TRB_HNB, hnbT, r_hnbT))[which]
            tb = bank_bf(bk)
            for kc in range(half * 2, half * 2 + 2):
                tr(tb[:, kc * 128:(kc + 1) * 128], src[:, kc * 128:(kc + 1) * 128], ident,
                   [rsrc, r_ident], [bank_res[bk]], kc == 7)
            if half == 3:
                cp("vector", dst, tb.rearrange("p (a b) -> p a b", a=8), [bank_res[bk]], [rdst])

        def b_b(j):
            for part in range(8):
                b_b_part(j, part)

        def b_c_kv(j):
            for kc in range(8):
                mm(bank(KVB)[:, 0:256], hnkvT[:, kc, :], wkv[:, kc, :], kc == 0, kc == 7,
                   [r_hnkvT, r_wkv], [bank_res[KVB]], kc == 7)
            tt("vector", kvf, bank(KVB)[:, 0:256], bkv_bc, ALU.add, [bank_res[KVB], r_bkv], [r_kvf])
            ksrc = kvf[:, 0:128].rearrange("p (h d) -> p h d", h=2)
            rotary("vector", ksrc, kz[:, :, 0, 0:32], kz[:, :, 0, 32:64], 2, j, [r_kvf], [r_kz], rtk, r_rtk)
            cp("vector", kz[:, :, 1, 64:128], kz[:, :, 0, 0:64], [r_kz], [r_kz])
            va = vaug[j % 3]; rva = r_vaug[j % 3]
            cp("vector", va[:, :, 0:64], kvf[:, 128:256].rearrange("p (h d) -> p h d", h=2), [r_kvf], [rva])

        def b_c_q(j):
            if j >= 1:
                for s in range(4):
                    for kc in range(8):
                        mm(bank(s), hnkvT[:, kc, :], bwin[:, kc, s * 512:(s + 1) * 512], kc == 0, kc == 7,
                           [r_hnkvT, r_bwin[s]], [bank_res[s]], kc == 7)
                sg = sgB[j % 2]; rsg = r_sgB[j % 2]
                for s in range(2):
                    tt("vector", qf[:, s * 512:(s + 1) * 512], bank(s), bq_bc[:, s * 512:(s + 1) * 512], ALU.add,
                       [bank_res[s], r_bq], [r_qf, r_on2[s]])
                for s in range(2):
                    act(sg[:, s * 512:(s + 1) * 512], bank(2 + s), AF.Tanh, [bank_res[2 + s]], [rsg], scale=0.5)
                for s in range(2):
                    stt(sg[:, s * 512:(s + 1) * 512], sg[:, s * 512:(s + 1) * 512], 1.0, bank(2 + s),
                        ALU.add, ALU.mult, [rsg, bank_res[2 + s]], [rsg])
                q3 = qf.rearrange("p (h d) -> p h d", h=16)
                qr3 = qrot.rearrange("p (h d) -> p h d", h=16)
                rotary_q(j, q3, qr3[:, :, 0:32], qr3[:, :, 32:64], [r_qf])

        def b_d(j):
            tb = bank_bf(KZB)
            for hk in range(2):
                for par in range(2):
                    c0 = (hk * 2 + par) * 128
                    tr(tb[:, c0:c0 + 128], kz[:, hk, par, :], ident, [r_kz, r_ident], [bank_res[KZB]],
                       hk == 1 and par == 1)
            cp("scalar", kTz[j % 3], tb[:, 0:512].rearrange("p (a b c) -> p a b c", a=2, b=2),
               [bank_res[KZB]], [r_kTz[j % 3]])
            if j >= 1:
                tb = bank_bf(TRB)
                for kc in range(8):
                    tr(tb[:, kc * 128:(kc + 1) * 128], qrot[:, kc * 128:(kc + 1) * 128], ident,
                       [r_qrot, r_qrot_hi, r_ident], [bank_res[TRB]], kc == 7)
                cp("vector", qT, tb.rearrange("p (a b) -> p a b", a=8), [bank_res[TRB]], [r_qT])

        st_rot = [0]

        def b_f(j, inter=None):
            npair = 0
            for hk in range(2):
                for kb in range(2):
                    jk = j - 1 + kb
                    kTk = kTz[jk % 3]; rkTk = r_kTz[jk % 3]
                    ng = negc if kb == 1 else (negp0 if j == 1 else negp)
                    rng_ = r_negc if kb == 1 else (r_negp0 if j == 1 else r_negp)
                    for par in range(2):
                        b = (8 - ST_RING) + st_rot[0] % ST_RING
                        st_rot[0] += 1
                        ob = bank(b).rearrange("p (a b) -> p a b", a=4)
                        mm(ob, ident, ng, True, False, [r_ident, rng_], [bank_res[b]], False)
                        mm(ob, kTk[:, hk, par, :], qT[:, hk * 4:(hk + 1) * 4, :], False, True,
                           [rkTk, r_qT], [bank_res[b]], True)
                        pv = PT[:, kb, hk * 8 + par:hk * 8 + 8:2, :]
                        act(pv, ob, AF.Exp, [bank_res[b]], [r_PT[kb][hk]], scale=0.125)
                        if inter is not None:
                            inter(npair)
                        npair += 1

        def b_g(j):
            o4 = PS[:, 0:2048].rearrange("p (h c) -> p h c", h=16)
            for h in range(16):
                hk = h // 8
                b = h // 4
                for kb in range(2):
                    jk = j - 1 + kb
                    mm(o4[:, h, 0:65], PT[:, kb, h, :], vaug[jk % 3][:, hk, :], kb == 0, kb == 1,
                       [r_PT[kb][hk], r_vaug[jk % 3]], [bank_res[b]], (kb == 1 and h % 4 == 3))
                if h % 8 == 7:
                    hh = h // 8
                    ro = [bank_res[2 * hh], bank_res[2 * hh + 1]]
                    hs = slice(hh * 8, hh * 8 + 8)
                    cs = slice(hh * 512, hh * 512 + 512)
                    dn = den[:, hs]
                    stt(dn, o4[:, hs, 64], 2.0, sinkexp[:, hs], ALU.mult, ALU.add, ro + [r_sink], [r_den[hh]])
                    P.op("vector", (lambda d_: (lambda e: e.reciprocal(d_, d_)))(dn), reads=[r_den[hh]], writes=[r_den[hh]])
                    tt("vector", on[:, cs].rearrange("p (h d) -> p h d", h=8), o4[:, hs, 0:64],
                       dn.unsqueeze(2).to_broadcast([128, 8, 64]), ALU.mult, ro + [r_den[hh]], [r_on2[hh], r_qf])
                    tt("gpsimd", ybs[j % 2][:, cs], on[:, cs], sgB[j % 2][:, cs], ALU.mult,
                       [r_on2[hh], r_sgB[j % 2]], [r_yb2[j % 2][hh]])

        def b_h(j):
            tb = bank_bf(YTB)
            for kc in range(8):
                tr(tb[:, kc * 128:(kc + 1) * 128], ybs[j % 2][:, kc * 128:(kc + 1) * 128], ident,
                   [r_yb2[j % 2][kc // 4], r_ident], [bank_res[YTB]], kc == 7)
            cp("scalar", yT, tb.rearrange("p (a b) -> p a b", a=8), [bank_res[YTB]], [r_yT])

        def b_i(j):
            h1 = h1_view(j)
            for hf in range(2):
                b = WO_BANKS[hf]
                for kc in range(8):
                    mm(bank(b), yT[:, kc, :], bwout[:, kc, hf * 512:(hf + 1) * 512], kc == 0, kc == 7,
                       [r_yT, r_bwout], [bank_res[b]], kc == 7)
            for hf in range(2):
                b = WO_BANKS[hf]
                tt("vector", h1[:, hf * 512:(hf + 1) * 512], bank(b), h1[:, hf * 512:(hf + 1) * 512], ALU.add,
                   [bank_res[b], r_slot[j]], [r_slot[j]])
            h2 = h1
            r_h2 = r_slot[j]
            act(sqB, h2, AF.Square, [r_h2], [r_sqB, r_ss2], accum=ss2)
            ts("vector", rstd2, ss2, 1.0 / D, EPS, ALU.mult, ALU.add, [r_ss2], [r_ss2])
            P.op("gpsimd", lambda e: e.tensor_tensor(rstd2, rstd2, negh, ALU.pow),
                 reads=[r_ss2, r_negh], writes=[r_ss2])
            stt(ot, h2, rstd2, fg_bc, ALU.mult, ALU.mult, [r_h2, r_ss2, r_fg], [r_ot])
            dma("sync", out[j - 1], ot, "o0", reads=[r_ot])

        b_a(0)
        b_a2(0)
        b_b(0)
        for i in range(NCH + 3):
            if i + 1 < NCH:
                b_a(i + 1)
            if i < NCH:
                b_c_kv(i)
            if i + 1 < NCH:
                b_a2(i + 1)
            if 1 <= i - 3 < NCH:
                b_i(i - 3)
            if i < NCH:
                b_c_q(i)
            if 1 <= i - 1 < NCH:
                if i + 1 < NCH:
                    b_f(i - 1, inter=(lambda jj: (lambda k: b_b_part(jj, k)))(i + 1))
                else:
                    b_f(i - 1)
                b_g(i - 1)
            elif i + 1 < NCH:
                b_b(i + 1)
            if 1 <= i - 2 < NCH:
                b_h(i - 2)
            if i < NCH:
                b_d(i)
        P.fence()
        P.emit(nc)
    return nc


def _host_inputs(inputs):
    x = np.ascontiguousarray(np.asarray(inputs["x"], dtype=np.float32))
    sq = lambda k: np.ascontiguousarray(np.asarray(inputs[k], dtype=np.float32))
    shared = {
        "a_norm_g": sq("a_norm_g")[0], "a_w_in": sq("a_w_in")[0], "a_ln_g": sq("a_ln_g")[0],
        "a_ln_b": sq("a_ln_b")[0], "a_ws": sq("a_ws")[0], "a_bs": sq("a_bs")[0].reshape(-1),
        "a_w_out": sq("a_w_out")[0], "kv_norm_g": sq("kv_norm_g"), "w_kv": sq("w_kv"), "b_kv": sq("b_kv"),
        "b_norm_g": sq("b_norm_g")[0], "b_w_in": sq("b_w_in")[0], "b_bq": sq("b_bq")[0],
        "b_sinks": sq("b_sinks")[0], "b_w_out": sq("b_w_out")[0], "final_norm_g": sq("final_norm_g"),
    }
    shared = {k: np.ascontiguousarray(v) for k, v in shared.items()}
    k_i = np.arange(128)[:, None]
    t_i = np.arange(128)[None, :]
    shared["ident"] = np.eye(128, dtype=np.float32)
    shared["maskc"] = (k_i <= t_i).astype(np.float32)
    NEG = np.float32(-30000.0)
    negc = np.where(k_i <= t_i, np.float32(0), NEG).astype(np.float32)
    negp = np.where(k_i > t_i, np.float32(0), NEG).astype(np.float32)
    rep4 = lambda m: np.ascontiguousarray(np.repeat(m[:, None, :], 4, axis=1))
    shared["negc"] = rep4(negc)
    shared["negp"] = rep4(negp)
    inv_freq = (10000.0 ** (-np.arange(0, 64, 2, dtype=np.float32) / 64)).astype(np.float32)
    in_maps = []
    for c in range(NCORES):
        b, hf = divmod(c, 2)
        xc = np.zeros((NCH, 128, D), np.float32)
        xc[1:] = x[b, hf * 2048:(hf + 1) * 2048].reshape(16, 128, D)
        if hf == 1:
            xc[0] = x[b, 2048 - 128:2048]
        pos = (hf * 2048 - 128 + np.arange(NCH * 128)).astype(np.float32)
        ang = pos[:, None] * inv_freq[None, :]
        cos_t = np.cos(ang).astype(np.float32).reshape(NCH, 128, 32).transpose(1, 0, 2)
        sin_t = np.sin(ang).astype(np.float32).reshape(NCH, 128, 32).transpose(1, 0, 2)
        m = dict(shared)
        m["x"] = xc
        m["cos_t"] = np.ascontiguousarray(cos_t)
        m["sin_t"] = np.ascontiguousarray(sin_t)
        m["negp0"] = shared["negp"] if hf == 1 else np.full((128, 4, 128), NEG, np.float32)
        in_maps.append(m)
    return in_maps


def run(inputs, stage="full"):
    in_maps = _host_inputs(inputs)
    nc = build(stage)
    res = run_bass_kernel_spmd(nc, in_maps, core_ids=list(range(NCORES)))
    outs = [np.asarray(r["out"]).reshape(2048, D) for r in res.results]
    full = np.stack([np.concatenate(outs[2 * b:2 * b + 2], axis=0) for b in range(4)], axis=0)
    return full.astype(np.float32)


def kernel(**inputs):
    return run(inputs, "full")
```

```python
from contextlib import ExitStack
import numpy as np
import concourse.bass as bass
import concourse.mybir as mybir
from concourse.bass_utils import run_bass_kernel_spmd

F32 = mybir.dt.float32
BF16 = mybir.dt.bfloat16
ALU = mybir.AluOpType
AF = mybir.ActivationFunctionType

ENGS = ["sync", "scalar", "vector", "gpsimd", "tensor"]
NCORES = 8
NCH = 17
D = 1024
AW = 2048
EPS = 1e-5
AV_ORDER = 1
AV_NORM = "pool"
AV_PARTS = 4
ROT_ENG = "gpsimd"
AV_SVBANK = "v"
ST_RING = 3
TRB = 5
TRB_HNB = 3
KZB = 6
KVB = 4
YTB = 7
WO_BANKS = (6, 7)
AV_VMM_ACT = 1
AV_VMM_BN = 1


class Res:
    __slots__ = ("name", "w", "r")

    def __init__(self, name):
        self.name = name
        self.w = None
        self.r = []


class Prog:
    def __init__(self):
        self.ops = {e: [] for e in ENGS}
        self.seen = {e: {} for e in ENGS}
        self.dcnt = {}
        self.nres = 0

    def res(self, name=None):
        self.nres += 1
        return Res(name or f"r{self.nres}")

    def _need(self, eng, deps, tok, raw):
        if tok is None:
            return
        key, seq, peng = tok
        if peng == eng and not raw:
            return
        if self.seen[eng].get(key, -1) >= seq:
            return
        self.seen[eng][key] = seq
        deps[key] = max(deps.get(key, -1), seq)

    def _deps(self, eng, reads, writes):
        deps = {}
        for r in reads:
            self._need(eng, deps, r.w, True)
        for w in writes:
            self._need(eng, deps, w.w, False)
            for t in w.r:
                self._need(eng, deps, t, False)
        return deps

    def _commit(self, tok, reads, writes):
        for r in reads:
            r.r.append(tok)
        for w in writes:
            w.w = tok
            w.r = []

    def op(self, eng, fn, reads=(), writes=(), signal=True):
        deps = self._deps(eng, reads, writes)
        seq = len(self.ops[eng])
        tok = ("E_" + eng, seq, eng)
        self._commit(tok, reads, writes)
        self.ops[eng].append(dict(fn=fn, deps=deps, sig_ok=signal, dma=None))

    def dma(self, eng, fn, sem, reads=(), writes=()):
        deps = self._deps(eng, reads, writes)
        key = "D_" + sem
        n = self.dcnt.get(key, 0) + 1
        self.dcnt[key] = n
        tok = (key, n, "dma:" + key)
        self._commit(tok, reads, writes)
        self.ops[eng].append(dict(fn=fn, deps=deps, sig_ok=False, dma=key))

    def fence(self):
        for e in ENGS:
            deps = {}
            for pe in ENGS:
                if pe != e:
                    last = [i for i, o in enumerate(self.ops[pe]) if o["sig_ok"]]
                    if last:
                        self._need(e, deps, ("E_" + pe, last[-1], pe), True)
            for k, n in self.dcnt.items():
                self._need(e, deps, (k, n, "dma:" + k), True)
            if deps:
                self.ops[e].append(dict(fn=None, deps=deps, sig_ok=False, dma=None))

    def emit(self, nc):
        needed = {e: set() for e in ENGS}
        sig_idx = {}
        for e in ENGS:
            idx = [i for i, o in enumerate(self.ops[e]) if o["sig_ok"]]
            sig_idx[e] = idx
        import bisect
        for e in ENGS:
            for o in self.ops[e]:
                nd = {}
                for k, seq in o["deps"].items():
                    if k.startswith("E_"):
                        pe = k[2:]
                        idx = sig_idx[pe]
                        p = bisect.bisect_left(idx, seq)
                        assert p < len(idx), ("no signalable op after", pe, seq)
                        s2 = idx[p]
                        needed[pe].add(s2)
                        nd[k] = s2
                    else:
                        nd[k] = seq
                o["deps"] = nd
        cnt_at = {}
        for e in ENGS:
            c = 0
            m = {}
            for i in range(len(self.ops[e])):
                if i in needed[e]:
                    c += 1
                    m[i] = c
            cnt_at[e] = m
        keys = ["E_" + e for e in ENGS if needed[e]] + sorted(self.dcnt.keys())
        with ExitStack() as es:
            sems = {k: es.enter_context(nc.semaphore(k)) for k in keys}
            block = es.enter_context(nc.Block())

            def mk(ename):
                def body(eng):
                    for i, o in enumerate(self.ops[ename]):
                        for k, v in o["deps"].items():
                            if k.startswith("E_"):
                                eng.wait_ge(sems[k], cnt_at[k[2:]][v])
                            else:
                                eng.wait_ge(sems[k], 16 * v)
                        if o["fn"] is None:
                            continue
                        ins = o["fn"](eng)
                        if o["dma"] is not None:
                            ins.then_inc(sems[o["dma"]], 16)
                        elif i in needed[ename]:
                            ins.then_inc(sems["E_" + ename], 1)
                return body

            for e in ENGS:
                if self.ops[e]:
                    getattr(block, e)(mk(e))
        return len(keys)


class Region:
    def __init__(self, t, nbytes):
        self.t = t
        self.nbytes = nbytes
        self.off = 0

    def reset(self):
        self.off = 0

    def take(self, dtype, shape):
        esz = 4 if dtype == F32 else 2
        n = 1
        for s in shape[1:]:
            n *= s
        nb = (n * esz + 31) // 32 * 32
        assert self.off + nb <= self.nbytes, (self.off, nb, self.nbytes)
        a = self.t[0:shape[0], self.off // 2:(self.off + n * esz) // 2]
        self.off += nb
        if dtype == F32:
            a = a.bitcast(F32)
        if len(shape) == 3:
            a = a.rearrange("p (a b) -> p a b", a=shape[1])
        elif len(shape) == 4:
            a = a.rearrange("p (a b c) -> p a b c", a=shape[1], b=shape[2])
        return a


def build(stage="full"):
    nc = bass.Bass("TRN2", target_bir_lowering=False)
    P = Prog()

    def din(name, shape):
        return nc.dram_tensor(name, list(shape), F32, kind="ExternalInput").ap()

    x = din("x", [NCH, 128, D])
    a_norm_g = din("a_norm_g", [D])
    a_w_in = din("a_w_in", [D, 3 * AW])
    a_ln_g = din("a_ln_g", [AW])
    a_ln_b = din("a_ln_b", [AW])
    a_ws = din("a_ws", [16, 128, 128])
    a_bs = din("a_bs", [AW])
    a_w_out = din("a_w_out", [AW, D])
    kv_norm_g = din("kv_norm_g", [D])
    w_kv = din("w_kv", [D, 256])
    b_kv = din("b_kv", [256])
    b_norm_g = din("b_norm_g", [D])
    b_w_in = din("b_w_in", [D, 2048])
    b_bq = din("b_bq", [D])
    b_sinks = din("b_sinks", [16])
    b_w_out = din("b_w_out", [D, D])
    final_norm_g = din("final_norm_g", [D])
    ident_d = din("ident", [128, 128])
    maskc_d = din("maskc", [128, 128])
    negc_d = din("negc", [128, 4, 128])
    negp_d = din("negp", [128, 4, 128])
    negp0_d = din("negp0", [128, 4, 128])
    cos_d = din("cos_t", [128, NCH, 32])
    sin_d = din("sin_t", [128, NCH, 32])
    out = nc.dram_tensor("out", [16, 128, D], F32, kind="ExternalOutput").ap()

    with ExitStack() as es:
        def sbt(name, nbytes):
            return Region(es.enter_context(nc.sbuf_tensor(name, [128, nbytes // 2], BF16)), nbytes)

        K = 1024
        SLOT = sbt("slot", NCH * 4 * K)
        RB = sbt("rb", 34 * K)
        RC = sbt("rc", 32 * K)
        RD = sbt("rd", 16 * K)
        RE = sbt("re", 4 * K)
        RF = sbt("rf", 22 * K)
        RW = sbt("rw", 24 * K)
        RM = sbt("rm", 5 * K + 512)
        PS = es.enter_context(nc.psum_tensor("ps", [128, 8 * 512], F32))
        bank_res = [P.res(f"bank{i}") for i in range(8)]

        def bank(i, n=1):
            return PS[:, i * 512:(i + n) * 512]

        def bank_bf(i):
            return PS[:, i * 512:(i + 1) * 512].bitcast(BF16)

        def mm(outap, lhsT, rhs, start, stop, reads, writes, signal):
            P.op("tensor", lambda e: e.matmul(outap, lhsT, rhs, start=start, stop=stop),
                 reads=reads, writes=writes, signal=signal)

        def tr(outap, inap, idap, reads, writes, signal):
            P.op("tensor", lambda e: e.transpose(outap, inap, idap),
                 reads=reads, writes=writes, signal=signal)

        def act(outap, inap, func, reads, writes, scale=1.0, bias=0.0, accum=None):
            if accum is None:
                P.op("scalar", lambda e: e.activation(outap, inap, func, bias=bias, scale=scale),
                     reads=reads, writes=writes)
            else:
                P.op("scalar", lambda e: e.activation(outap, inap, func, bias=bias, scale=scale, accum_out=accum),
                     reads=reads, writes=writes)

        def tt(eng, outap, a, b, op, reads, writes):
            P.op(eng, lambda e: e.tensor_tensor(outap, a, b, op), reads=reads, writes=writes)

        def stt(outap, a, sc, b, op0, op1, reads, writes):
            P.op("vector", lambda e: e.scalar_tensor_tensor(outap, a, sc, b, op0, op1),
                 reads=reads, writes=writes)

        def ts(eng, outap, a, s1, s2, op0, op1, reads, writes):
            if op1 is None:
                P.op(eng, lambda e: e.tensor_scalar(outap, a, s1, None, op0), reads=reads, writes=writes)
            else:
                P.op(eng, lambda e: e.tensor_scalar(outap, a, s1, s2, op0, op1), reads=reads, writes=writes)

        def cp(eng, outap, inap, reads, writes):
            if eng == "scalar":
                P.op("scalar", lambda e: e.copy(outap, inap), reads=reads, writes=writes)
            else:
                P.op(eng, lambda e: e.tensor_copy(outap, inap), reads=reads, writes=writes)

        def dma(eng, outap, inap, sem, reads=(), writes=(), slow=False):
            if slow:
                P.dma(eng, lambda e: e.dma_start(out=outap, in_=inap, allow_slow_non_contiguous=True),
                      sem, reads=reads, writes=writes)
            else:
                P.dma(eng, lambda e: e.dma_start(out=outap, in_=inap), sem, reads=reads, writes=writes)

        def rstd_from(outap, ssap, n, reads, writes):
            ts("vector", outap, ssap, 1.0 / n, EPS, ALU.mult, ALU.add, reads, writes)
            P.op("gpsimd", lambda e: e.tensor_tensor(outap, outap, negh, ALU.pow),
                 reads=list(writes) + [r_negh], writes=writes)

        ident = RM.take(BF16, [128, 128]); r_ident = P.res()
        maskc = RM.take(BF16, [128, 128]); r_maskc = P.res()
        ones_bf = RM.take(BF16, [128, 128]); r_ones = P.res()
        negc = RM.take(BF16, [128, 4, 128]); r_negc = P.res()
        negp = RM.take(BF16, [128, 4, 128]); r_negp = P.res()
        negp0 = RM.take(BF16, [128, 4, 128]); r_negp0 = P.res()
        lg_pp = RM.take(F32, [128, 16]); r_lg = P.res()
        lb_pp = RM.take(F32, [128, 16]); r_lb = P.res()
        sinkexp = RM.take(F32, [128, 16]); r_sink = P.res()
        stat = RM.take(F32, [128, 128])
        negh = RM.take(F32, [128, 1]); r_negh = P.res()
        vaug = [RM.take(BF16, [128, 2, 65]) for _ in range(3)]
        r_vaug = [P.res() for _ in range(3)]
        wkv = RE.take(BF16, [128, 8, 256]); r_wkv = P.res()

        dma("gpsimd", ident, ident_d, "c0", writes=[r_ident])
        dma("gpsimd", maskc, maskc_d, "c1", writes=[r_maskc])
        dma("sync", lg_pp, a_ln_g.rearrange("(g d) -> d g", g=16), "c4", writes=[r_lg], slow=True)
        dma("sync", lb_pp, a_ln_b.rearrange("(g d) -> d g", g=16), "c5", writes=[r_lb], slow=True)
        dma("sync", sinkexp, b_sinks.partition_broadcast(128), "c6", writes=[r_sink])
        act(sinkexp, sinkexp, AF.Exp, [r_sink], [r_sink])
        ts("vector", sinkexp, sinkexp, 2.0, None, ALU.mult, None, [r_sink], [r_sink])
        P.op("vector", lambda e: e.memset(ones_bf, 1.0), writes=[r_ones])
        P.op("vector", lambda e: e.memset(negh, -0.5), writes=[r_negh])
        for i in range(3):
            P.op("vector", (lambda v: (lambda e: e.memset(v, 1.0)))(vaug[i]), writes=[r_vaug[i]])

        ga_bc = RF.take(F32, [128, D]); r_ga = P.res()
        Cc = RF.take(F32, [128, 16, 128]); r_C = P.res()
        wsT = RF.take(BF16, [128, 16, 128]); r_wsT = P.res()
        dma("sync", ga_bc, a_norm_g.partition_broadcast(128), "c7", writes=[r_ga])
        wug0 = RF.take(BF16, [128, 8, 2, 128])
        RE.reset()
        wug1 = RE.take(BF16, [128, 8, 2, 128])

        RW.reset()
        xs = [RW.take(F32, [128, D]) for _ in range(2)]; r_xs = [P.res() for _ in range(2)]
        sq = RW.take(BF16, [128, D]); r_sq = P.res()
        hn = RW.take(BF16, [128, D]); r_hn = P.res()
        vn = [RW.take(BF16, [128, AW]) for _ in range(2)]; r_vn = [P.res() for _ in range(2)]
        ident_f = RW.take(F32, [128, 128]); r_identf = P.res()
        RD.reset()
        vraw = [RD.take(F32, [128, AW]) for _ in range(2)]
        r_vraw = [[P.res() for _ in range(4)] for _ in range(2)]
        ws_st = vraw[0].rearrange("p (g t) -> p g t", g=16); r_wsst_l = r_vraw[0]
        bs_bc = vraw[1].rearrange("p (g t) -> p g t", g=16); r_bs_l = r_vraw[1]
        dma("sync", ws_st, a_ws.rearrange("g t s -> t g s"), "c8", writes=r_wsst_l)
        dma("sync", bs_bc, a_bs.partition_broadcast(128).rearrange("p (g t) -> p g t", g=16), "c9", writes=r_bs_l)
        dma("sync", ident_f, ident_d, "c10", writes=[r_identf])

        Wv = RC.take(BF16, [128, 8, 2048])
        r_Wv = [P.res() for _ in range(4)]
        w_in_r = a_w_in.rearrange("(kc p) n -> p kc n", p=128)
        for s in range(4):
            dma("gpsimd", Wv[:, :, s * 512:(s + 1) * 512], w_in_r[:, :, AW + s * 512:AW + (s + 1) * 512],
                f"wv{s}", writes=[r_Wv[s]])

        dma("gpsimd", negc, negc_d, "c2", writes=[r_negc])
        dma("gpsimd", negp, negp_d, "c3", writes=[r_negp])
        dma("gpsimd", negp0, negp0_d, "c11", writes=[r_negp0])

        for gq in range(4):
            b = gq % 2
            for gi in range(4):
                g = gq * 4 + gi
                tr(bank(b)[:, gi * 128:(gi + 1) * 128], ws_st[:, g, :], ident_f,
                   r_wsst_l + [r_identf], [bank_res[b]], gi == 3)
            tt("vector", wsT[:, gq * 4:(gq + 1) * 4, :],
               bank(b).rearrange("p (a b) -> p a b", a=4),
               maskc.unsqueeze(1).to_broadcast([128, 4, 128]), ALU.mult,
               [bank_res[b], r_maskc], [r_wsT])
        for gq in range(4):
            b = 2 + gq % 2
            mm(bank(b), ones_bf, wsT[:, gq * 4:(gq + 1) * 4, :].rearrange("p a b -> p (a b)"), True, True,
               [r_ones, r_wsT], [bank_res[b]], True)
            for gi in range(4):
                g = gq * 4 + gi
                stt(Cc[:, g, :], bank(b)[:, gi * 128:(gi + 1) * 128], lb_pp[:, g:g + 1], bs_bc[:, g, :],
                    ALU.mult, ALU.add, [bank_res[b], r_lb] + r_bs_l, [r_C])
        if stage == "setup":
            dma("sync", out[0][:, 0:128], ident_f, "o0", reads=[r_identf])
            P.fence()
            P.emit(nc)
            return nc

        hnT = RB.take(BF16, [128, 8, NCH * 128])
        r_hnT = [P.res() for _ in range(NCH)]
        r_slot = [P.res() for _ in range(NCH)]

        def S_view(j):
            return SLOT.t[:, j * 2048:(j + 1) * 2048].rearrange("p (g t) -> p g t", g=16)

        def h1_view(j):
            return SLOT.t[:, j * 2048:(j + 1) * 2048].bitcast(F32)

        ssA = [stat[:, 0:1], stat[:, 1:2]]
        rstdA = [stat[:, 2:3], stat[:, 3:4]]
        r_ssA = [P.res(), P.res()]
        bnst = [stat[:, 8:32], stat[:, 32:56]]
        mv = [stat[:, 56:58], stat[:, 58:60]]
        rstdv = [stat[:, 60:61], stat[:, 61:62]]
        nmr = [stat[:, 62:63], stat[:, 63:64]]
        r_bn = [P.res(), P.res()]

        def av_front(j):
            p = j % 2
            xb = xs[p]; rxb = r_xs[p]
            dma("sync", xb, x[j], f"x{p}", writes=[rxb])
            act(sq, xb, AF.Square, [rxb], [r_sq, r_ssA[p]], accum=ssA[p])
            ts("vector", rstdA[p], ssA[p], 1.0 / D, EPS, ALU.mult, ALU.add, [r_ssA[p]], [r_ssA[p]])
            P.op("gpsimd", lambda e: e.tensor_tensor(rstdA[p], rstdA[p], negh, ALU.pow),
                 reads=[r_ssA[p], r_negh], writes=[r_ssA[p]])

        def av_front_a2(j):
            p = j % 2
            xb = xs[p]; rxb = r_xs[p]
            stt(hn, xb, rstdA[p], ga_bc, ALU.mult, ALU.mult, [rxb, r_ssA[p], r_ga], [r_hn])

        def av_front_b(j):
            tb = bank_bf(4)
            for kc in range(8):
                tr(tb[:, kc * 128:(kc + 1) * 128], hn[:, kc * 128:(kc + 1) * 128], ident,
                   [r_hn, r_ident], [bank_res[4]], kc == 7)
            cp("scalar", hnT[:, :, j * 128:(j + 1) * 128], tb.rearrange("p (a b) -> p a b", a=8),
               [bank_res[4]], [r_hnT[j]])

        def av_vmm(j, slices=range(4)):
            p = j % 2
            for s in slices:
                for kc in range(8):
                    mm(bank(s), hnT[:, kc, j * 128:(j + 1) * 128], Wv[:, kc, s * 512:(s + 1) * 512],
                       kc == 0, kc == 7, [r_hnT[j], r_Wv[s]], [bank_res[s]], kc == 7)
                if AV_VMM_ACT:
                    cp("scalar", vraw[p][:, s * 512:(s + 1) * 512], bank(s), [bank_res[s]], [r_vraw[p][s]])
                if AV_VMM_BN:
                    P.op("vector", (lambda o, i: (lambda e: e.bn_stats(o, i)))(bnst[p][:, s * 6:(s + 1) * 6],
                                                                               vraw[p][:, s * 512:(s + 1) * 512]),
                         reads=[r_vraw[p][s]], writes=[r_bn[p]])

        def av_mid(j):
            p = j % 2
            P.op("vector", lambda e: e.bn_aggr(mv[p], bnst[p]), reads=[r_bn[p]], writes=[r_bn[p]])
            ts("vector", rstdv[p], mv[p][:, 1:2], EPS, None, ALU.add, None, [r_bn[p]], [r_bn[p]])
            P.op("gpsimd", lambda e: e.tensor_tensor(rstdv[p], rstdv[p], negh, ALU.pow),
                 reads=[r_bn[p], r_negh], writes=[r_bn[p]])
            stt(nmr[p], mv[p][:, 0:1], -1.0, rstdv[p], ALU.mult, ALU.mult, [r_bn[p]], [r_bn[p]])
            if AV_NORM == "pool":
                ts("gpsimd", vn[p], vraw[p], rstdv[p], nmr[p], ALU.mult, ALU.add, r_vraw[p] + [r_bn[p]], [r_vn[p]])
            elif AV_NORM == "dve":
                ts("vector", vn[p], vraw[p], rstdv[p], nmr[p], ALU.mult, ALU.add, r_vraw[p] + [r_bn[p]], [r_vn[p]])
            else:
                act(vn[p], vraw[p], AF.Identity, r_vraw[p] + [r_bn[p]], [r_vn[p]], scale=rstdv[p], bias=nmr[p])

        sv_rot = [0]

        def av_back(j, gqs=range(4)):
            p = j % 2
            Sj = S_view(j)
            for gq in gqs:
                b = (5, 6, 7, 3)[gq] if AV_SVBANK == "v" else 5 + sv_rot[0] % 3
                sv_rot[0] += 1
                for gi in range(4):
                    g = gq * 4 + gi
                    mm(bank(b)[:, gi * 128:(gi + 1) * 128], vn[p][:, g * 128:(g + 1) * 128], wsT[:, g, :],
                       True, True, [r_vn[p], r_wsT], [bank_res[b]], gi == 3)
                for gi in range(4):
                    g = gq * 4 + gi
                    stt(Sj[:, g, :], bank(b)[:, gi * 128:(gi + 1) * 128], lg_pp[:, g:g + 1], Cc[:, g, :],
                        ALU.mult, ALU.add, [bank_res[b], r_lg, r_C], [r_slot[j]])

        r_wug = [P.res() for _ in range(4)]
        r_wug2 = [P.res() for _ in range(4)]
        wug_pre = [wug0, wug1]

        def load_wug_pre(g):
            dma("gpsimd", wug_pre[g][:, :, 0, :], w_in_r[:, :, g * 128:(g + 1) * 128], f"wugu{g}",
                writes=[r_wug[g]])
            dma("gpsimd", wug_pre[g][:, :, 1, :], w_in_r[:, :, 2 * AW + g * 128:2 * AW + (g + 1) * 128],
                f"wugg{g}", writes=[r_wug2[g]])

        if AV_ORDER == 1:
            av_front(0)
            av_front_a2(0)
            av_front_b(0)
            av_front(1)
            for j in range(NCH):
                if j + 2 < NCH:
                    av_front(j + 2)
                if j + 1 < NCH:
                    av_front_a2(j + 1)
                av_vmm(j)
                if j + 1 < NCH:
                    av_front_b(j + 1)
                if j >= 1:
                    av_back(j - 1)
                av_mid(j)
                if j == NCH - 4:
                    load_wug_pre(0)
                    load_wug_pre(1)
            av_back(NCH - 1)
        else:
            for j in range(NCH):
                av_front(j)
                av_front_a2(j)
                av_front_b(j)
                if AV_PARTS >= 2:
                    av_vmm(j)
                if AV_PARTS >= 3:
                    av_mid(j)
                if AV_PARTS >= 4:
                    av_back(j)
        if stage == "av":
            for j in range(1, NCH):
                dma("sync", out[j - 1], h1_view(j), "o0", reads=[r_slot[j]])
            P.fence()
            P.emit(nc)
            return nc
        AUG_PB = [6, 4, 0, 2]
        for (j0_, nj_), pb_ in zip([(0, 4), (4, 4)], AUG_PB[:2]):
            n_ = nj_ * 128
            rh_ = [r_hnT[j] for j in range(j0_, j0_ + nj_)]
            for kc in range(8):
                mm(bank(pb_)[:, 0:n_], wug0[:, kc, 0, :], hnT[:, kc, j0_ * 128:j0_ * 128 + n_],
                   kc == 0, kc == 7, rh_ + [r_wug[0]], [bank_res[pb_]], kc == 7)
            for kc in range(8):
                mm(bank(pb_ + 1)[:, 0:n_], wug0[:, kc, 1, :], hnT[:, kc, j0_ * 128:j0_ * 128 + n_],
                   kc == 0, kc == 7, rh_ + [r_wug2[0]], [bank_res[pb_ + 1]], kc == 7)
        P.fence()

        RC.reset()
        Wout = RC.take(BF16, [128, 16, D]); r_Wout = [P.res() for _ in range(2)]
        w_out_r = a_w_out.rearrange("(g p) n -> p g n", p=128)
        RW.reset()
        sgs = [RW.take(F32, [128, 512]) for _ in range(3)]; r_sgs = [P.res() for _ in range(3)]
        tus = [RW.take(F32, [128, 512]) for _ in range(3)]; r_tus = [P.res() for _ in range(3)]
        xs_o = [RW.take(F32, [128, D]) for _ in range(2)]; r_xso = [P.res() for _ in range(2)]
        RD.reset()
        wug = [wug0, wug1] + [RD.take(BF16, [128, 8, 2, 128]) for _ in range(2)]
        batches = [(0, 4), (4, 4), (8, 3), (11, 3), (14, 3)]
        S4 = SLOT.t[:, :].rearrange("p (j g t) -> p j g t", j=NCH, g=16)
        w_in_4 = a_w_in.rearrange("(kc p) (th n) -> p kc th n", p=128, th=3)

        def load_wug(g):
            dma("gpsimd", wug[g % 4][:, :, 0, :], w_in_r[:, :, g * 128:(g + 1) * 128], f"wugu{g % 4}",
                writes=[r_wug[g % 4]])
            dma("gpsimd", wug[g % 4][:, :, 1, :], w_in_r[:, :, 2 * AW + g * 128:2 * AW + (g + 1) * 128],
                f"wugg{g % 4}", writes=[r_wug2[g % 4]])

        it = 0
        for g in range(16):
            if g + 2 < 16:
                load_wug(g + 2)
            if g == 1:
                for hf in range(2):
                    dma("gpsimd", Wout[:, hf * 8:(hf + 1) * 8, :], w_out_r[:, hf * 8:(hf + 1) * 8, :],
                        f"wout{hf}", writes=[r_Wout[hf]])
            if g == 14:
                for jj in range(2):
                    dma("sync", xs_o[jj], x[jj], f"x{jj}", writes=[r_xso[jj]])
            wb = wug[g % 4]; rwb = r_wug[g % 4]; rwb2 = r_wug2[g % 4]
            for (j0, nj) in batches:
                n = nj * 128
                pb = AUG_PB[it % 4]
                pre_issued = it < 2
                sg = sgs[it % 3]; rsg = r_sgs[it % 3]
                tu = tus[it % 3]; rtu = r_tus[it % 3]
                it += 1
                rh = [r_hnT[j] for j in range(j0, j0 + nj)]
                rs = [r_slot[j] for j in range(j0, j0 + nj)]
                for kc in range(8):
                    if pre_issued:
                        break
                    mm(bank(pb)[:, 0:n], wb[:, kc, 0, :], hnT[:, kc, j0 * 128:j0 * 128 + n],
                       kc == 0, kc == 7, rh + [rwb], [bank_res[pb]], kc == 7)
                for kc in range(8):
                    if pre_issued:
                        break
                    mm(bank(pb + 1)[:, 0:n], wb[:, kc, 1, :], hnT[:, kc, j0 * 128:j0 * 128 + n],
                       kc == 0, kc == 7, rh + [rwb2], [bank_res[pb + 1]], kc == 7)
                act(sg[:, 0:n], bank(pb + 1)[:, 0:n], AF.Silu, [bank_res[pb + 1]], [rsg])
                tt("vector", tu[:, 0:n], bank(pb)[:, 0:n], sg[:, 0:n], ALU.mult, [bank_res[pb], rsg], [rtu])
                Sv = S4[:, j0:j0 + nj, g, :]
                tt("gpsimd", Sv, Sv, tu[:, 0:n].rearrange("p (j t) -> p j t", j=nj), ALU.mult,
                   rs + [rtu], rs)
        if stage == "aug":
            for j in range(1, NCH):
                dma("sync", out[j - 1], h1_view(j), "o0", reads=[r_slot[j]])
            P.fence()
            P.emit(nc)
            return nc

        def aout_chunk(j):
            xb = xs_o[j % 2]; rxb = r_xso[j % 2]
            if j >= 2:
                dma("sync", xb, x[j], f"x{j % 2}", writes=[rxb])
            Sj = S_view(j)
            pb = 4 * (j % 2)
            for hf in range(2):
                b = pb + hf
                for g in range(16):
                    mm(bank(b), Sj[:, g, :], Wout[:, g, hf * 512:(hf + 1) * 512], g == 0, g == 15,
                       [r_slot[j], r_Wout[g // 8]], [bank_res[b]], g == 15)
            h1 = h1_view(j)
            for hf in range(2):
                tt("vector", h1[:, hf * 512:(hf + 1) * 512], bank(pb + hf), xb[:, hf * 512:(hf + 1) * 512],
                   ALU.add, [bank_res[pb + hf], rxb], [r_slot[j]])

        aout_chunk(0)
        aout_chunk(1)
        P.fence()

        RF.reset()
        kvg_pp = RF.take(F32, [128, 8]); r_kvg = P.res()
        bg_pp = RF.take(F32, [128, 8]); r_bg = P.res()
        fg_bc = RF.take(F32, [128, D]); r_fg = P.res()
        bq_bc = RF.take(F32, [128, D]); r_bq = P.res()
        bkv_bc = RF.take(F32, [128, 256]); r_bkv = P.res()
        cos_t = RF.take(F32, [128, NCH, 32]); r_cos = P.res()
        sin_t = RF.take(F32, [128, NCH, 32]); r_sin = P.res()
        dma("sync", kvg_pp, kv_norm_g.rearrange("(kc p) -> p kc", p=128), "c4", writes=[r_kvg], slow=True)
        dma("sync", bg_pp, b_norm_g.rearrange("(kc p) -> p kc", p=128), "c5", writes=[r_bg], slow=True)
        dma("sync", fg_bc, final_norm_g.partition_broadcast(128), "c6", writes=[r_fg])
        dma("sync", bq_bc, b_bq.partition_broadcast(128), "c7", writes=[r_bq])
        dma("sync", bkv_bc, b_kv.partition_broadcast(128), "c8", writes=[r_bkv])
        dma("sync", cos_t, cos_d, "c9", writes=[r_cos])
        dma("sync", sin_t, sin_d, "c10", writes=[r_sin])

        RE.reset()
        wkv = RE.take(BF16, [128, 8, 256])
        dma("gpsimd", wkv, w_kv.rearrange("(kc p) n -> p kc n", p=128), "wkv", writes=[r_wkv])
        RB.reset()
        bwin = RB.take(BF16, [128, 8, 2048]); r_bwin = [P.res() for _ in range(4)]
        RD.reset()
        bwout = RD.take(BF16, [128, 8, D]); r_bwout = P.res()
        b_w_in_r = b_w_in.rearrange("(kc p) n -> p kc n", p=128)
        for s in range(4):
            dma("gpsimd", bwin[:, :, s * 512:(s + 1) * 512], b_w_in_r[:, :, s * 512:(s + 1) * 512],
                f"wv{s}", writes=[r_bwin[s]])
        dma("gpsimd", bwout, b_w_out.rearrange("(kc p) n -> p kc n", p=128), "wout0", writes=[r_bwout])
        def fold_gain(step):
            kc = step % 8
            if step < 8:
                ts("vector", wkv[:, kc, :], wkv[:, kc, :], kvg_pp[:, kc:kc + 1], None, ALU.mult, None,
                   [r_wkv, r_kvg], [r_wkv])
            else:
                ts("vector", bwin[:, kc, :], bwin[:, kc, :], bg_pp[:, kc:kc + 1], None, ALU.mult, None,
                   list(r_bwin) + [r_bg], list(r_bwin))
        for j in range(2, NCH):
            aout_chunk(j)
            fold_gain(j - 2)
        fold_gain(15)

        if stage == "h1":
            for j in range(1, NCH):
                dma("sync", out[j - 1], h1_view(j), "o0", reads=[r_slot[j]])
            P.fence()
            P.emit(nc)
            return nc
        P.fence()

        RC.reset(); RW.reset()
        hnkv = RC.take(BF16, [128, D]); r_hnkv = P.res()
        hnb = RC.take(BF16, [128, D]); r_hnb = P.res()
        hnkvT = RC.take(BF16, [128, 8, 128]); r_hnkvT = P.res()
        hnbT = RC.take(BF16, [128, 8, 128]); r_hnbT = P.res()
        qf = RC.take(F32, [128, D]); r_qf = P.res()
        qrot = RC.take(BF16, [128, D]); r_qrot = P.res()
        qT = RC.take(BF16, [128, 8, 128]); r_qT = P.res()
        sgB = [RC.take(F32, [128, D]) for _ in range(2)]; r_sgB = [P.res(), P.res()]
        PT = RC.take(BF16, [128, 2, 16, 128]); r_PT = [[P.res(), P.res()], [P.res(), P.res()]]
        on = qf; r_on = r_qf
        kvf = RW.take(F32, [128, 256]); r_kvf = P.res()
        kz = RW.take(BF16, [128, 2, 2, 128]); r_kz = P.res()
        kTz = [RW.take(BF16, [128, 2, 2, 128]) for _ in range(3)]; r_kTz = [P.res() for _ in range(3)]
        rt = [RW.take(F32, [128, 16, 32]) for _ in range(2)]; r_rt = [P.res() for _ in range(2)]
        rtk = [RW.take(F32, [128, 2, 32]) for _ in range(2)]; r_rtk = [P.res() for _ in range(2)]
        yb = RW.take(BF16, [128, D]); r_yb = P.res()
        yT = RW.take(BF16, [128, 8, 128]); r_yT = P.res()
        sqB = RW.take(BF16, [128, D]); r_sqB = P.res()
        rt2 = [RW.take(F32, [128, 16, 32]) for _ in range(2)]; r_rt2 = [P.res() for _ in range(2)]
        r_qrot_hi = P.res()
        ot = RW.take(F32, [128, D]); r_ot = P.res()
        ssB = [stat[:, 64:65], stat[:, 65:66]]
        rstdB = [stat[:, 66:67], stat[:, 67:68]]
        r_ssB = [P.res(), P.res()]
        ss2 = stat[:, 68:69]
        rstd2 = stat[:, 69:70]
        r_ss2 = P.res()
        den = stat[:, 72:88]; r_den = [P.res(), P.res()]
        r_on2 = [P.res(), P.res()]
        r_yb2 = [[P.res(), P.res()], [P.res(), P.res()]]
        ybs = [yb, hnb]
        P.op("vector", lambda e: e.memset(kz.rearrange("p a b c -> p (a b c)"), 0.0), writes=[r_kz])

        def rotary(eng, src, dst_lo, dst_hi, nh, j, rsrc, rdst, rt, r_rt):
            cb = cos_t[:, j, :].unsqueeze(1).to_broadcast([128, nh, 32])
            sb_ = sin_t[:, j, :].unsqueeze(1).to_broadcast([128, nh, 32])
            x1 = src[:, :, 0:32]
            x2 = src[:, :, 32:64]
            a, b = (rt[i][:, 0:nh, :] for i in range(2))
            tt(eng, a, x1, cb, ALU.mult, rsrc + [r_cos], [r_rt[0]])
            tt(eng, b, x2, sb_, ALU.mult, rsrc + [r_sin], [r_rt[1]])
            tt(eng, dst_lo, a, b, ALU.subtract, [r_rt[0], r_rt[1]], rdst)
            tt(eng, a, x2, cb, ALU.mult, rsrc + [r_cos], [r_rt[0]])
            tt(eng, b, x1, sb_, ALU.mult, rsrc + [r_sin], [r_rt[1]])
            tt(eng, dst_hi, a, b, ALU.add, [r_rt[0], r_rt[1]], rdst)

        def rotary_q(j, src, dst_lo, dst_hi, rsrc):
            nh = 16
            cb = cos_t[:, j, :].unsqueeze(1).to_broadcast([128, nh, 32])
            sb_ = sin_t[:, j, :].unsqueeze(1).to_broadcast([128, nh, 32])
            x1 = src[:, :, 0:32]
            x2 = src[:, :, 32:64]
            a, b = rt[0], rt[1]
            c, d_ = rt2[0], rt2[1]
            tt("gpsimd", a, x1, cb, ALU.mult, rsrc + [r_cos], [r_rt[0]])
            tt("vector", c, x2, cb, ALU.mult, rsrc + [r_cos], [r_rt2[0]])
            tt("gpsimd", b, x2, sb_, ALU.mult, rsrc + [r_sin], [r_rt[1]])
            tt("vector", d_, x1, sb_, ALU.mult, rsrc + [r_sin], [r_rt2[1]])
            tt("gpsimd", dst_lo, a, b, ALU.subtract, [r_rt[0], r_rt[1]], [r_qrot])
            tt("vector", dst_hi, c, d_, ALU.add, [r_rt2[0], r_rt2[1]], [r_qrot_hi])

        def b_a(j):
            p = j % 2
            h1 = h1_view(j)
            act(sqB, h1, AF.Square, [r_slot[j]], [r_sqB, r_ssB[p]], accum=ssB[p])
            ts("vector", rstdB[p], ssB[p], 1.0 / D, EPS, ALU.mult, ALU.add, [r_ssB[p]], [r_ssB[p]])
            P.op("gpsimd", lambda e: e.tensor_tensor(rstdB[p], rstdB[p], negh, ALU.pow),
                 reads=[r_ssB[p], r_negh], writes=[r_ssB[p]])

        def b_a2(j):
            p = j % 2
            h1 = h1_view(j)
            ts("vector", hnkv, h1, rstdB[p], None, ALU.mult, None, [r_slot[j], r_ssB[p]], [r_hnkv])

        def b_b_part(j, part):
            which, half = divmod(part, 4)
            if which == 1:
                return
            src, rsrc, bk, dst, rdst = ((hnkv, r_hnkv, 4, hnkvT, r_hnkvT), (hnb, r_hnb, TRB_HNB, hnbT, r_hnbT))[which]
            tb = bank_bf(bk)
            for kc in range(half * 2, half * 2 + 2):
                tr(tb[:, kc * 128:(kc + 1) * 128], src[:, kc * 128:(kc + 1) * 128], ident,
                   [rsrc, r_ident], [bank_res[bk]], kc == 7)
            if half == 3:
                cp("vector", dst, tb.rearrange("p (a b) -> p a b", a=8), [bank_res[bk]], [rdst])

        def b_b(j):
            for part in range(8):
                b_b_part(j, part)

        def b_c_kv(j):
            for kc in range(8):
                mm(bank(KVB)[:, 0:256], hnkvT[:, kc, :], wkv[:, kc, :], kc == 0, kc == 7,
                   [r_hnkvT, r_wkv], [bank_res[KVB]], kc == 7)
            tt("vector", kvf, bank(KVB)[:, 0:256], bkv_bc, ALU.add, [bank_res[KVB], r_bkv], [r_kvf])
            ksrc = kvf[:, 0:128].rearrange("p (h d) -> p h d", h=2)
            rotary("vector", ksrc, kz[:, :, 0, 0:32], kz[:, :, 0, 32:64], 2, j, [r_kvf], [r_kz], rtk, r_rtk)
            cp("vector", kz[:, :, 1, 64:128], kz[:, :, 0, 0:64], [r_kz], [r_kz])
            va = vaug[j % 3]; rva = r_vaug[j % 3]
            cp("vector", va[:, :, 0:64], kvf[:, 128:256].rearrange("p (h d) -> p h d", h=2), [r_kvf], [rva])

        def b_c_q(j):
            if j >= 1:
                for s in range(4):
                    for kc in range(8):
                        mm(bank(s), hnkvT[:, kc, :], bwin[:, kc, s * 512:(s + 1) * 512], kc == 0, kc == 7,
                           [r_hnkvT, r_bwin[s]], [bank_res[s]], kc == 7)
                sg = sgB[j % 2]; rsg = r_sgB[j % 2]
                for s in range(2):
                    tt("vector", qf[:, s * 512:(s + 1) * 512], bank(s), bq_bc[:, s * 512:(s + 1) * 512], ALU.add,
                       [bank_res[s], r_bq], [r_qf, r_on2[s]])
                for s in range(2):
                    act(sg[:, s * 512:(s + 1) * 512], bank(2 + s), AF.Tanh, [bank_res[2 + s]], [rsg], scale=0.5)
                for s in range(2):
                    stt(sg[:, s * 512:(s + 1) * 512], sg[:, s * 512:(s + 1) * 512], 1.0, bank(2 + s),
                        ALU.add, ALU.mult, [rsg, bank_res[2 + s]], [rsg])
                q3 = qf.rearrange("p (h d) -> p h d", h=16)
                qr3 = qrot.rearrange("p (h d) -> p h d", h=16)
                rotary_q(j, q3, qr3[:, :, 0:32], qr3[:, :, 32:64], [r_qf])

        def b_d(j):
            tb = bank_bf(KZB)
            for hk in range(2):
                for par in range(2):
                    c0 = (hk * 2 + par) * 128
                    tr(tb[:, c0:c0 + 128], kz[:, hk, par, :], ident, [r_kz, r_ident], [bank_res[KZB]],
                       hk == 1 and par == 1)
            cp("scalar", kTz[j % 3], tb[:, 0:512].rearrange("p (a b c) -> p a b c", a=2, b=2),
               [bank_res[KZB]], [r_kTz[j % 3]])
            if j >= 1:
                tb = bank_bf(TRB)
                for kc in range(8):
                    tr(tb[:, kc * 128:(kc + 1) * 128], qrot[:, kc * 128:(kc + 1) * 128], ident,
                       [r_qrot, r_qrot_hi, r_ident], [bank_res[TRB]], kc == 7)
                cp("vector", qT, tb.rearrange("p (a b) -> p a b", a=8), [bank_res[TRB]], [r_qT])

        st_rot = [0]

        def b_f(j, inter=None):
            npair = 0
            for hk in range(2):
                for kb in range(2):
                    jk = j - 1 + kb
                    kTk = kTz[jk % 3]; rkTk = r_kTz[jk % 3]
                    ng = negc if kb == 1 else (negp0 if j == 1 else negp)
                    rng_ = r_negc if kb == 1 else (r_negp0 if j == 1 else r_negp)
                    for par in range(2):
                        b = (8 - ST_RING) + st_rot[0] % ST_RING
                        st_rot[0] += 1
                        ob = bank(b).rearrange("p (a b) -> p a b", a=4)
                        mm(ob, ident, ng, True, False, [r_ident, rng_], [bank_res[b]], False)
                        mm(ob, kTk[:, hk, par, :], qT[:, hk * 4:(hk + 1) * 4, :], False, True,
                           [rkTk, r_qT], [bank_res[b]], True)
                        pv = PT[:, kb, hk * 8 + par:hk * 8 + 8:2, :]
                        act(pv, ob, AF.Exp, [bank_res[b]], [r_PT[kb][hk]], scale=0.125)
                        if inter is not None:
                            inter(npair)
                        npair += 1

        def b_g(j):
            o4 = PS[:, 0:2048].rearrange("p (h c) -> p h c", h=16)
            for h in range(16):
                hk = h // 8
                b = h // 4
                for kb in range(2):
                    jk = j - 1 + kb
                    mm(o4[:, h, 0:65], PT[:, kb, h, :], vaug[jk % 3][:, hk, :], kb == 0, kb == 1,
                       [r_PT[kb][hk], r_vaug[jk % 3]], [bank_res[b]], (kb == 1 and h % 4 == 3))
                if h % 8 == 7:
                    hh = h // 8
                    ro = [bank_res[2 * hh], bank_res[2 * hh + 1]]
                    hs = slice(hh * 8, hh * 8 + 8)
                    cs = slice(hh * 512, hh * 512 + 512)
                    dn = den[:, hs]
                    stt(dn, o4[:, hs, 64], 2.0, sinkexp[:, hs], ALU.mult, ALU.add, ro + [r_sink], [r_den[hh]])
                    P.op("vector", (lambda d_: (lambda e: e.reciprocal(d_, d_)))(dn), reads=[r_den[hh]], writes=[r_den[hh]])
                    tt("vector", on[:, cs].rearrange("p (h d) -> p h d", h=8), o4[:, hs, 0:64],
                       dn.unsqueeze(2).to_broadcast([128, 8, 64]), ALU.mult, ro + [r_den[hh]], [r_on2[hh], r_qf])
                    tt("gpsimd", ybs[j % 2][:, cs], on[:, cs], sgB[j % 2][:, cs], ALU.mult,
                       [r_on2[hh], r_sgB[j % 2]], [r_yb2[j % 2][hh]])

        def b_h(j):
            tb = bank_bf(YTB)
            for kc in range(8):
                tr(tb[:, kc * 128:(kc + 1) * 128], ybs[j % 2][:, kc * 128:(kc + 1) * 128], ident,
                   [r_yb2[j % 2][kc // 4], r_ident], [bank_res[YTB]], kc == 7)
            cp("scalar", yT, tb.rearrange("p (a b) -> p a b", a=8), [bank_res[YTB]], [r_yT])

        def b_i(j):
            h1 = h1_view(j)
            for hf in range(2):
                b = WO_BANKS[hf]
                for kc in range(8):
                    mm(bank(b), yT[:, kc, :], bwout[:, kc, hf * 512:(hf + 1) * 512], kc == 0, kc == 7,
                       [r_yT, r_bwout], [bank_res[b]], kc == 7)
            for hf in range(2):
                b = WO_BANKS[hf]
                tt("vector", h1[:, hf * 512:(hf + 1) * 512], bank(b), h1[:, hf * 512:(hf + 1) * 512], ALU.add,
                   [bank_res[b], r_slot[j]], [r_slot[j]])
            h2 = h1
            r_h2 = r_slot[j]
            act(sqB, h2, AF.Square, [r_h2], [r_sqB, r_ss2], accum=ss2)
            ts("vector", rstd2, ss2, 1.0 / D, EPS, ALU.mult, ALU.add, [r_ss2], [r_ss2])
            P.op("gpsimd", lambda e: e.tensor_tensor(rstd2, rstd2, negh, ALU.pow),
                 reads=[r_ss2, r_negh], writes=[r_ss2])
            stt(ot, h2, rstd2, fg_bc, ALU.mult, ALU.mult, [r_h2, r_ss2, r_fg], [r_ot])
            dma("sync", out[j - 1], ot, "o0", reads=[r_ot])

        b_a(0)
        b_a2(0)
        b_b(0)
        for i in range(NCH + 3):
            if i + 1 < NCH:
                b_a(i + 1)
            if i < NCH:
                b_c_kv(i)
            if i + 1 < NCH:
                b_a2(i + 1)
            if 1 <= i - 3 < NCH:
                b_i(i - 3)
            if i < NCH:
                b_c_q(i)
            if 1 <= i - 1 < NCH:
                if i + 1 < NCH:
                    b_f(i - 1, inter=(lambda jj: (lambda k: b_b_part(jj, k)))(i + 1))
                else:
                    b_f(i - 1)
                b_g(i - 1)
            elif i + 1 < NCH:
                b_b(i + 1)
            if 1 <= i - 2 < NCH:
                b_h(i - 2)
            if i < NCH:
                b_d(i)
        P.fence()
        P.emit(nc)
    return nc


def _host_inputs(inputs):
    x = np.ascontiguousarray(np.asarray(inputs["x"], dtype=np.float32))
    sq = lambda k: np.ascontiguousarray(np.asarray(inputs[k], dtype=np.float32))
    shared = {
        "a_norm_g": sq("a_norm_g")[0], "a_w_in": sq("a_w_in")[0], "a_ln_g": sq("a_ln_g")[0],
        "a_ln_b": sq("a_ln_b")[0], "a_ws": sq("a_ws")[0], "a_bs": sq("a_bs")[0].reshape(-1),
        "a_w_out": sq("a_w_out")[0], "kv_norm_g": sq("kv_norm_g"), "w_kv": sq("w_kv"), "b_kv": sq("b_kv"),
        "b_norm_g": sq("b_norm_g")[0], "b_w_in": sq("b_w_in")[0], "b_bq": sq("b_bq")[0],
        "b_sinks": sq("b_sinks")[0], "b_w_out": sq("b_w_out")[0], "final_norm_g": sq("final_norm_g"),
    }
    shared = {k: np.ascontiguousarray(v) for k, v in shared.items()}
    k_i = np.arange(128)[:, None]
    t_i = np.arange(128)[None, :]
    shared["ident"] = np.eye(128, dtype=np.float32)
    shared["maskc"] = (k_i <= t_i).astype(np.float32)
    NEG = np.float32(-30000.0)
    negc = np.where(k_i <= t_i, np.float32(0), NEG).astype(np.float32)
    negp = np.where(k_i > t_i, np.float32(0), NEG).astype(np.float32)
    rep4 = lambda m: np.ascontiguousarray(np.repeat(m[:, None, :], 4, axis=1))
    shared["negc"] = rep4(negc)
    shared["negp"] = rep4(negp)
    inv_freq = (10000.0 ** (-np.arange(0, 64, 2, dtype=np.float32) / 64)).astype(np.float32)
    in_maps = []
    for c in range(NCORES):
        b, hf = divmod(c, 2)
        xc = np.zeros((NCH, 128, D), np.float32)
        xc[1:] = x[b, hf * 2048:(hf + 1) * 2048].reshape(16, 128, D)
        if hf == 1:
            xc[0] = x[b, 2048 - 128:2048]
        pos = (hf * 2048 - 128 + np.arange(NCH * 128)).astype(np.float32)
        ang = pos[:, None] * inv_freq[None, :]
        cos_t = np.cos(ang).astype(np.float32).reshape(NCH, 128, 32).transpose(1, 0, 2)
        sin_t = np.sin(ang).astype(np.float32).reshape(NCH, 128, 32).transpose(1, 0, 2)
        m = dict(shared)
        m["x"] = xc
        m["cos_t"] = np.ascontiguousarray(cos_t)
        m["sin_t"] = np.ascontiguousarray(sin_t)
        m["negp0"] = shared["negp"] if hf == 1 else np.full((128, 4, 128), NEG, np.float32)
        in_maps.append(m)
    return in_maps


def run(inputs, stage="full"):
    in_maps = _host_inputs(inputs)
    nc = build(stage)
    res = run_bass_kernel_spmd(nc, in_maps, core_ids=list(range(NCORES)))
    outs = [np.asarray(r["out"]).reshape(2048, D) for r in res.results]
    full = np.stack([np.concatenate(outs[2 * b:2 * b + 2], axis=0) for b in range(4)], axis=0)
    return full.astype(np.float32)


def kernel(**inputs):
    return run(inputs, "full")
```

```python
from contextlib import ExitStack
import numpy as np
import concourse.bass as bass
import concourse.mybir as mybir
from concourse.bass_utils import run_bass_kernel_spmd

F32 = mybir.dt.float32
BF16 = mybir.dt.bfloat16
ALU = mybir.AluOpType
AF = mybir.ActivationFunctionType

ENGS = ["sync", "scalar", "vector", "gpsimd", "tensor"]
NCORES = 8
NCH = 17
D = 1024
AW = 2048
EPS = 1e-5
AV_ORDER = 1
AV_NORM = "pool"
AV_PARTS = 4
ROT_ENG = "gpsimd"
AV_SVBANK = "v"
ST_RING = 3
TRB = 5
TRB_HNB = 3
KZB = 6
KVB = 4
YTB = 7
WO_BANKS = (6, 7)
AV_VMM_ACT = 1
AV_VMM_BN = 1


class Res:
    __slots__ = ("name", "w", "r")

    def __init__(self, name):
        self.name = name
        self.w = None
        self.r = []


class Prog:
    def __init__(self):
        self.ops = {e: [] for e in ENGS}
        self.seen = {e: {} for e in ENGS}
        self.dcnt = {}
        self.nres = 0

    def res(self, name=None):
        self.nres += 1
        return Res(name or f"r{self.nres}")

    def _need(self, eng, deps, tok, raw):
        if tok is None:
            return
        key, seq, peng = tok
        if peng == eng and not raw:
            return
        if self.seen[eng].get(key, -1) >= seq:
            return
        self.seen[eng][key] = seq
        deps[key] = max(deps.get(key, -1), seq)

    def _deps(self, eng, reads, writes):
        deps = {}
        for r in reads:
            self._need(eng, deps, r.w, True)
        for w in writes:
            self._need(eng, deps, w.w, False)
            for t in w.r:
                self._need(eng, deps, t, False)
        return deps

    def _commit(self, tok, reads, writes):
        for r in reads:
            r.r.append(tok)
        for w in writes:
            w.w = tok
            w.r = []

    def op(self, eng, fn, reads=(), writes=(), signal=True):
        deps = self._deps(eng, reads, writes)
        seq = len(self.ops[eng])
        tok = ("E_" + eng, seq, eng)
        self._commit(tok, reads, writes)
        self.ops[eng].append(dict(fn=fn, deps=deps, sig_ok=signal, dma=None))

    def dma(self, eng, fn, sem, reads=(), writes=()):
        deps = self._deps(eng, reads, writes)
        key = "D_" + sem
        n = self.dcnt.get(key, 0) + 1
        self.dcnt[key] = n
        tok = (key, n, "dma:" + key)
        self._commit(tok, reads, writes)
        self.ops[eng].append(dict(fn=fn, deps=deps, sig_ok=False, dma=key))

    def fence(self):
        for e in ENGS:
            deps = {}
            for pe in ENGS:
                if pe != e:
                    last = [i for i, o in enumerate(self.ops[pe]) if o["sig_ok"]]
                    if last:
                        self._need(e, deps, ("E_" + pe, last[-1], pe), True)
            for k, n in self.dcnt.items():
                self._need(e, deps, (k, n, "dma:" + k), True)
            if deps:
                self.ops[e].append(dict(fn=None, deps=deps, sig_ok=False, dma=None))

    def emit(self, nc):
        needed = {e: set() for e in ENGS}
        sig_idx = {}
        for e in ENGS:
            idx = [i for i, o in enumerate(self.ops[e]) if o["sig_ok"]]
            sig_idx[e] = idx
        import bisect
        for e in ENGS:
            for o in self.ops[e]:
                nd = {}
                for k, seq in o["deps"].items():
                    if k.startswith("E_"):
                        pe = k[2:]
                        idx = sig_idx[pe]
                        p = bisect.bisect_left(idx, seq)
                        assert p < len(idx), ("no signalable op after", pe, seq)
                        s2 = idx[p]
                        needed[pe].add(s2)
                        nd[k] = s2
                    else:
                        nd[k] = seq
                o["deps"] = nd
        cnt_at = {}
        for e in ENGS:
            c = 0
            m = {}
            for i in range(len(self.ops[e])):
                if i in needed[e]:
                    c += 1
                    m[i] = c
            cnt_at[e] = m
        keys = ["E_" + e for e in ENGS if needed[e]] + sorted(self.dcnt.keys())
        with ExitStack() as es:
            sems = {k: es.enter_context(nc.semaphore(k)) for k in keys}
            block = es.enter_context(nc.Block())

            def mk(ename):
                def body(eng):
                    for i, o in enumerate(self.ops[ename]):
                        for k, v in o["deps"].items():
                            if k.startswith("E_"):
                                eng.wait_ge(sems[k], cnt_at[k[2:]][v])
                            else:
                                eng.wait_ge(sems[k], 16 * v)
                        if o["fn"] is None:
                            continue
                        ins = o["fn"](eng)
                        if o["dma"] is not None:
                            ins.then_inc(sems[o["dma"]], 16)
                        elif i in needed[ename]:
                            ins.then_inc(sems["E_" + ename], 1)
                return body

            for e in ENGS:
                if self.ops[e]:
                    getattr(block, e)(mk(e))
        return len(keys)


class Region:
    def __init__(self, t, nbytes):
        self.t = t
        self.nbytes = nbytes
        self.off = 0

    def reset(self):
        self.off = 0

    def take(self, dtype, shape):
        esz = 4 if dtype == F32 else 2
        n = 1
        for s in shape[1:]:
            n *= s
        nb = (n * esz + 31) // 32 * 32
        assert self.off + nb <= self.nbytes, (self.off, nb, self.nbytes)
        a = self.t[0:shape[0], self.off // 2:(self.off + n * esz) // 2]
        self.off += nb
        if dtype == F32:
            a = a.bitcast(F32)
        if len(shape) == 3:
            a = a.rearrange("p (a b) -> p a b", a=shape[1])
        elif len(shape) == 4:
            a = a.rearrange("p (a b c) -> p a b c", a=shape[1], b=shape[2])
        return a


def build(stage="full"):
    nc = bass.Bass("TRN2", target_bir_lowering=False)
    P = Prog()

    def din(name, shape):
        return nc.dram_tensor(name, list(shape), F32, kind="ExternalInput").ap()

    x = din("x", [NCH, 128, D])
    a_norm_g = din("a_norm_g", [D])
    a_w_in = din("a_w_in", [D, 3 * AW])
    a_ln_g = din("a_ln_g", [AW])
    a_ln_b = din("a_ln_b", [AW])
    a_ws = din("a_ws", [16, 128, 128])
    a_bs = din("a_bs", [AW])
    a_w_out = din("a_w_out", [AW, D])
    kv_norm_g = din("kv_norm_g", [D])
    w_kv = din("w_kv", [D, 256])
    b_kv = din("b_kv", [256])
    b_norm_g = din("b_norm_g", [D])
    b_w_in = din("b_w_in", [D, 2048])
    b_bq = din("b_bq", [D])
    b_sinks = din("b_sinks", [16])
    b_w_out = din("b_w_out", [D, D])
    final_norm_g = din("final_norm_g", [D])
    ident_d = din("ident", [128, 128])
    maskc_d = din("maskc", [128, 128])
    negc_d = din("negc", [128, 4, 128])
    negp_d = din("negp", [128, 4, 128])
    negp0_d = din("negp0", [128, 4, 128])
    cos_d = din("cos_t", [128, NCH, 32])
    sin_d = din("sin_t", [128, NCH, 32])
    out = nc.dram_tensor("out", [16, 128, D], F32, kind="ExternalOutput").ap()

    with ExitStack() as es:
        def sbt(name, nbytes):
            return Region(es.enter_context(nc.sbuf_tensor(name, [128, nbytes // 2], BF16)), nbytes)

        K = 1024
        SLOT = sbt("slot", NCH * 4 * K)
        RB = sbt("rb", 34 * K)
        RC = sbt("rc", 32 * K)
        RD = sbt("rd", 16 * K)
        RE = sbt("re", 4 * K)
        RF = sbt("rf", 22 * K)
        RW = sbt("rw", 24 * K)
        RM = sbt("rm", 5 * K + 512)
        PS = es.enter_context(nc.psum_tensor("ps", [128, 8 * 512], F32))
        bank_res = [P.res(f"bank{i}") for i in range(8)]

        def bank(i, n=1):
            return PS[:, i * 512:(i + n) * 512]

        def bank_bf(i):
            return PS[:, i * 512:(i + 1) * 512].bitcast(BF16)

        def mm(outap, lhsT, rhs, start, stop, reads, writes, signal):
            P.op("tensor", lambda e: e.matmul(outap, lhsT, rhs, start=start, stop=stop),
                 reads=reads, writes=writes, signal=signal)

        def tr(outap, inap, idap, reads, writes, signal):
            P.op("tensor", lambda e: e.transpose(outap, inap, idap),
                 reads=reads, writes=writes, signal=signal)

        def act(outap, inap, func, reads, writes, scale=1.0, bias=0.0, accum=None):
            if accum is None:
                P.op("scalar", lambda e: e.activation(outap, inap, func, bias=bias, scale=scale),
                     reads=reads, writes=writes)
            else:
                P.op("scalar", lambda e: e.activation(outap, inap, func, bias=bias, scale=scale, accum_out=accum),
                     reads=reads, writes=writes)

        def tt(eng, outap, a, b, op, reads, writes):
            P.op(eng, lambda e: e.tensor_tensor(outap, a, b, op), reads=reads, writes=writes)

        def stt(outap, a, sc, b, op0, op1, reads, writes):
            P.op("vector", lambda e: e.scalar_tensor_tensor(outap, a, sc, b, op0, op1),
                 reads=reads, writes=writes)

        def ts(eng, outap, a, s1, s2, op0, op1, reads, writes):
            if op1 is None:
                P.op(eng, lambda e: e.tensor_scalar(outap, a, s1, None, op0), reads=reads, writes=writes)
            else:
                P.op(eng, lambda e: e.tensor_scalar(outap, a, s1, s2, op0, op1), reads=reads, writes=writes)

        def cp(eng, outap, inap, reads, writes):
            if eng == "scalar":
                P.op("scalar", lambda e: e.copy(outap, inap), reads=reads, writes=writes)
            else:
                P.op(eng, lambda e: e.tensor_copy(outap, inap), reads=reads, writes=writes)

        def dma(eng, outap, inap, sem, reads=(), writes=(), slow=False):
            if slow:
                P.dma(eng, lambda e: e.dma_start(out=outap, in_=inap, allow_slow_non_contiguous=True),
                      sem, reads=reads, writes=writes)
            else:
                P.dma(eng, lambda e: e.dma_start(out=outap, in_=inap), sem, reads=reads, writes=writes)

        def rstd_from(outap, ssap, n, reads, writes):
            ts("vector", outap, ssap, 1.0 / n, EPS, ALU.mult, ALU.add, reads, writes)
            P.op("gpsimd", lambda e: e.tensor_tensor(outap, outap, negh, ALU.pow),
                 reads=list(writes) + [r_negh], writes=writes)

        ident = RM.take(BF16, [128, 128]); r_ident = P.res()
        maskc = RM.take(BF16, [128, 128]); r_maskc = P.res()
        ones_bf = RM.take(BF16, [128, 128]); r_ones = P.res()
        negc = RM.take(BF16, [128, 4, 128]); r_negc = P.res()
        negp = RM.take(BF16, [128, 4, 128]); r_negp = P.res()
        negp0 = RM.take(BF16, [128, 4, 128]); r_negp0 = P.res()
        lg_pp = RM.take(F32, [128, 16]); r_lg = P.res()
        lb_pp = RM.take(F32, [128, 16]); r_lb = P.res()
        sinkexp = RM.take(F32, [128, 16]); r_sink = P.res()
        stat = RM.take(F32, [128, 128])
        negh = RM.take(F32, [128, 1]); r_negh = P.res()
        vaug = [RM.take(BF16, [128, 2, 65]) for _ in range(3)]
        r_vaug = [P.res() for _ in range(3)]
        wkv = RE.take(BF16, [128, 8, 256]); r_wkv = P.res()

        dma("gpsimd", ident, ident_d, "c0", writes=[r_ident])
        dma("gpsimd", maskc, maskc_d, "c1", writes=[r_maskc])
        dma("sync", lg_pp, a_ln_g.rearrange("(g d) -> d g", g=16), "c4", writes=[r_lg], slow=True)
        dma("sync", lb_pp, a_ln_b.rearrange("(g d) -> d g", g=16), "c5", writes=[r_lb], slow=True)
        dma("sync", sinkexp, b_sinks.partition_broadcast(128), "c6", writes=[r_sink])
        act(sinkexp, sinkexp, AF.Exp, [r_sink], [r_sink])
        ts("vector", sinkexp, sinkexp, 2.0, None, ALU.mult, None, [r_sink], [r_sink])
        P.op("vector", lambda e: e.memset(ones_bf, 1.0), writes=[r_ones])
        P.op("vector", lambda e: e.memset(negh, -0.5), writes=[r_negh])
        for i in range(3):
            P.op("vector", (lambda v: (lambda e: e.memset(v, 1.0)))(vaug[i]), writes=[r_vaug[i]])

        ga_bc = RF.take(F32, [128, D]); r_ga = P.res()
        Cc = RF.take(F32, [128, 16, 128]); r_C = P.res()
        wsT = RF.take(BF16, [128, 16, 128]); r_wsT = P.res()
        dma("sync", ga_bc, a_norm_g.partition_broadcast(128), "c7", writes=[r_ga])
        wug0 = RF.take(BF16, [128, 8, 2, 128])
        RE.reset()
        wug1 = RE.take(BF16, [128, 8, 2, 128])

        RW.reset()
        xs = [RW.take(F32, [128, D]) for _ in range(2)]; r_xs = [P.res() for _ in range(2)]
        sq = RW.take(BF16, [128, D]); r_sq = P.res()
        hn = RW.take(BF16, [128, D]); r_hn = P.res()
        vn = [RW.take(BF16, [128, AW]) for _ in range(2)]; r_vn = [P.res() for _ in range(2)]
        ident_f = RW.take(F32, [128, 128]); r_identf = P.res()
        RD.reset()
        vraw = [RD.take(F32, [128, AW]) for _ in range(2)]
        r_vraw = [[P.res() for _ in range(4)] for _ in range(2)]
        ws_st = vraw[0].rearrange("p (g t) -> p g t", g=16); r_wsst_l = r_vraw[0]
        bs_bc = vraw[1].rearrange("p (g t) -> p g t", g=16); r_bs_l = r_vraw[1]
        dma("sync", ws_st, a_ws.rearrange("g t s -> t g s"), "c8", writes=r_wsst_l)
        dma("sync", bs_bc, a_bs.partition_broadcast(128).rearrange("p (g t) -> p g t", g=16), "c9", writes=r_bs_l)
        dma("sync", ident_f, ident_d, "c10", writes=[r_identf])

        Wv = RC.take(BF16, [128, 8, 2048])
        r_Wv = [P.res() for _ in range(4)]
        w_in_r = a_w_in.rearrange("(kc p) n -> p kc n", p=128)
        for s in range(4):
            dma("gpsimd", Wv[:, :, s * 512:(s + 1) * 512], w_in_r[:, :, AW + s * 512:AW + (s + 1) * 512],
                f"wv{s}", writes=[r_Wv[s]])

        dma("gpsimd", negc, negc_d, "c2", writes=[r_negc])
        dma("gpsimd", negp, negp_d, "c3", writes=[r_negp])
        dma("gpsimd", negp0, negp0_d, "c11", writes=[r_negp0])

        def setup_compute():
            for gq in range(4):
                b = gq % 2
                for gi in range(4):
                    g = gq * 4 + gi
                    tr(bank(b)[:, gi * 128:(gi + 1) * 128], ws_st[:, g, :], ident_f,
                       r_wsst_l + [r_identf], [bank_res[b]], gi == 3)
                tt("vector", wsT[:, gq * 4:(gq + 1) * 4, :],
                   bank(b).rearrange("p (a b) -> p a b", a=4),
                   maskc.unsqueeze(1).to_broadcast([128, 4, 128]), ALU.mult,
                   [bank_res[b], r_maskc], [r_wsT])
            for gq in range(4):
                b = (5, 6, 7, 4)[gq]
                mm(bank(b), ones_bf, wsT[:, gq * 4:(gq + 1) * 4, :].rearrange("p a b -> p (a b)"), True, True,
                   [r_ones, r_wsT], [bank_res[b]], True)
                for gi in range(4):
                    g = gq * 4 + gi
                    stt(Cc[:, g, :], bank(b)[:, gi * 128:(gi + 1) * 128], lb_pp[:, g:g + 1], bs_bc[:, g, :],
                        ALU.mult, ALU.add, [bank_res[b], r_lb] + r_bs_l, [r_C])

        hnT = RB.take(BF16, [128, 8, NCH * 128])
        r_hnT = [P.res() for _ in range(NCH)]
        r_slot = [P.res() for _ in range(NCH)]

        def S_view(j):
            return SLOT.t[:, j * 2048:(j + 1) * 2048].rearrange("p (g t) -> p g t", g=16)

        def h1_view(j):
            return SLOT.t[:, j * 2048:(j + 1) * 2048].bitcast(F32)

        ssA = [stat[:, 0:1], stat[:, 1:2]]
        rstdA = [stat[:, 2:3], stat[:, 3:4]]
        r_ssA = [P.res(), P.res()]
        bnst = [stat[:, 8:32], stat[:, 32:56]]
        mv = [stat[:, 56:58], stat[:, 58:60]]
        rstdv = [stat[:, 60:61], stat[:, 61:62]]
        nmr = [stat[:, 62:63], stat[:, 63:64]]
        r_bn = [P.res(), P.res()]

        def av_front(j):
            p = j % 2
            xb = xs[p]; rxb = r_xs[p]
            dma("sync", xb, x[j], f"x{p}", writes=[rxb])
            act(sq, xb, AF.Square, [rxb], [r_sq, r_ssA[p]], accum=ssA[p])
            ts("vector", rstdA[p], ssA[p], 1.0 / D, EPS, ALU.mult, ALU.add, [r_ssA[p]], [r_ssA[p]])
            P.op("gpsimd", lambda e: e.tensor_tensor(rstdA[p], rstdA[p], negh, ALU.pow),
                 reads=[r_ssA[p], r_negh], writes=[r_ssA[p]])

        def av_front_a2(j):
            p = j % 2
            xb = xs[p]; rxb = r_xs[p]
            stt(hn, xb, rstdA[p], ga_bc, ALU.mult, ALU.mult, [rxb, r_ssA[p], r_ga], [r_hn])

        def av_front_b(j):
            tb = bank_bf(4)
            for kc in range(8):
                tr(tb[:, kc * 128:(kc + 1) * 128], hn[:, kc * 128:(kc + 1) * 128], ident,
                   [r_hn, r_ident], [bank_res[4]], kc == 7)
            cp("scalar", hnT[:, :, j * 128:(j + 1) * 128], tb.rearrange("p (a b) -> p a b", a=8),
               [bank_res[4]], [r_hnT[j]])

        def av_vmm(j, slices=range(4)):
            p = j % 2
            for s in slices:
                for kc in range(8):
                    mm(bank(s), hnT[:, kc, j * 128:(j + 1) * 128], Wv[:, kc, s * 512:(s + 1) * 512],
                       kc == 0, kc == 7, [r_hnT[j], r_Wv[s]], [bank_res[s]], kc == 7)
                if AV_VMM_ACT:
                    cp("scalar", vraw[p][:, s * 512:(s + 1) * 512], bank(s), [bank_res[s]], [r_vraw[p][s]])
                if AV_VMM_BN:
                    P.op("vector", (lambda o, i: (lambda e: e.bn_stats(o, i)))(bnst[p][:, s * 6:(s + 1) * 6],
                                                                               vraw[p][:, s * 512:(s + 1) * 512]),
                         reads=[r_vraw[p][s]], writes=[r_bn[p]])

        def av_mid(j):
            p = j % 2
            P.op("vector", lambda e: e.bn_aggr(mv[p], bnst[p]), reads=[r_bn[p]], writes=[r_bn[p]])
            ts("vector", rstdv[p], mv[p][:, 1:2], EPS, None, ALU.add, None, [r_bn[p]], [r_bn[p]])
            P.op("gpsimd", lambda e: e.tensor_tensor(rstdv[p], rstdv[p], negh, ALU.pow),
                 reads=[r_bn[p], r_negh], writes=[r_bn[p]])
            stt(nmr[p], mv[p][:, 0:1], -1.0, rstdv[p], ALU.mult, ALU.mult, [r_bn[p]], [r_bn[p]])
            if AV_NORM == "pool":
                ts("gpsimd", vn[p], vraw[p], rstdv[p], nmr[p], ALU.mult, ALU.add, r_vraw[p] + [r_bn[p]], [r_vn[p]])
            elif AV_NORM == "dve":
                ts("vector", vn[p], vraw[p], rstdv[p], nmr[p], ALU.mult, ALU.add, r_vraw[p] + [r_bn[p]], [r_vn[p]])
            else:
                act(vn[p], vraw[p], AF.Identity, r_vraw[p] + [r_bn[p]], [r_vn[p]], scale=rstdv[p], bias=nmr[p])

        sv_rot = [0]

        def av_back(j, gqs=range(4)):
            p = j % 2
            Sj = S_view(j)
            for gq in gqs:
                b = (5, 6, 7, 3)[gq] if AV_SVBANK == "v" else 5 + sv_rot[0] % 3
                sv_rot[0] += 1
                for gi in range(4):
                    g = gq * 4 + gi
                    mm(bank(b)[:, gi * 128:(gi + 1) * 128], vn[p][:, g * 128:(g + 1) * 128], wsT[:, g, :],
                       True, True, [r_vn[p], r_wsT], [bank_res[b]], gi == 3)
                for gi in range(4):
                    g = gq * 4 + gi
                    stt(Sj[:, g, :], bank(b)[:, gi * 128:(gi + 1) * 128], lg_pp[:, g:g + 1], Cc[:, g, :],
                        ALU.mult, ALU.add, [bank_res[b], r_lg, r_C], [r_slot[j]])

        r_wug = [P.res() for _ in range(4)]
        r_wug2 = [P.res() for _ in range(4)]
        wug_pre = [wug0, wug1]

        def load_wug_pre(g):
            dma("gpsimd", wug_pre[g][:, :, 0, :], w_in_r[:, :, g * 128:(g + 1) * 128], f"wugu{g}",
                writes=[r_wug[g]])
            dma("gpsimd", wug_pre[g][:, :, 1, :], w_in_r[:, :, 2 * AW + g * 128:2 * AW + (g + 1) * 128],
                f"wugg{g}", writes=[r_wug2[g]])

        if AV_ORDER == 1:
            av_front(0)
            av_front_a2(0)
            av_front_b(0)
            av_front(1)
            setup_compute()
            for j in range(NCH):
                if j + 2 < NCH:
                    av_front(j + 2)
                if j + 1 < NCH:
                    av_front_a2(j + 1)
                av_vmm(j)
                if j + 1 < NCH:
                    av_front_b(j + 1)
                if j >= 1:
                    av_back(j - 1)
                av_mid(j)
                if j == NCH - 4:
                    load_wug_pre(0)
                    load_wug_pre(1)
            av_back(NCH - 1)
        else:
            for j in range(NCH):
                av_front(j)
                av_front_a2(j)
                av_front_b(j)
                if AV_PARTS >= 2:
                    av_vmm(j)
                if AV_PARTS >= 3:
                    av_mid(j)
                if AV_PARTS >= 4:
                    av_back(j)
        if stage == "av":
            for j in range(1, NCH):
                dma("sync", out[j - 1], h1_view(j), "o0", reads=[r_slot[j]])
            P.fence()
            P.emit(nc)
            return nc
        AUG_PB = [6, 4, 0, 2]
        for (j0_, nj_), pb_ in zip([(0, 4), (4, 4)], AUG_PB[:2]):
            n_ = nj_ * 128
            rh_ = [r_hnT[j] for j in range(j0_, j0_ + nj_)]
            for kc in range(8):
                mm(bank(pb_)[:, 0:n_], wug0[:, kc, 0, :], hnT[:, kc, j0_ * 128:j0_ * 128 + n_],
                   kc == 0, kc == 7, rh_ + [r_wug[0]], [bank_res[pb_]], kc == 7)
            for kc in range(8):
                mm(bank(pb_ + 1)[:, 0:n_], wug0[:, kc, 1, :], hnT[:, kc, j0_ * 128:j0_ * 128 + n_],
                   kc == 0, kc == 7, rh_ + [r_wug2[0]], [bank_res[pb_ + 1]], kc == 7)
        P.fence()

        RC.reset()
        Wout = RC.take(BF16, [128, 16, D]); r_Wout = [P.res() for _ in range(2)]
        w_out_r = a_w_out.rearrange("(g p) n -> p g n", p=128)
        RW.reset()
        sgs = [RW.take(F32, [128, 512]) for _ in range(3)]; r_sgs = [P.res() for _ in range(3)]
        tus = [RW.take(F32, [128, 512]) for _ in range(3)]; r_tus = [P.res() for _ in range(3)]
        xs_o = [RW.take(F32, [128, D]) for _ in range(2)]; r_xso = [P.res() for _ in range(2)]
        RD.reset()
        wug = [wug0, wug1] + [RD.take(BF16, [128, 8, 2, 128]) for _ in range(2)]
        batches = [(0, 4), (4, 4), (8, 3), (11, 3), (14, 3)]
        S4 = SLOT.t[:, :].rearrange("p (j g t) -> p j g t", j=NCH, g=16)
        w_in_4 = a_w_in.rearrange("(kc p) (th n) -> p kc th n", p=128, th=3)

        def load_wug(g):
            dma("gpsimd", wug[g % 4][:, :, 0, :], w_in_r[:, :, g * 128:(g + 1) * 128], f"wugu{g % 4}",
                writes=[r_wug[g % 4]])
            dma("gpsimd", wug[g % 4][:, :, 1, :], w_in_r[:, :, 2 * AW + g * 128:2 * AW + (g + 1) * 128],
                f"wugg{g % 4}", writes=[r_wug2[g % 4]])

        it = 0
        for g in range(16):
            if g + 2 < 16:
                load_wug(g + 2)
            if g == 1:
                for hf in range(2):
                    dma("gpsimd", Wout[:, hf * 8:(hf + 1) * 8, :], w_out_r[:, hf * 8:(hf + 1) * 8, :],
                        f"wout{hf}", writes=[r_Wout[hf]])
            if g == 14:
                for jj in range(2):
                    dma("sync", xs_o[jj], x[jj], f"x{jj}", writes=[r_xso[jj]])
            wb = wug[g % 4]; rwb = r_wug[g % 4]; rwb2 = r_wug2[g % 4]
            for (j0, nj) in batches:
                n = nj * 128
                pb = AUG_PB[it % 4]
                pre_issued = it < 2
                sg = sgs[it % 3]; rsg = r_sgs[it % 3]
                tu = tus[it % 3]; rtu = r_tus[it % 3]
                it += 1
                rh = [r_hnT[j] for j in range(j0, j0 + nj)]
                rs = [r_slot[j] for j in range(j0, j0 + nj)]
                for kc in range(8):
                    if pre_issued:
                        break
                    mm(bank(pb)[:, 0:n], wb[:, kc, 0, :], hnT[:, kc, j0 * 128:j0 * 128 + n],
                       kc == 0, kc == 7, rh + [rwb], [bank_res[pb]], kc == 7)
                for kc in range(8):
                    if pre_issued:
                        break
                    mm(bank(pb + 1)[:, 0:n], wb[:, kc, 1, :], hnT[:, kc, j0 * 128:j0 * 128 + n],
                       kc == 0, kc == 7, rh + [rwb2], [bank_res[pb + 1]], kc == 7)
                act(sg[:, 0:n], bank(pb + 1)[:, 0:n], AF.Silu, [bank_res[pb + 1]], [rsg])
                tt("vector", tu[:, 0:n], bank(pb)[:, 0:n], sg[:, 0:n], ALU.mult, [bank_res[pb], rsg], [rtu])
                Sv = S4[:, j0:j0 + nj, g, :]
                tt("gpsimd", Sv, Sv, tu[:, 0:n].rearrange("p (j t) -> p j t", j=nj), ALU.mult,
                   rs + [rtu], rs)
        if stage == "aug":
            for j in range(1, NCH):
                dma("sync", out[j - 1], h1_view(j), "o0", reads=[r_slot[j]])
            P.fence()
            P.emit(nc)
            return nc
        P.fence()

        RF.reset()
        kvg_pp = RF.take(F32, [128, 8]); r_kvg = P.res()
        bg_pp = RF.take(F32, [128, 8]); r_bg = P.res()
        fg_bc = RF.take(F32, [128, D]); r_fg = P.res()
        bq_bc = RF.take(F32, [128, D]); r_bq = P.res()
        bkv_bc = RF.take(F32, [128, 256]); r_bkv = P.res()
        cos_t = RF.take(F32, [128, NCH, 32]); r_cos = P.res()
        sin_t = RF.take(F32, [128, NCH, 32]); r_sin = P.res()
        dma("sync", kvg_pp, kv_norm_g.rearrange("(kc p) -> p kc", p=128), "c4", writes=[r_kvg], slow=True)
        dma("sync", bg_pp, b_norm_g.rearrange("(kc p) -> p kc", p=128), "c5", writes=[r_bg], slow=True)
        dma("sync", fg_bc, final_norm_g.partition_broadcast(128), "c6", writes=[r_fg])
        dma("sync", bq_bc, b_bq.partition_broadcast(128), "c7", writes=[r_bq])
        dma("sync", bkv_bc, b_kv.partition_broadcast(128), "c8", writes=[r_bkv])
        dma("sync", cos_t, cos_d, "c9", writes=[r_cos])
        dma("sync", sin_t, sin_d, "c10", writes=[r_sin])

        RE.reset()
        wkv = RE.take(BF16, [128, 8, 256])
        dma("gpsimd", wkv, w_kv.rearrange("(kc p) n -> p kc n", p=128), "wkv", writes=[r_wkv])
        RB.reset()
        bwin = RB.take(BF16, [128, 8, 2048]); r_bwin = [P.res() for _ in range(4)]
        RD.reset()
        bwout = RD.take(BF16, [128, 8, D]); r_bwout = P.res()
        b_w_in_r = b_w_in.rearrange("(kc p) n -> p kc n", p=128)
        for s in range(4):
            dma("gpsimd", bwin[:, :, s * 512:(s + 1) * 512], b_w_in_r[:, :, s * 512:(s + 1) * 512],
                f"wv{s}", writes=[r_bwin[s]])
        dma("gpsimd", bwout, b_w_out.rearrange("(kc p) n -> p kc n", p=128), "wout0", writes=[r_bwout])
        def fold_gain(step):
            kc = step % 8
            if step < 8:
                ts("vector", wkv[:, kc, :], wkv[:, kc, :], kvg_pp[:, kc:kc + 1], None, ALU.mult, None,
                   [r_wkv, r_kvg], [r_wkv])
            else:
                ts("vector", bwin[:, kc, :], bwin[:, kc, :], bg_pp[:, kc:kc + 1], None, ALU.mult, None,
                   list(r_bwin) + [r_bg], list(r_bwin))
        for j in range(NCH):
            xb = xs_o[j % 2]; rxb = r_xso[j % 2]
            if j >= 2:
                dma("sync", xb, x[j], f"x{j % 2}", writes=[rxb])
            Sj = S_view(j)
            pb = 4 * (j % 2)
            for hf in range(2):
                b = pb + hf
                for g in range(16):
                    mm(bank(b), Sj[:, g, :], Wout[:, g, hf * 512:(hf + 1) * 512], g == 0, g == 15,
                       [r_slot[j], r_Wout[g // 8]], [bank_res[b]], g == 15)
            h1 = h1_view(j)
            for hf in range(2):
                tt("vector", h1[:, hf * 512:(hf + 1) * 512], bank(pb + hf), xb[:, hf * 512:(hf + 1) * 512],
                   ALU.add, [bank_res[pb + hf], rxb], [r_slot[j]])
            if j >= 1:
                fold_gain(j - 1)

        if stage == "h1":
            for j in range(1, NCH):
                dma("sync", out[j - 1], h1_view(j), "o0", reads=[r_slot[j]])
            P.fence()
            P.emit(nc)
            return nc
        P.fence()

        RC.reset(); RW.reset()
        hnkv = RC.take(BF16, [128, D]); r_hnkv = P.res()
        hnb = RC.take(BF16, [128, D]); r_hnb = P.res()
        hnkvT = RC.take(BF16, [128, 8, 128]); r_hnkvT = P.res()
        hnbT = RC.take(BF16, [128, 8, 128]); r_hnbT = P.res()
        qf = RC.take(F32, [128, D]); r_qf = P.res()
        qrot = RC.take(BF16, [128, D]); r_qrot = P.res()
        qT = RC.take(BF16, [128, 8, 128]); r_qT = P.res()
        sgB = [RC.take(F32, [128, D]) for _ in range(2)]; r_sgB = [P.res(), P.res()]
        PT = RC.take(BF16, [128, 2, 16, 128]); r_PT = [[P.res(), P.res()], [P.res(), P.res()]]
        on = qf; r_on = r_qf
        kvf = RW.take(F32, [128, 256]); r_kvf = P.res()
        kz = RW.take(BF16, [128, 2, 2, 128]); r_kz = P.res()
        kTz = [RW.take(BF16, [128, 2, 2, 128]) for _ in range(3)]; r_kTz = [P.res() for _ in range(3)]
        rt = [RW.take(F32, [128, 16, 32]) for _ in range(2)]; r_rt = [P.res() for _ in range(2)]
        rtk = [RW.take(F32, [128, 2, 32]) for _ in range(2)]; r_rtk = [P.res() for _ in range(2)]
        yb = RW.take(BF16, [128, D]); r_yb = P.res()
        yT = RW.take(BF16, [128, 8, 128]); r_yT = P.res()
        sqB = RW.take(BF16, [128, D]); r_sqB = P.res()
        rt2 = [RW.take(F32, [128, 16, 32]) for _ in range(2)]; r_rt2 = [P.res() for _ in range(2)]
        r_qrot_hi = P.res()
        ot = RW.take(F32, [128, D]); r_ot = P.res()
        ssB = [stat[:, 64:65], stat[:, 65:66]]
        rstdB = [stat[:, 66:67], stat[:, 67:68]]
        r_ssB = [P.res(), P.res()]
        ss2 = stat[:, 68:69]
        rstd2 = stat[:, 69:70]
        r_ss2 = P.res()
        den = stat[:, 72:88]; r_den = [P.res(), P.res()]
        r_on2 = [P.res(), P.res()]
        r_yb2 = [[P.res(), P.res()], [P.res(), P.res()]]
        ybs = [yb, hnb]
        P.op("vector", lambda e: e.memset(kz.rearrange("p a b c -> p (a b c)"), 0.0), writes=[r_kz])

        def rotary(eng, src, dst_lo, dst_hi, nh, j, rsrc, rdst, rt, r_rt):
            cb = cos_t[:, j, :].unsqueeze(1).to_broadcast([128, nh, 32])
            sb_ = sin_t[:, j, :].unsqueeze(1).to_broadcast([128, nh, 32])
            x1 = src[:, :, 0:32]
            x2 = src[:, :, 32:64]
            a, b = (rt[i][:, 0:nh, :] for i in range(2))
            tt(eng, a, x1, cb, ALU.mult, rsrc + [r_cos], [r_rt[0]])
            tt(eng, b, x2, sb_, ALU.mult, rsrc + [r_sin], [r_rt[1]])
            tt(eng, dst_lo, a, b, ALU.subtract, [r_rt[0], r_rt[1]], rdst)
            tt(eng, a, x2, cb, ALU.mult, rsrc + [r_cos], [r_rt[0]])
            tt(eng, b, x1, sb_, ALU.mult, rsrc + [r_sin], [r_rt[1]])
            tt(eng, dst_hi, a, b, ALU.add, [r_rt[0], r_rt[1]], rdst)

        def rotary_q(j, src, dst_lo, dst_hi, rsrc):
            nh = 16
            cb = cos_t[:, j, :].unsqueeze(1).to_broadcast([128, nh, 32])
            sb_ = sin_t[:, j, :].unsqueeze(1).to_broadcast([128, nh, 32])
            x1 = src[:, :, 0:32]
            x2 = src[:, :, 32:64]
            a, b = rt[0], rt[1]
            c, d_ = rt2[0], rt2[1]
            tt("gpsimd", a, x1, cb, ALU.mult, rsrc + [r_cos], [r_rt[0]])
            tt("vector", c, x2, cb, ALU.mult, rsrc + [r_cos], [r_rt2[0]])
            tt("gpsimd", b, x2, sb_, ALU.mult, rsrc + [r_sin], [r_rt[1]])
            tt("vector", d_, x1, sb_, ALU.mult, rsrc + [r_sin], [r_rt2[1]])
            tt("gpsimd", dst_lo, a, b, ALU.subtract, [r_rt[0], r_rt[1]], [r_qrot])
            tt("vector", dst_hi, c, d_, ALU.add, [r_rt2[0], r_rt2[1]], [r_qrot_hi])

        def b_a(j):
            p = j % 2
            h1 = h1_view(j)
            act(sqB, h1, AF.Square, [r_slot[j]], [r_sqB, r_ssB[p]], accum=ssB[p])
            ts("vector", rstdB[p], ssB[p], 1.0 / D, EPS, ALU.mult, ALU.add, [r_ssB[p]], [r_ssB[p]])
            P.op("gpsimd", lambda e: e.tensor_tensor(rstdB[p], rstdB[p], negh, ALU.pow),
                 reads=[r_ssB[p], r_negh], writes=[r_ssB[p]])

        def b_a2(j):
            p = j % 2
            h1 = h1_view(j)
            ts("vector", hnkv, h1, rstdB[p], None, ALU.mult, None, [r_slot[j], r_ssB[p]], [r_hnkv])

        def b_b_part(j, part):
            which, half = divmod(part, 4)
            if which == 1:
                return
            src, rsrc, bk, dst, rdst = ((hnkv, r_hnkv, 4, hnkvT, r_hnkvT), (hnb, r_hnb, TRB_HNB, hnbT, r_hnbT))[which]
            tb = bank_bf(bk)
            for kc in range(half * 2, half * 2 + 2):
                tr(tb[:, kc * 128:(kc + 1) * 128], src[:, kc * 128:(kc + 1) * 128], ident,
                   [rsrc, r_ident], [bank_res[bk]], kc == 7)
            if half == 3:
                cp("vector", dst, tb.rearrange("p (a b) -> p a b", a=8), [bank_res[bk]], [rdst])

        def b_b(j):
            for part in range(8):
                b_b_part(j, part)

        def b_c_kv(j):
            for kc in range(8):
                mm(bank(KVB)[:, 0:256], hnkvT[:, kc, :], wkv[:, kc, :], kc == 0, kc == 7,
                   [r_hnkvT, r_wkv], [bank_res[KVB]], kc == 7)
            tt("vector", kvf, bank(KVB)[:, 0:256], bkv_bc, ALU.add, [bank_res[KVB], r_bkv], [r_kvf])
            ksrc = kvf[:, 0:128].rearrange("p (h d) -> p h d", h=2)
            rotary("vector", ksrc, kz[:, :, 0, 0:32], kz[:, :, 0, 32:64], 2, j, [r_kvf], [r_kz], rtk, r_rtk)
            cp("vector", kz[:, :, 1, 64:128], kz[:, :, 0, 0:64], [r_kz], [r_kz])
            va = vaug[j % 3]; rva = r_vaug[j % 3]
            cp("vector", va[:, :, 0:64], kvf[:, 128:256].rearrange("p (h d) -> p h d", h=2), [r_kvf], [rva])

        def b_c_q(j):
            if j >= 1:
                for s in range(4):
                    for kc in range(8):
                        mm(bank(s), hnkvT[:, kc, :], bwin[:, kc, s * 512:(s + 1) * 512], kc == 0, kc == 7,
                           [r_hnkvT, r_bwin[s]], [bank_res[s]], kc == 7)
                sg = sgB[j % 2]; rsg = r_sgB[j % 2]
                for s in range(2):
                    tt("vector", qf[:, s * 512:(s + 1) * 512], bank(s), bq_bc[:, s * 512:(s + 1) * 512], ALU.add,
                       [bank_res[s], r_bq], [r_qf, r_on2[s]])
                for s in range(2):
                    act(sg[:, s * 512:(s + 1) * 512], bank(2 + s), AF.Tanh, [bank_res[2 + s]], [rsg], scale=0.5)
                for s in range(2):
                    stt(sg[:, s * 512:(s + 1) * 512], sg[:, s * 512:(s + 1) * 512], 1.0, bank(2 + s),
                        ALU.add, ALU.mult, [rsg, bank_res[2 + s]], [rsg])
                q3 = qf.rearrange("p (h d) -> p h d", h=16)
                qr3 = qrot.rearrange("p (h d) -> p h d", h=16)
                rotary_q(j, q3, qr3[:, :, 0:32], qr3[:, :, 32:64], [r_qf])

        def b_d(j):
            tb = bank_bf(KZB)
            for hk in range(2):
                for par in range(2):
                    c0 = (hk * 2 + par) * 128
                    tr(tb[:, c0:c0 + 128], kz[:, hk, par, :], ident, [r_kz, r_ident], [bank_res[KZB]],
                       hk == 1 and par == 1)
            cp("scalar", kTz[j % 3], tb[:, 0:512].rearrange("p (a b c) -> p a b c", a=2, b=2),
               [bank_res[KZB]], [r_kTz[j % 3]])
            if j >= 1:
                tb = bank_bf(TRB)
                for kc in range(8):
                    tr(tb[:, kc * 128:(kc + 1) * 128], qrot[:, kc * 128:(kc + 1) * 128], ident,
                       [r_qrot, r_qrot_hi, r_ident], [bank_res[TRB]], kc == 7)
                cp("vector", qT, tb.rearrange("p (a b) -> p a b", a=8), [bank_res[TRB]], [r_qT])

        st_rot = [0]

        def b_f(j, inter=None):
            npair = 0
            for hk in range(2):
                for kb in range(2):
                    jk = j - 1 + kb
                    kTk = kTz[jk % 3]; rkTk = r_kTz[jk % 3]
                    ng = negc if kb == 1 else (negp0 if j == 1 else negp)
                    rng_ = r_negc if kb == 1 else (r_negp0 if j == 1 else r_negp)
                    for par in range(2):
                        b = (8 - ST_RING) + st_rot[0] % ST_RING
                        st_rot[0] += 1
                        ob = bank(b).rearrange("p (a b) -> p a b", a=4)
                        mm(ob, ident, ng, True, False, [r_ident, rng_], [bank_res[b]], False)
                        mm(ob, kTk[:, hk, par, :], qT[:, hk * 4:(hk + 1) * 4, :], False, True,
                           [rkTk, r_qT], [bank_res[b]], True)
                        pv = PT[:, kb, hk * 8 + par:hk * 8 + 8:2, :]
                        act(pv, ob, AF.Exp, [bank_res[b]], [r_PT[kb][hk]], scale=0.125)
                        if inter is not None:
                            inter(npair)
                        npair += 1

        def b_g(j):
            o4 = PS[:, 0:2048].rearrange("p (h c) -> p h c", h=16)
            for h in range(16):
                hk = h // 8
                b = h // 4
                for kb in range(2):
                    jk = j - 1 + kb
                    mm(o4[:, h, 0:65], PT[:, kb, h, :], vaug[jk % 3][:, hk, :], kb == 0, kb == 1,
                       [r_PT[kb][hk], r_vaug[jk % 3]], [bank_res[b]], (kb == 1 and h % 4 == 3))
                if h % 8 == 7:
                    hh = h // 8
                    ro = [bank_res[2 * hh], bank_res[2 * hh + 1]]
                    hs = slice(hh * 8, hh * 8 + 8)
                    cs = slice(hh * 512, hh * 512 + 512)
                    dn = den[:, hs]
                    stt(dn, o4[:, hs, 64], 2.0, sinkexp[:, hs], ALU.mult, ALU.add, ro + [r_sink], [r_den[hh]])
                    P.op("vector", (lambda d_: (lambda e: e.reciprocal(d_, d_)))(dn), reads=[r_den[hh]], writes=[r_den[hh]])
                    tt("vector", on[:, cs].rearrange("p (h d) -> p h d", h=8), o4[:, hs, 0:64],
                       dn.unsqueeze(2).to_broadcast([128, 8, 64]), ALU.mult, ro + [r_den[hh]], [r_on2[hh], r_qf])
                    tt("gpsimd", ybs[j % 2][:, cs], on[:, cs], sgB[j % 2][:, cs], ALU.mult,
                       [r_on2[hh], r_sgB[j % 2]], [r_yb2[j % 2][hh]])

        def b_h(j):
            tb = bank_bf(YTB)
            for kc in range(8):
                tr(tb[:, kc * 128:(kc + 1) * 128], ybs[j % 2][:, kc * 128:(kc + 1) * 128], ident,
                   [r_yb2[j % 2][kc // 4], r_ident], [bank_res[YTB]], kc == 7)
            cp("scalar", yT, tb.rearrange("p (a b) -> p a b", a=8), [bank_res[YTB]], [r_yT])

        def b_i(j):
            h1 = h1_view(j)
            for hf in range(2):
                b = WO_BANKS[hf]
                for kc in range(8):
                    mm(bank(b), yT[:, kc, :], bwout[:, kc, hf * 512:(hf + 1) * 512], kc == 0, kc == 7,
                       [r_yT, r_bwout], [bank_res[b]], kc == 7)
            for hf in range(2):
                b = WO_BANKS[hf]
                tt("vector", h1[:, hf * 512:(hf + 1) * 512], bank(b), h1[:, hf * 512:(hf + 1) * 512], ALU.add,
                   [bank_res[b], r_slot[j]], [r_slot[j]])
            h2 = h1
            r_h2 = r_slot[j]
            act(sqB, h2, AF.Square, [r_h2], [r_sqB, r_ss2], accum=ss2)
            ts("vector", rstd2, ss2, 1.0 / D, EPS, ALU.mult, ALU.add, [r_ss2], [r_ss2])
            P.op("gpsimd", lambda e: e.tensor_tensor(rstd2, rstd2, negh, ALU.pow),
                 reads=[r_ss2, r_negh], writes=[r_ss2])
            stt(ot, h2, rstd2, fg_bc, ALU.mult, ALU.mult, [r_h2, r_ss2, r_fg], [r_ot])
            dma("sync", out[j - 1], ot, "o0", reads=[r_ot])

        b_a(0)
        b_a2(0)
        b_b(0)
        for i in range(NCH + 3):
            if i + 1 < NCH:
                b_a(i + 1)
            if i < NCH:
                b_c_kv(i)
            if i + 1 < NCH:
                b_a2(i + 1)
            if 1 <= i - 3 < NCH:
                b_i(i - 3)
            if i < NCH:
                b_c_q(i)
            if 1 <= i - 1 < NCH:
                if i + 1 < NCH:
                    b_f(i - 1, inter=(lambda jj: (lambda k: b_b_part(jj, k)))(i + 1))
                else:
                    b_f(i - 1)
                b_g(i - 1)
            elif i + 1 < NCH:
                b_b(i + 1)
            if 1 <= i - 2 < NCH:
                b_h(i - 2)
            if i < NCH:
                b_d(i)
        P.fence()
        P.emit(nc)
    return nc


def _host_inputs(inputs):
    x = np.ascontiguousarray(np.asarray(inputs["x"], dtype=np.float32))
    sq = lambda k: np.ascontiguousarray(np.asarray(inputs[k], dtype=np.float32))
    shared = {
        "a_norm_g": sq("a_norm_g")[0], "a_w_in": sq("a_w_in")[0], "a_ln_g": sq("a_ln_g")[0],
        "a_ln_b": sq("a_ln_b")[0], "a_ws": sq("a_ws")[0], "a_bs": sq("a_bs")[0].reshape(-1),
        "a_w_out": sq("a_w_out")[0], "kv_norm_g": sq("kv_norm_g"), "w_kv": sq("w_kv"), "b_kv": sq("b_kv"),
        "b_norm_g": sq("b_norm_g")[0], "b_w_in": sq("b_w_in")[0], "b_bq": sq("b_bq")[0],
        "b_sinks": sq("b_sinks")[0], "b_w_out": sq("b_w_out")[0], "final_norm_g": sq("final_norm_g"),
    }
    shared = {k: np.ascontiguousarray(v) for k, v in shared.items()}
    k_i = np.arange(128)[:, None]
    t_i = np.arange(128)[None, :]
    shared["ident"] = np.eye(128, dtype=np.float32)
    shared["maskc"] = (k_i <= t_i).astype(np.float32)
    NEG = np.float32(-30000.0)
    negc = np.where(k_i <= t_i, np.float32(0), NEG).astype(np.float32)
    negp = np.where(k_i > t_i, np.float32(0), NEG).astype(np.float32)
    rep4 = lambda m: np.ascontiguousarray(np.repeat(m[:, None, :], 4, axis=1))
    shared["negc"] = rep4(negc)
    shared["negp"] = rep4(negp)
    inv_freq = (10000.0 ** (-np.arange(0, 64, 2, dtype=np.float32) / 64)).astype(np.float32)
    in_maps = []
    for c in range(NCORES):
        b, hf = divmod(c, 2)
        xc = np.zeros((NCH, 128, D), np.float32)
        xc[1:] = x[b, hf * 2048:(hf + 1) * 2048].reshape(16, 128, D)
        if hf == 1:
            xc[0] = x[b, 2048 - 128:2048]
        pos = (hf * 2048 - 128 + np.arange(NCH * 128)).astype(np.float32)
        ang = pos[:, None] * inv_freq[None, :]
        cos_t = np.cos(ang).astype(np.float32).reshape(NCH, 128, 32).transpose(1, 0, 2)
        sin_t = np.sin(ang).astype(np.float32).reshape(NCH, 128, 32).transpose(1, 0, 2)
        m = dict(shared)
        m["x"] = xc
        m["cos_t"] = np.ascontiguousarray(cos_t)
        m["sin_t"] = np.ascontiguousarray(sin_t)
        m["negp0"] = shared["negp"] if hf == 1 else np.full((128, 4, 128), NEG, np.float32)
        in_maps.append(m)
    return in_maps


def run(inputs, stage="full"):
    in_maps = _host_inputs(inputs)
    nc = build(stage)
    res = run_bass_kernel_spmd(nc, in_maps, core_ids=list(range(NCORES)))
    outs = [np.asarray(r["out"]).reshape(2048, D) for r in res.results]
    full = np.stack([np.concatenate(outs[2 * b:2 * b + 2], axis=0) for b in range(4)], axis=0)
    return full.astype(np.float32)


def kernel(**inputs):
    return run(inputs, "full")
```

```python
from contextlib import ExitStack
import numpy as np
import concourse.bass as bass
import concourse.mybir as mybir
from concourse.bass_utils import run_bass_kernel_spmd

F32 = mybir.dt.float32
BF16 = mybir.dt.bfloat16
ALU = mybir.AluOpType
AF = mybir.ActivationFunctionType

ENGS = ["sync", "scalar", "vector", "gpsimd", "tensor"]
NCORES = 8
NCH = 17
D = 1024
AW = 2048
EPS = 1e-5
AV_ORDER = 1
AV_NORM = "pool"
AV_PARTS = 4
ROT_ENG = "gpsimd"
AV_SVBANK = "v"
ST_RING = 3
TRB = 5
TRB_HNB = 3
KZB = 6
KVB = 4
YTB = 7
WO_BANKS = (6, 7)
AV_VMM_ACT = 1
AV_VMM_BN = 1


class Res:
    __slots__ = ("name", "w", "r")

    def __init__(self, name):
        self.name = name
        self.w = None
        self.r = []


class Prog:
    def __init__(self):
        self.ops = {e: [] for e in ENGS}
        self.seen = {e: {} for e in ENGS}
        self.dcnt = {}
        self.nres = 0

    def res(self, name=None):
        self.nres += 1
        return Res(name or f"r{self.nres}")

    def _need(self, eng, deps, tok, raw):
        if tok is None:
            return
        key, seq, peng = tok
        if peng == eng and not raw:
            return
        if self.seen[eng].get(key, -1) >= seq:
            return
        self.seen[eng][key] = seq
        deps[key] = max(deps.get(key, -1), seq)

    def _deps(self, eng, reads, writes):
        deps = {}
        for r in reads:
            self._need(eng, deps, r.w, True)
        for w in writes:
            self._need(eng, deps, w.w, False)
            for t in w.r:
                self._need(eng, deps, t, False)
        return deps

    def _commit(self, tok, reads, writes):
        for r in reads:
            r.r.append(tok)
        for w in writes:
            w.w = tok
            w.r = []

    def op(self, eng, fn, reads=(), writes=(), signal=True):
        deps = self._deps(eng, reads, writes)
        seq = len(self.ops[eng])
        tok = ("E_" + eng, seq, eng)
        self._commit(tok, reads, writes)
        self.ops[eng].append(dict(fn=fn, deps=deps, sig_ok=signal, dma=None))

    def dma(self, eng, fn, sem, reads=(), writes=()):
        deps = self._deps(eng, reads, writes)
        key = "D_" + sem
        n = self.dcnt.get(key, 0) + 1
        self.dcnt[key] = n
        tok = (key, n, "dma:" + key)
        self._commit(tok, reads, writes)
        self.ops[eng].append(dict(fn=fn, deps=deps, sig_ok=False, dma=key))

    def fence(self):
        for e in ENGS:
            deps = {}
            for pe in ENGS:
                if pe != e:
                    last = [i for i, o in enumerate(self.ops[pe]) if o["sig_ok"]]
                    if last:
                        self._need(e, deps, ("E_" + pe, last[-1], pe), True)
            for k, n in self.dcnt.items():
                self._need(e, deps, (k, n, "dma:" + k), True)
            if deps:
                self.ops[e].append(dict(fn=None, deps=deps, sig_ok=False, dma=None))

    def emit(self, nc):
        needed = {e: set() for e in ENGS}
        sig_idx = {}
        for e in ENGS:
            idx = [i for i, o in enumerate(self.ops[e]) if o["sig_ok"]]
            sig_idx[e] = idx
        import bisect
        for e in ENGS:
            for o in self.ops[e]:
                nd = {}
                for k, seq in o["deps"].items():
                    if k.startswith("E_"):
                        pe = k[2:]
                        idx = sig_idx[pe]
                        p = bisect.bisect_left(idx, seq)
                        assert p < len(idx), ("no signalable op after", pe, seq)
                        s2 = idx[p]
                        needed[pe].add(s2)
                        nd[k] = s2
                    else:
                        nd[k] = seq
                o["deps"] = nd
        cnt_at = {}
        for e in ENGS:
            c = 0
            m = {}
            for i in range(len(self.ops[e])):
                if i in needed[e]:
                    c += 1
                    m[i] = c
            cnt_at[e] = m
        keys = ["E_" + e for e in ENGS if needed[e]] + sorted(self.dcnt.keys())
        with ExitStack() as es:
            sems = {k: es.enter_context(nc.semaphore(k)) for k in keys}
            block = es.enter_context(nc.Block())

            def mk(ename):
                def body(eng):
                    for i, o in enumerate(self.ops[ename]):
                        for k, v in o["deps"].items():
                            if k.startswith("E_"):
                                eng.wait_ge(sems[k], cnt_at[k[2:]][v])
                            else:
                                eng.wait_ge(sems[k], 16 * v)
                        if o["fn"] is None:
                            continue
                        ins = o["fn"](eng)
                        if o["dma"] is not None:
                            ins.then_inc(sems[o["dma"]], 16)
                        elif i in needed[ename]:
                            ins.then_inc(sems["E_" + ename], 1)
                return body

            for e in ENGS:
                if self.ops[e]:
                    getattr(block, e)(mk(e))
        return len(keys)


class Region:
    def __init__(self, t, nbytes):
        self.t = t
        self.nbytes = nbytes
        self.off = 0

    def reset(self):
        self.off = 0

    def take(self, dtype, shape):
        esz = 4 if dtype == F32 else 2
        n = 1
        for s in shape[1:]:
            n *= s
        nb = (n * esz + 31) // 32 * 32
        assert self.off + nb <= self.nbytes, (self.off, nb, self.nbytes)
        a = self.t[0:shape[0], self.off // 2:(self.off + n * esz) // 2]
        self.off += nb
        if dtype == F32:
            a = a.bitcast(F32)
        if len(shape) == 3:
            a = a.rearrange("p (a b) -> p a b", a=shape[1])
        elif len(shape) == 4:
            a = a.rearrange("p (a b c) -> p a b c", a=shape[1], b=shape[2])
        return a


def build(stage="full"):
    nc = bass.Bass("TRN2", target_bir_lowering=False)
    P = Prog()

    def din(name, shape):
        return nc.dram_tensor(name, list(shape), F32, kind="ExternalInput").ap()

    x = din("x", [NCH, 128, D])
    a_norm_g = din("a_norm_g", [D])
    a_w_in = din("a_w_in", [D, 3 * AW])
    a_ln_g = din("a_ln_g", [AW])
    a_ln_b = din("a_ln_b", [AW])
    a_ws = din("a_ws", [16, 128, 128])
    a_bs = din("a_bs", [AW])
    a_w_out = din("a_w_out", [AW, D])
    kv_norm_g = din("kv_norm_g", [D])
    w_kv = din("w_kv", [D, 256])
    b_kv = din("b_kv", [256])
    b_norm_g = din("b_norm_g", [D])
    b_w_in = din("b_w_in", [D, 2048])
    b_bq = din("b_bq", [D])
    b_sinks = din("b_sinks", [16])
    b_w_out = din("b_w_out", [D, D])
    final_norm_g = din("final_norm_g", [D])
    ident_d = din("ident", [128, 128])
    maskc_d = din("maskc", [128, 128])
    negc_d = din("negc", [128, 4, 128])
    negp_d = din("negp", [128, 4, 128])
    negp0_d = din("negp0", [128, 4, 128])
    cos_d = din("cos_t", [128, NCH, 32])
    sin_d = din("sin_t", [128, NCH, 32])
    out = nc.dram_tensor("out", [16, 128, D], F32, kind="ExternalOutput").ap()

    with ExitStack() as es:
        def sbt(name, nbytes):
            return Region(es.enter_context(nc.sbuf_tensor(name, [128, nbytes // 2], BF16)), nbytes)

        K = 1024
        SLOT = sbt("slot", NCH * 4 * K)
        RB = sbt("rb", 34 * K)
        RC = sbt("rc", 32 * K)
        RD = sbt("rd", 16 * K)
        RE = sbt("re", 4 * K)
        RF = sbt("rf", 22 * K)
        RW = sbt("rw", 24 * K)
        RM = sbt("rm", 5 * K + 512)
        PS = es.enter_context(nc.psum_tensor("ps", [128, 8 * 512], F32))
        bank_res = [P.res(f"bank{i}") for i in range(8)]

        def bank(i, n=1):
            return PS[:, i * 512:(i + n) * 512]

        def bank_bf(i):
            return PS[:, i * 512:(i + 1) * 512].bitcast(BF16)

        def mm(outap, lhsT, rhs, start, stop, reads, writes, signal):
            P.op("tensor", lambda e: e.matmul(outap, lhsT, rhs, start=start, stop=stop),
                 reads=reads, writes=writes, signal=signal)

        def tr(outap, inap, idap, reads, writes, signal):
            P.op("tensor", lambda e: e.transpose(outap, inap, idap),
                 reads=reads, writes=writes, signal=signal)

        def act(outap, inap, func, reads, writes, scale=1.0, bias=0.0, accum=None):
            if accum is None:
                P.op("scalar", lambda e: e.activation(outap, inap, func, bias=bias, scale=scale),
                     reads=reads, writes=writes)
            else:
                P.op("scalar", lambda e: e.activation(outap, inap, func, bias=bias, scale=scale, accum_out=accum),
                     reads=reads, writes=writes)

        def tt(eng, outap, a, b, op, reads, writes):
            P.op(eng, lambda e: e.tensor_tensor(outap, a, b, op), reads=reads, writes=writes)

        def stt(outap, a, sc, b, op0, op1, reads, writes):
            P.op("vector", lambda e: e.scalar_tensor_tensor(outap, a, sc, b, op0, op1),
                 reads=reads, writes=writes)

        def ts(eng, outap, a, s1, s2, op0, op1, reads, writes):
            if op1 is None:
                P.op(eng, lambda e: e.tensor_scalar(outap, a, s1, None, op0), reads=reads, writes=writes)
            else:
                P.op(eng, lambda e: e.tensor_scalar(outap, a, s1, s2, op0, op1), reads=reads, writes=writes)

        def cp(eng, outap, inap, reads, writes):
            if eng == "scalar":
                P.op("scalar", lambda e: e.copy(outap, inap), reads=reads, writes=writes)
            else:
                P.op(eng, lambda e: e.tensor_copy(outap, inap), reads=reads, writes=writes)

        def dma(eng, outap, inap, sem, reads=(), writes=(), slow=False):
            if slow:
                P.dma(eng, lambda e: e.dma_start(out=outap, in_=inap, allow_slow_non_contiguous=True),
                      sem, reads=reads, writes=writes)
            else:
                P.dma(eng, lambda e: e.dma_start(out=outap, in_=inap), sem, reads=reads, writes=writes)

        def rstd_from(outap, ssap, n, reads, writes):
            ts("vector", outap, ssap, 1.0 / n, EPS, ALU.mult, ALU.add, reads, writes)
            P.op("gpsimd", lambda e: e.tensor_tensor(outap, outap, negh, ALU.pow),
                 reads=list(writes) + [r_negh], writes=writes)

        ident = RM.take(BF16, [128, 128]); r_ident = P.res()
        maskc = RM.take(BF16, [128, 128]); r_maskc = P.res()
        ones_bf = RM.take(BF16, [128, 128]); r_ones = P.res()
        negc = RM.take(BF16, [128, 4, 128]); r_negc = P.res()
        negp = RM.take(BF16, [128, 4, 128]); r_negp = P.res()
        negp0 = RM.take(BF16, [128, 4, 128]); r_negp0 = P.res()
        lg_pp = RM.take(F32, [128, 16]); r_lg = P.res()
        lb_pp = RM.take(F32, [128, 16]); r_lb = P.res()
        sinkexp = RM.take(F32, [128, 16]); r_sink = P.res()
        stat = RM.take(F32, [128, 128])
        negh = RM.take(F32, [128, 1]); r_negh = P.res()
        vaug = [RM.take(BF16, [128, 2, 65]) for _ in range(3)]
        r_vaug = [P.res() for _ in range(3)]
        wkv = RE.take(BF16, [128, 8, 256]); r_wkv = P.res()

        dma("gpsimd", ident, ident_d, "c0", writes=[r_ident])
        dma("gpsimd", maskc, maskc_d, "c1", writes=[r_maskc])
        dma("sync", sinkexp, b_sinks.partition_broadcast(128), "c6", writes=[r_sink])
        act(sinkexp, sinkexp, AF.Exp, [r_sink], [r_sink])
        ts("vector", sinkexp, sinkexp, 2.0, None, ALU.mult, None, [r_sink], [r_sink])
        P.op("vector", lambda e: e.memset(ones_bf, 1.0), writes=[r_ones])
        P.op("vector", lambda e: e.memset(negh, -0.5), writes=[r_negh])
        for i in range(3):
            P.op("vector", (lambda v: (lambda e: e.memset(v, 1.0)))(vaug[i]), writes=[r_vaug[i]])

        ga_bc = RF.take(F32, [128, D]); r_ga = P.res()
        Cc = RF.take(F32, [128, 16, 128]); r_C = P.res()
        wsT = RF.take(BF16, [128, 16, 128]); r_wsT = P.res()
        dma("sync", ga_bc, a_norm_g.partition_broadcast(128), "c7", writes=[r_ga])
        wug0 = RF.take(BF16, [128, 8, 2, 128])
        RE.reset()
        wug1 = RE.take(BF16, [128, 8, 2, 128])

        RW.reset()
        xs = [RW.take(F32, [128, D]) for _ in range(2)]; r_xs = [P.res() for _ in range(2)]
        sq = RW.take(BF16, [128, D]); r_sq = P.res()
        hn = RW.take(BF16, [128, D]); r_hn = P.res()
        vn = [RW.take(BF16, [128, AW]) for _ in range(2)]; r_vn = [P.res() for _ in range(2)]
        ident_f = RW.take(F32, [128, 128]); r_identf = P.res()
        lgb_st = RW.take(F32, [16, 2, 128]); r_lgbst = [P.res(), P.res()]
        dma("sync", lgb_st[:, 0, :], a_ln_g.rearrange("(g d) -> g d", g=16), "c4", writes=[r_lgbst[0]])
        dma("sync", lgb_st[:, 1, :], a_ln_b.rearrange("(g d) -> g d", g=16), "c5", writes=[r_lgbst[1]])
        RD.reset()
        vraw = [RD.take(F32, [128, AW]) for _ in range(2)]
        r_vraw = [[P.res() for _ in range(4)] for _ in range(2)]
        ws_st = vraw[0].rearrange("p (g t) -> p g t", g=16); r_wsst_l = r_vraw[0]
        bs_bc = vraw[1].rearrange("p (g t) -> p g t", g=16); r_bs_l = r_vraw[1]
        dma("sync", ws_st, a_ws.rearrange("g t s -> t g s"), "c8", writes=r_wsst_l)
        dma("sync", bs_bc, a_bs.partition_broadcast(128).rearrange("p (g t) -> p g t", g=16), "c9", writes=r_bs_l)
        dma("sync", ident_f, ident_d, "c10", writes=[r_identf])

        Wv = RC.take(BF16, [128, 8, 2048])
        r_Wv = [P.res() for _ in range(4)]
        w_in_r = a_w_in.rearrange("(kc p) n -> p kc n", p=128)
        for s in range(4):
            dma("gpsimd", Wv[:, :, s * 512:(s + 1) * 512], w_in_r[:, :, AW + s * 512:AW + (s + 1) * 512],
                f"wv{s}", writes=[r_Wv[s]])

        dma("gpsimd", negc, negc_d, "c2", writes=[r_negc])
        dma("gpsimd", negp, negp_d, "c3", writes=[r_negp])
        dma("gpsimd", negp0, negp0_d, "c11", writes=[r_negp0])

        def setup_compute():
            for k_, (dst_, rdst_) in enumerate(((lg_pp, r_lg), (lb_pp, r_lb))):
                tr(bank(0)[:, k_ * 16:(k_ + 1) * 16], lgb_st[:, k_, :], ident_f[0:16, 0:16],
                   [r_lgbst[k_], r_identf], [bank_res[0]], True)
                cp("vector", dst_, bank(0)[:, k_ * 16:(k_ + 1) * 16], [bank_res[0]], [rdst_])
            for gq in range(4):
                b = gq % 2
                for gi in range(4):
                    g = gq * 4 + gi
                    tr(bank(b)[:, gi * 128:(gi + 1) * 128], ws_st[:, g, :], ident_f,
                       r_wsst_l + [r_identf], [bank_res[b]], gi == 3)
                tt("vector", wsT[:, gq * 4:(gq + 1) * 4, :],
                   bank(b).rearrange("p (a b) -> p a b", a=4),
                   maskc.unsqueeze(1).to_broadcast([128, 4, 128]), ALU.mult,
                   [bank_res[b], r_maskc], [r_wsT])
            for gq in range(4):
                b = (5, 6, 7, 4)[gq]
                mm(bank(b), ones_bf, wsT[:, gq * 4:(gq + 1) * 4, :].rearrange("p a b -> p (a b)"), True, True,
                   [r_ones, r_wsT], [bank_res[b]], True)
                for gi in range(4):
                    g = gq * 4 + gi
                    stt(Cc[:, g, :], bank(b)[:, gi * 128:(gi + 1) * 128], lb_pp[:, g:g + 1], bs_bc[:, g, :],
                        ALU.mult, ALU.add, [bank_res[b], r_lb] + r_bs_l, [r_C])

        hnT = RB.take(BF16, [128, 8, NCH * 128])
        r_hnT = [P.res() for _ in range(NCH)]
        r_slot = [P.res() for _ in range(NCH)]

        def S_view(j):
            return SLOT.t[:, j * 2048:(j + 1) * 2048].rearrange("p (g t) -> p g t", g=16)

        def h1_view(j):
            return SLOT.t[:, j * 2048:(j + 1) * 2048].bitcast(F32)

        ssA = [stat[:, 0:1], stat[:, 1:2]]
        rstdA = [stat[:, 2:3], stat[:, 3:4]]
        r_ssA = [P.res(), P.res()]
        bnst = [stat[:, 8:32], stat[:, 32:56]]
        mv = [stat[:, 56:58], stat[:, 58:60]]
        rstdv = [stat[:, 60:61], stat[:, 61:62]]
        nmr = [stat[:, 62:63], stat[:, 63:64]]
        r_bn = [P.res(), P.res()]

        def av_front(j):
            p = j % 2
            xb = xs[p]; rxb = r_xs[p]
            dma("sync", xb, x[j], f"x{p}", writes=[rxb])
            act(sq, xb, AF.Square, [rxb], [r_sq, r_ssA[p]], accum=ssA[p])
            ts("vector", rstdA[p], ssA[p], 1.0 / D, EPS, ALU.mult, ALU.add, [r_ssA[p]], [r_ssA[p]])
            P.op("gpsimd", lambda e: e.tensor_tensor(rstdA[p], rstdA[p], negh, ALU.pow),
                 reads=[r_ssA[p], r_negh], writes=[r_ssA[p]])

        def av_front_a2(j):
            p = j % 2
            xb = xs[p]; rxb = r_xs[p]
            stt(hn, xb, rstdA[p], ga_bc, ALU.mult, ALU.mult, [rxb, r_ssA[p], r_ga], [r_hn])

        def av_front_b(j):
            tb = bank_bf(4)
            for kc in range(8):
                tr(tb[:, kc * 128:(kc + 1) * 128], hn[:, kc * 128:(kc + 1) * 128], ident,
                   [r_hn, r_ident], [bank_res[4]], kc == 7)
            cp("scalar", hnT[:, :, j * 128:(j + 1) * 128], tb.rearrange("p (a b) -> p a b", a=8),
               [bank_res[4]], [r_hnT[j]])

        def av_vmm(j, slices=range(4)):
            p = j % 2
            for s in slices:
                for kc in range(8):
                    mm(bank(s), hnT[:, kc, j * 128:(j + 1) * 128], Wv[:, kc, s * 512:(s + 1) * 512],
                       kc == 0, kc == 7, [r_hnT[j], r_Wv[s]], [bank_res[s]], kc == 7)
                if AV_VMM_ACT:
                    cp("scalar", vraw[p][:, s * 512:(s + 1) * 512], bank(s), [bank_res[s]], [r_vraw[p][s]])
                if AV_VMM_BN:
                    P.op("vector", (lambda o, i: (lambda e: e.bn_stats(o, i)))(bnst[p][:, s * 6:(s + 1) * 6],
                                                                               vraw[p][:, s * 512:(s + 1) * 512]),
                         reads=[r_vraw[p][s]], writes=[r_bn[p]])

        def av_mid(j):
            p = j % 2
            P.op("vector", lambda e: e.bn_aggr(mv[p], bnst[p]), reads=[r_bn[p]], writes=[r_bn[p]])
            ts("vector", rstdv[p], mv[p][:, 1:2], EPS, None, ALU.add, None, [r_bn[p]], [r_bn[p]])
            P.op("gpsimd", lambda e: e.tensor_tensor(rstdv[p], rstdv[p], negh, ALU.pow),
                 reads=[r_bn[p], r_negh], writes=[r_bn[p]])
            stt(nmr[p], mv[p][:, 0:1], -1.0, rstdv[p], ALU.mult, ALU.mult, [r_bn[p]], [r_bn[p]])
            if AV_NORM == "pool":
                ts("gpsimd", vn[p], vraw[p], rstdv[p], nmr[p], ALU.mult, ALU.add, r_vraw[p] + [r_bn[p]], [r_vn[p]])
            elif AV_NORM == "dve":
                ts("vector", vn[p], vraw[p], rstdv[p], nmr[p], ALU.mult, ALU.add, r_vraw[p] + [r_bn[p]], [r_vn[p]])
            else:
                act(vn[p], vraw[p], AF.Identity, r_vraw[p] + [r_bn[p]], [r_vn[p]], scale=rstdv[p], bias=nmr[p])

        sv_rot = [0]

        def av_back(j, gqs=range(4)):
            p = j % 2
            Sj = S_view(j)
            for gq in gqs:
                b = (5, 6, 7, 3)[gq] if AV_SVBANK == "v" else 5 + sv_rot[0] % 3
                sv_rot[0] += 1
                for gi in range(4):
                    g = gq * 4 + gi
                    mm(bank(b)[:, gi * 128:(gi + 1) * 128], vn[p][:, g * 128:(g + 1) * 128], wsT[:, g, :],
                       True, True, [r_vn[p], r_wsT], [bank_res[b]], gi == 3)
                for gi in range(4):
                    g = gq * 4 + gi
                    stt(Sj[:, g, :], bank(b)[:, gi * 128:(gi + 1) * 128], lg_pp[:, g:g + 1], Cc[:, g, :],
                        ALU.mult, ALU.add, [bank_res[b], r_lg, r_C], [r_slot[j]])

        r_wug = [P.res() for _ in range(4)]
        r_wug2 = [P.res() for _ in range(4)]
        wug_pre = [wug0, wug1]

        def load_wug_pre(g):
            dma("gpsimd", wug_pre[g][:, :, 0, :], w_in_r[:, :, g * 128:(g + 1) * 128], f"wugu{g}",
                writes=[r_wug[g]])
            dma("gpsimd", wug_pre[g][:, :, 1, :], w_in_r[:, :, 2 * AW + g * 128:2 * AW + (g + 1) * 128],
                f"wugg{g}", writes=[r_wug2[g]])

        if AV_ORDER == 1:
            av_front(0)
            av_front_a2(0)
            av_front_b(0)
            av_front(1)
            setup_compute()
            for j in range(NCH):
                if j + 2 < NCH:
                    av_front(j + 2)
                if j + 1 < NCH:
                    av_front_a2(j + 1)
                av_vmm(j)
                if j + 1 < NCH:
                    av_front_b(j + 1)
                if j >= 1:
                    av_back(j - 1)
                av_mid(j)
                if j == NCH - 4:
                    load_wug_pre(0)
                    load_wug_pre(1)
            av_back(NCH - 1)
        else:
            for j in range(NCH):
                av_front(j)
                av_front_a2(j)
                av_front_b(j)
                if AV_PARTS >= 2:
                    av_vmm(j)
                if AV_PARTS >= 3:
                    av_mid(j)
                if AV_PARTS >= 4:
                    av_back(j)
        if stage == "av":
            for j in range(1, NCH):
                dma("sync", out[j - 1], h1_view(j), "o0", reads=[r_slot[j]])
            P.fence()
            P.emit(nc)
            return nc
        AUG_PB = [6, 4, 0, 2]
        for (j0_, nj_), pb_ in zip([(0, 4), (4, 4)], AUG_PB[:2]):
            n_ = nj_ * 128
            rh_ = [r_hnT[j] for j in range(j0_, j0_ + nj_)]
            for kc in range(8):
                mm(bank(pb_)[:, 0:n_], wug0[:, kc, 0, :], hnT[:, kc, j0_ * 128:j0_ * 128 + n_],
                   kc == 0, kc == 7, rh_ + [r_wug[0]], [bank_res[pb_]], kc == 7)
            for kc in range(8):
                mm(bank(pb_ + 1)[:, 0:n_], wug0[:, kc, 1, :], hnT[:, kc, j0_ * 128:j0_ * 128 + n_],
                   kc == 0, kc == 7, rh_ + [r_wug2[0]], [bank_res[pb_ + 1]], kc == 7)
        P.fence()

        RC.reset()
        Wout = RC.take(BF16, [128, 16, D]); r_Wout = [P.res() for _ in range(2)]
        w_out_r = a_w_out.rearrange("(g p) n -> p g n", p=128)
        RW.reset()
        sgs = [RW.take(F32, [128, 512]) for _ in range(3)]; r_sgs = [P.res() for _ in range(3)]
        tus = [RW.take(F32, [128, 512]) for _ in range(3)]; r_tus = [P.res() for _ in range(3)]
        xs_o = [RW.take(F32, [128, D]) for _ in range(2)]; r_xso = [P.res() for _ in range(2)]
        RD.reset()
        wug = [wug0, wug1] + [RD.take(BF16, [128, 8, 2, 128]) for _ in range(2)]
        batches = [(0, 4), (4, 4), (8, 3), (11, 3), (14, 3)]
        S4 = SLOT.t[:, :].rearrange("p (j g t) -> p j g t", j=NCH, g=16)
        w_in_4 = a_w_in.rearrange("(kc p) (th n) -> p kc th n", p=128, th=3)

        def load_wug(g):
            dma("gpsimd", wug[g % 4][:, :, 0, :], w_in_r[:, :, g * 128:(g + 1) * 128], f"wugu{g % 4}",
                writes=[r_wug[g % 4]])
            dma("gpsimd", wug[g % 4][:, :, 1, :], w_in_r[:, :, 2 * AW + g * 128:2 * AW + (g + 1) * 128],
                f"wugg{g % 4}", writes=[r_wug2[g % 4]])

        it = 0
        for g in range(16):
            if g + 2 < 16:
                load_wug(g + 2)
            if g == 1:
                for hf in range(2):
                    dma("gpsimd", Wout[:, hf * 8:(hf + 1) * 8, :], w_out_r[:, hf * 8:(hf + 1) * 8, :],
                        f"wout{hf}", writes=[r_Wout[hf]])
            if g == 14:
                for jj in range(2):
                    dma("sync", xs_o[jj], x[jj], f"x{jj}", writes=[r_xso[jj]])
            wb = wug[g % 4]; rwb = r_wug[g % 4]; rwb2 = r_wug2[g % 4]
            for (j0, nj) in batches:
                n = nj * 128
                pb = AUG_PB[it % 4]
                pre_issued = it < 2
                sg = sgs[it % 3]; rsg = r_sgs[it % 3]
                tu = tus[it % 3]; rtu = r_tus[it % 3]
                it += 1
                rh = [r_hnT[j] for j in range(j0, j0 + nj)]
                rs = [r_slot[j] for j in range(j0, j0 + nj)]
                for kc in range(8):
                    if pre_issued:
                        break
                    mm(bank(pb)[:, 0:n], wb[:, kc, 0, :], hnT[:, kc, j0 * 128:j0 * 128 + n],
                       kc == 0, kc == 7, rh + [rwb], [bank_res[pb]], kc == 7)
                for kc in range(8):
                    if pre_issued:
                        break
                    mm(bank(pb + 1)[:, 0:n], wb[:, kc, 1, :], hnT[:, kc, j0 * 128:j0 * 128 + n],
                       kc == 0, kc == 7, rh + [rwb2], [bank_res[pb + 1]], kc == 7)
                act(sg[:, 0:n], bank(pb + 1)[:, 0:n], AF.Silu, [bank_res[pb + 1]], [rsg])
                tt("vector", tu[:, 0:n], bank(pb)[:, 0:n], sg[:, 0:n], ALU.mult, [bank_res[pb], rsg], [rtu])
                Sv = S4[:, j0:j0 + nj, g, :]
                tt("gpsimd", Sv, Sv, tu[:, 0:n].rearrange("p (j t) -> p j t", j=nj), ALU.mult,
                   rs + [rtu], rs)
        if stage == "aug":
            for j in range(1, NCH):
                dma("sync", out[j - 1], h1_view(j), "o0", reads=[r_slot[j]])
            P.fence()
            P.emit(nc)
            return nc
        P.fence()

        RF.reset()
        kvg_pp = RF.take(F32, [128, 8]); r_kvg = P.res()
        bg_pp = RF.take(F32, [128, 8]); r_bg = P.res()
        fg_bc = RF.take(F32, [128, D]); r_fg = P.res()
        bq_bc = RF.take(F32, [128, D]); r_bq = P.res()
        bkv_bc = RF.take(F32, [128, 256]); r_bkv = P.res()
        cos_t = RF.take(F32, [128, NCH, 32]); r_cos = P.res()
        sin_t = RF.take(F32, [128, NCH, 32]); r_sin = P.res()
        dma("sync", kvg_pp, kv_norm_g.rearrange("(kc p) -> p kc", p=128), "c4", writes=[r_kvg], slow=True)
        dma("sync", bg_pp, b_norm_g.rearrange("(kc p) -> p kc", p=128), "c5", writes=[r_bg], slow=True)
        dma("sync", fg_bc, final_norm_g.partition_broadcast(128), "c6", writes=[r_fg])
        dma("sync", bq_bc, b_bq.partition_broadcast(128), "c7", writes=[r_bq])
        dma("sync", bkv_bc, b_kv.partition_broadcast(128), "c8", writes=[r_bkv])
        dma("sync", cos_t, cos_d, "c9", writes=[r_cos])
        dma("sync", sin_t, sin_d, "c10", writes=[r_sin])

        RE.reset()
        wkv = RE.take(BF16, [128, 8, 256])
        dma("gpsimd", wkv, w_kv.rearrange("(kc p) n -> p kc n", p=128), "wkv", writes=[r_wkv])
        RB.reset()
        bwin = RB.take(BF16, [128, 8, 2048]); r_bwin = [P.res() for _ in range(4)]
        RD.reset()
        bwout = RD.take(BF16, [128, 8, D]); r_bwout = P.res()
        b_w_in_r = b_w_in.rearrange("(kc p) n -> p kc n", p=128)
        for s in range(4):
            dma("gpsimd", bwin[:, :, s * 512:(s + 1) * 512], b_w_in_r[:, :, s * 512:(s + 1) * 512],
                f"wv{s}", writes=[r_bwin[s]])
        dma("gpsimd", bwout, b_w_out.rearrange("(kc p) n -> p kc n", p=128), "wout0", writes=[r_bwout])
        def fold_gain(step):
            kc = step % 8
            if step < 8:
                ts("vector", wkv[:, kc, :], wkv[:, kc, :], kvg_pp[:, kc:kc + 1], None, ALU.mult, None,
                   [r_wkv, r_kvg], [r_wkv])
            else:
                ts("vector", bwin[:, kc, :], bwin[:, kc, :], bg_pp[:, kc:kc + 1], None, ALU.mult, None,
                   list(r_bwin) + [r_bg], list(r_bwin))
        for j in range(NCH):
            xb = xs_o[j % 2]; rxb = r_xso[j % 2]
            if j >= 2:
                dma("sync", xb, x[j], f"x{j % 2}", writes=[rxb])
            Sj = S_view(j)
            pb = 4 * (j % 2)
            for hf in range(2):
                b = pb + hf
                for g in range(16):
                    mm(bank(b), Sj[:, g, :], Wout[:, g, hf * 512:(hf + 1) * 512], g == 0, g == 15,
                       [r_slot[j], r_Wout[g // 8]], [bank_res[b]], g == 15)
            h1 = h1_view(j)
            for hf in range(2):
                tt("vector", h1[:, hf * 512:(hf + 1) * 512], bank(pb + hf), xb[:, hf * 512:(hf + 1) * 512],
                   ALU.add, [bank_res[pb + hf], rxb], [r_slot[j]])
            if j >= 1:
                fold_gain(j - 1)

        if stage == "h1":
            for j in range(1, NCH):
                dma("sync", out[j - 1], h1_view(j), "o0", reads=[r_slot[j]])
            P.fence()
            P.emit(nc)
            return nc
        P.fence()

        RC.reset(); RW.reset()
        hnkv = RC.take(BF16, [128, D]); r_hnkv = P.res()
        hnb = RC.take(BF16, [128, D]); r_hnb = P.res()
        hnkvT = RC.take(BF16, [128, 8, 128]); r_hnkvT = P.res()
        hnbT = RC.take(BF16, [128, 8, 128]); r_hnbT = P.res()
        qf = RC.take(F32, [128, D]); r_qf = P.res()
        qrot = RC.take(BF16, [128, D]); r_qrot = P.res()
        qT = RC.take(BF16, [128, 8, 128]); r_qT = P.res()
        sgB = [RC.take(F32, [128, D]) for _ in range(2)]; r_sgB = [P.res(), P.res()]
        PT = RC.take(BF16, [128, 2, 16, 128]); r_PT = [[P.res(), P.res()], [P.res(), P.res()]]
        on = qf; r_on = r_qf
        kvf = RW.take(F32, [128, 256]); r_kvf = P.res()
        kz = RW.take(BF16, [128, 2, 2, 128]); r_kz = P.res()
        kTz = [RW.take(BF16, [128, 2, 2, 128]) for _ in range(3)]; r_kTz = [P.res() for _ in range(3)]
        rt = [RW.take(F32, [128, 16, 32]) for _ in range(2)]; r_rt = [P.res() for _ in range(2)]
        rtk = [RW.take(F32, [128, 2, 32]) for _ in range(2)]; r_rtk = [P.res() for _ in range(2)]
        yb = RW.take(BF16, [128, D]); r_yb = P.res()
        yT = RW.take(BF16, [128, 8, 128]); r_yT = P.res()
        sqB = RW.take(BF16, [128, D]); r_sqB = P.res()
        rt2 = [RW.take(F32, [128, 16, 32]) for _ in range(2)]; r_rt2 = [P.res() for _ in range(2)]
        r_qrot_hi = P.res()
        ot = RW.take(F32, [128, D]); r_ot = P.res()
        ssB = [stat[:, 64:65], stat[:, 65:66]]
        rstdB = [stat[:, 66:67], stat[:, 67:68]]
        r_ssB = [P.res(), P.res()]
        ss2 = stat[:, 68:69]
        rstd2 = stat[:, 69:70]
        r_ss2 = P.res()
        den = stat[:, 72:88]; r_den = [P.res(), P.res()]
        r_on2 = [P.res(), P.res()]
        r_yb2 = [[P.res(), P.res()], [P.res(), P.res()]]
        ybs = [yb, hnb]
        P.op("vector", lambda e: e.memset(kz.rearrange("p a b c -> p (a b c)"), 0.0), writes=[r_kz])

        def rotary(eng, src, dst_lo, dst_hi, nh, j, rsrc, rdst, rt, r_rt):
            cb = cos_t[:, j, :].unsqueeze(1).to_broadcast([128, nh, 32])
            sb_ = sin_t[:, j, :].unsqueeze(1).to_broadcast([128, nh, 32])
            x1 = src[:, :, 0:32]
            x2 = src[:, :, 32:64]
            a, b = (rt[i][:, 0:nh, :] for i in range(2))
            tt(eng, a, x1, cb, ALU.mult, rsrc + [r_cos], [r_rt[0]])
            tt(eng, b, x2, sb_, ALU.mult, rsrc + [r_sin], [r_rt[1]])
            tt(eng, dst_lo, a, b, ALU.subtract, [r_rt[0], r_rt[1]], rdst)
            tt(eng, a, x2, cb, ALU.mult, rsrc + [r_cos], [r_rt[0]])
            tt(eng, b, x1, sb_, ALU.mult, rsrc + [r_sin], [r_rt[1]])
            tt(eng, dst_hi, a, b, ALU.add, [r_rt[0], r_rt[1]], rdst)

        def rotary_q(j, src, dst_lo, dst_hi, rsrc):
            nh = 16
            cb = cos_t[:, j, :].unsqueeze(1).to_broadcast([128, nh, 32])
            sb_ = sin_t[:, j, :].unsqueeze(1).to_broadcast([128, nh, 32])
            x1 = src[:, :, 0:32]
            x2 = src[:, :, 32:64]
            a, b = rt[0], rt[1]
            c, d_ = rt2[0], rt2[1]
            tt("gpsimd", a, x1, cb, ALU.mult, rsrc + [r_cos], [r_rt[0]])
            tt("vector", c, x2, cb, ALU.mult, rsrc + [r_cos], [r_rt2[0]])
            tt("gpsimd", b, x2, sb_, ALU.mult, rsrc + [r_sin], [r_rt[1]])
            tt("vector", d_, x1, sb_, ALU.mult, rsrc + [r_sin], [r_rt2[1]])
            tt("gpsimd", dst_lo, a, b, ALU.subtract, [r_rt[0], r_rt[1]], [r_qrot])
            tt("vector", dst_hi, c, d_, ALU.add, [r_rt2[0], r_rt2[1]], [r_qrot_hi])

        def b_a(j):
            p = j % 2
            h1 = h1_view(j)
            act(sqB, h1, AF.Square, [r_slot[j]], [r_sqB, r_ssB[p]], accum=ssB[p])
            ts("vector", rstdB[p], ssB[p], 1.0 / D, EPS, ALU.mult, ALU.add, [r_ssB[p]], [r_ssB[p]])
            P.op("gpsimd", lambda e: e.tensor_tensor(rstdB[p], rstdB[p], negh, ALU.pow),
                 reads=[r_ssB[p], r_negh], writes=[r_ssB[p]])

        def b_a2(j):
            p = j % 2
            h1 = h1_view(j)
            ts("vector", hnkv, h1, rstdB[p], None, ALU.mult, None, [r_slot[j], r_ssB[p]], [r_hnkv])

        def b_b_part(j, part):
            which, half = divmod(part, 4)
            if which == 1:
                return
            src, rsrc, bk, dst, rdst = ((hnkv, r_hnkv, 4, hnkvT, r_hnkvT), (hnb, r_hnb, TRB_HNB, hnbT, r_hnbT))[which]
            tb = bank_bf(bk)
            for kc in range(half * 2, half * 2 + 2):
                tr(tb[:, kc * 128:(kc + 1) * 128], src[:, kc * 128:(kc + 1) * 128], ident,
                   [rsrc, r_ident], [bank_res[bk]], kc == 7)
            if half == 3:
                cp("vector", dst, tb.rearrange("p (a b) -> p a b", a=8), [bank_res[bk]], [rdst])

        def b_b(j):
            for part in range(8):
                b_b_part(j, part)

        def b_c_kv(j):
            for kc in range(8):
                mm(bank(KVB)[:, 0:256], hnkvT[:, kc, :], wkv[:, kc, :], kc == 0, kc == 7,
                   [r_hnkvT, r_wkv], [bank_res[KVB]], kc == 7)
            tt("vector", kvf, bank(KVB)[:, 0:256], bkv_bc, ALU.add, [bank_res[KVB], r_bkv], [r_kvf])
            ksrc = kvf[:, 0:128].rearrange("p (h d) -> p h d", h=2)
            rotary("vector", ksrc, kz[:, :, 0, 0:32], kz[:, :, 0, 32:64], 2, j, [r_kvf], [r_kz], rtk, r_rtk)
            cp("vector", kz[:, :, 1, 64:128], kz[:, :, 0, 0:64], [r_kz], [r_kz])
            va = vaug[j % 3]; rva = r_vaug[j % 3]
            cp("vector", va[:, :, 0:64], kvf[:, 128:256].rearrange("p (h d) -> p h d", h=2), [r_kvf], [rva])

        def b_c_q(j):
            if j >= 1:
                for s in range(4):
                    for kc in range(8):
                        mm(bank(s), hnkvT[:, kc, :], bwin[:, kc, s * 512:(s + 1) * 512], kc == 0, kc == 7,
                           [r_hnkvT, r_bwin[s]], [bank_res[s]], kc == 7)
                sg = sgB[j % 2]; rsg = r_sgB[j % 2]
                for s in range(2):
                    tt("vector", qf[:, s * 512:(s + 1) * 512], bank(s), bq_bc[:, s * 512:(s + 1) * 512], ALU.add,
                       [bank_res[s], r_bq], [r_qf, r_on2[s]])
                for s in range(2):
                    act(sg[:, s * 512:(s + 1) * 512], bank(2 + s), AF.Tanh, [bank_res[2 + s]], [rsg], scale=0.5)
                for s in range(2):
                    stt(sg[:, s * 512:(s + 1) * 512], sg[:, s * 512:(s + 1) * 512], 1.0, bank(2 + s),
                        ALU.add, ALU.mult, [rsg, bank_res[2 + s]], [rsg])
                q3 = qf.rearrange("p (h d) -> p h d", h=16)
                qr3 = qrot.rearrange("p (h d) -> p h d", h=16)
                rotary_q(j, q3, qr3[:, :, 0:32], qr3[:, :, 32:64], [r_qf])

        def b_d(j):
            tb = bank_bf(KZB)
            for hk in range(2):
                for par in range(2):
                    c0 = (hk * 2 + par) * 128
                    tr(tb[:, c0:c0 + 128], kz[:, hk, par, :], ident, [r_kz, r_ident], [bank_res[KZB]],
                       hk == 1 and par == 1)
            cp("scalar", kTz[j % 3], tb[:, 0:512].rearrange("p (a b c) -> p a b c", a=2, b=2),
               [bank_res[KZB]], [r_kTz[j % 3]])
            if j >= 1:
                tb = bank_bf(TRB)
                for kc in range(8):
                    tr(tb[:, kc * 128:(kc + 1) * 128], qrot[:, kc * 128:(kc + 1) * 128], ident,
                       [r_qrot, r_qrot_hi, r_ident], [bank_res[TRB]], kc == 7)
                cp("vector", qT, tb.rearrange("p (a b) -> p a b", a=8), [bank_res[TRB]], [r_qT])

        st_rot = [0]

        def b_f(j, inter=None):
            npair = 0
            for hk in range(2):
                for kb in range(2):
                    jk = j - 1 + kb
                    kTk = kTz[jk % 3]; rkTk = r_kTz[jk % 3]
                    ng = negc if kb == 1 else (negp0 if j == 1 else negp)
                    rng_ = r_negc if kb == 1 else (r_negp0 if j == 1 else r_negp)
                    for par in range(2):
                        b = (8 - ST_RING) + st_rot[0] % ST_RING
                        st_rot[0] += 1
                        ob = bank(b).rearrange("p (a b) -> p a b", a=4)
                        mm(ob, ident, ng, True, False, [r_ident, rng_], [bank_res[b]], False)
                        mm(ob, kTk[:, hk, par, :], qT[:, hk * 4:(hk + 1) * 4, :], False, True,
                           [rkTk, r_qT], [bank_res[b]], True)
                        pv = PT[:, kb, hk * 8 + par:hk * 8 + 8:2, :]
                        act(pv, ob, AF.Exp, [bank_res[b]], [r_PT[kb][hk]], scale=0.125)
                        if inter is not None:
                            inter(npair)
                        npair += 1

        def b_g(j):
            o4 = PS[:, 0:2048].rearrange("p (h c) -> p h c", h=16)
            for h in range(16):
                hk = h // 8
                b = h // 4
                for kb in range(2):
                    jk = j - 1 + kb
                    mm(o4[:, h, 0:65], PT[:, kb, h, :], vaug[jk % 3][:, hk, :], kb == 0, kb == 1,
                       [r_PT[kb][hk], r_vaug[jk % 3]], [bank_res[b]], (kb == 1 and h % 4 == 3))
                if h % 8 == 7:
                    hh = h // 8
                    ro = [bank_res[2 * hh], bank_res[2 * hh + 1]]
                    hs = slice(hh * 8, hh * 8 + 8)
                    cs = slice(hh * 512, hh * 512 + 512)
                    dn = den[:, hs]
                    stt(dn, o4[:, hs, 64], 2.0, sinkexp[:, hs], ALU.mult, ALU.add, ro + [r_sink], [r_den[hh]])
                    P.op("vector", (lambda d_: (lambda e: e.reciprocal(d_, d_)))(dn), reads=[r_den[hh]], writes=[r_den[hh]])
                    tt("vector", on[:, cs].rearrange("p (h d) -> p h d", h=8), o4[:, hs, 0:64],
                       dn.unsqueeze(2).to_broadcast([128, 8, 64]), ALU.mult, ro + [r_den[hh]], [r_on2[hh], r_qf])
                    tt("gpsimd", ybs[j % 2][:, cs], on[:, cs], sgB[j % 2][:, cs], ALU.mult,
                       [r_on2[hh], r_sgB[j % 2]], [r_yb2[j % 2][hh]])

        def b_h(j):
            tb = bank_bf(YTB)
            for kc in range(8):
                tr(tb[:, kc * 128:(kc + 1) * 128], ybs[j % 2][:, kc * 128:(kc + 1) * 128], ident,
                   [r_yb2[j % 2][kc // 4], r_ident], [bank_res[YTB]], kc == 7)
            cp("scalar", yT, tb.rearrange("p (a b) -> p a b", a=8), [bank_res[YTB]], [r_yT])

        def b_i(j):
            h1 = h1_view(j)
            for hf in range(2):
                b = WO_BANKS[hf]
                for kc in range(8):
                    mm(bank(b), yT[:, kc, :], bwout[:, kc, hf * 512:(hf + 1) * 512], kc == 0, kc == 7,
                       [r_yT, r_bwout], [bank_res[b]], kc == 7)
            for hf in range(2):
                b = WO_BANKS[hf]
                tt("vector", h1[:, hf * 512:(hf + 1) * 512], bank(b), h1[:, hf * 512:(hf + 1) * 512], ALU.add,
                   [bank_res[b], r_slot[j]], [r_slot[j]])
            h2 = h1
            r_h2 = r_slot[j]
            act(sqB, h2, AF.Square, [r_h2], [r_sqB, r_ss2], accum=ss2)
            ts("vector", rstd2, ss2, 1.0 / D, EPS, ALU.mult, ALU.add, [r_ss2], [r_ss2])
            P.op("gpsimd", lambda e: e.tensor_tensor(rstd2, rstd2, negh, ALU.pow),
                 reads=[r_ss2, r_negh], writes=[r_ss2])
            stt(ot, h2, rstd2, fg_bc, ALU.mult, ALU.mult, [r_h2, r_ss2, r_fg], [r_ot])
            dma("sync", out[j - 1], ot, "o0", reads=[r_ot])

        b_a(0)
        b_a2(0)
        b_b(0)
        for i in range(NCH + 3):
            if i + 1 < NCH:
                b_a(i + 1)
            if i < NCH:
                b_c_kv(i)
            if i + 1 < NCH:
                b_a2(i + 1)
            if 1 <= i - 3 < NCH:
                b_i(i - 3)
            if i < NCH:
                b_c_q(i)
            if 1 <= i - 1 < NCH:
                if i + 1 < NCH:
                    b_f(i - 1, inter=(lambda jj: (lambda k: b_b_part(jj, k)))(i + 1))
                else:
                    b_f(i - 1)
                b_g(i - 1)
            elif i + 1 < NCH:
                b_b(i + 1)
            if 1 <= i - 2 < NCH:
                b_h(i - 2)
            if i < NCH:
                b_d(i)
        P.fence()
        P.emit(nc)
    return nc


def _host_inputs(inputs):
    x = np.ascontiguousarray(np.asarray(inputs["x"], dtype=np.float32))
    sq = lambda k: np.ascontiguousarray(np.asarray(inputs[k], dtype=np.float32))
    shared = {
        "a_norm_g": sq("a_norm_g")[0], "a_w_in": sq("a_w_in")[0], "a_ln_g": sq("a_ln_g")[0],
        "a_ln_b": sq("a_ln_b")[0], "a_ws": sq("a_ws")[0], "a_bs": sq("a_bs")[0].reshape(-1),
        "a_w_out": sq("a_w_out")[0], "kv_norm_g": sq("kv_norm_g"), "w_kv": sq("w_kv"), "b_kv": sq("b_kv"),
        "b_norm_g": sq("b_norm_g")[0], "b_w_in": sq("b_w_in")[0], "b_bq": sq("b_bq")[0],
        "b_sinks": sq("b_sinks")[0], "b_w_out": sq("b_w_out")[0], "final_norm_g": sq("final_norm_g"),
    }
    shared = {k: np.ascontiguousarray(v) for k, v in shared.items()}
    k_i = np.arange(128)[:, None]
    t_i = np.arange(128)[None, :]
    shared["ident"] = np.eye(128, dtype=np.float32)
    shared["maskc"] = (k_i <= t_i).astype(np.float32)
    NEG = np.float32(-30000.0)
    negc = np.where(k_i <= t_i, np.float32(0), NEG).astype(np.float32)
    negp = np.where(k_i > t_i, np.float32(0), NEG).astype(np.float32)
    rep4 = lambda m: np.ascontiguousarray(np.repeat(m[:, None, :], 4, axis=1))
    shared["negc"] = rep4(negc)
    shared["negp"] = rep4(negp)
    inv_freq = (10000.0 ** (-np.arange(0, 64, 2, dtype=np.float32) / 64)).astype(np.float32)
    in_maps = []
    for c in range(NCORES):
        b, hf = divmod(c, 2)
        xc = np.zeros((NCH, 128, D), np.float32)
        xc[1:] = x[b, hf * 2048:(hf + 1) * 2048].reshape(16, 128, D)
        if hf == 1:
            xc[0] = x[b, 2048 - 128:2048]
        pos = (hf * 2048 - 128 + np.arange(NCH * 128)).astype(np.float32)
        ang = pos[:, None] * inv_freq[None, :]
        cos_t = np.cos(ang).astype(np.float32).reshape(NCH, 128, 32).transpose(1, 0, 2)
        sin_t = np.sin(ang).astype(np.float32).reshape(NCH, 128, 32).transpose(1, 0, 2)
        m = dict(shared)
        m["x"] = xc
        m["cos_t"] = np.ascontiguousarray(cos_t)
        m["sin_t"] = np.ascontiguousarray(sin_t)
        m["negp0"] = shared["negp"] if hf == 1 else np.full((128, 4, 128), NEG, np.float32)
        in_maps.append(m)
    return in_maps


def run(inputs, stage="full"):
    in_maps = _host_inputs(inputs)
    nc = build(stage)
    res = run_bass_kernel_spmd(nc, in_maps, core_ids=list(range(NCORES)))
    outs = [np.asarray(r["out"]).reshape(2048, D) for r in res.results]
    full = np.stack([np.concatenate(outs[2 * b:2 * b + 2], axis=0) for b in range(4)], axis=0)
    return full.astype(np.float32)


def kernel(**inputs):
    return run(inputs, "full")
```

```python
from contextlib import ExitStack
import numpy as np
import concourse.bass as bass
import concourse.mybir as mybir
from concourse.bass_utils import run_bass_kernel_spmd

F32 = mybir.dt.float32
BF16 = mybir.dt.bfloat16
ALU = mybir.AluOpType
AF = mybir.ActivationFunctionType

ENGS = ["sync", "scalar", "vector", "gpsimd", "tensor"]
NCORES = 8
NCH = 17
D = 1024
AW = 2048
EPS = 1e-5
AV_ORDER = 1
AV_NORM = "pool"
AV_PARTS = 4
ROT_ENG = "gpsimd"
AV_SVBANK = "v"
ST_RING = 3
TRB = 5
TRB_HNB = 3
KZB = 6
KVB = 4
YTB = 7
WO_BANKS = (6, 7)
AV_VMM_ACT = 1
AV_VMM_BN = 1


class Res:
    __slots__ = ("name", "w", "r")

    def __init__(self, name):
        self.name = name
        self.w = None
        self.r = []


class Prog:
    def __init__(self):
        self.ops = {e: [] for e in ENGS}
        self.seen = {e: {} for e in ENGS}
        self.dcnt = {}
        self.nres = 0

    def res(self, name=None):
        self.nres += 1
        return Res(name or f"r{self.nres}")

    def _need(self, eng, deps, tok, raw):
        if tok is None:
            return
        key, seq, peng = tok
        if peng == eng and not raw:
            return
        if self.seen[eng].get(key, -1) >= seq:
            return
        self.seen[eng][key] = seq
        deps[key] = max(deps.get(key, -1), seq)

    def _deps(self, eng, reads, writes):
        deps = {}
        for r in reads:
            self._need(eng, deps, r.w, True)
        for w in writes:
            self._need(eng, deps, w.w, False)
            for t in w.r:
                self._need(eng, deps, t, False)
        return deps

    def _commit(self, tok, reads, writes):
        for r in reads:
            r.r.append(tok)
        for w in writes:
            w.w = tok
            w.r = []

    def op(self, eng, fn, reads=(), writes=(), signal=True):
        deps = self._deps(eng, reads, writes)
        seq = len(self.ops[eng])
        tok = ("E_" + eng, seq, eng)
        self._commit(tok, reads, writes)
        self.ops[eng].append(dict(fn=fn, deps=deps, sig_ok=signal, dma=None))

    def dma(self, eng, fn, sem, reads=(), writes=()):
        deps = self._deps(eng, reads, writes)
        key = "D_" + sem
        n = self.dcnt.get(key, 0) + 1
        self.dcnt[key] = n
        tok = (key, n, "dma:" + key)
        self._commit(tok, reads, writes)
        self.ops[eng].append(dict(fn=fn, deps=deps, sig_ok=False, dma=key))

    def fence(self):
        for e in ENGS:
            deps = {}
            for pe in ENGS:
                if pe != e:
                    last = [i for i, o in enumerate(self.ops[pe]) if o["sig_ok"]]
                    if last:
                        self._need(e, deps, ("E_" + pe, last[-1], pe), True)
            for k, n in self.dcnt.items():
                self._need(e, deps, (k, n, "dma:" + k), True)
            if deps:
                self.ops[e].append(dict(fn=None, deps=deps, sig_ok=False, dma=None))

    def emit(self, nc):
        needed = {e: set() for e in ENGS}
        sig_idx = {}
        for e in ENGS:
            idx = [i for i, o in enumerate(self.ops[e]) if o["sig_ok"]]
            sig_idx[e] = idx
        import bisect
        for e in ENGS:
            for o in self.ops[e]:
                nd = {}
                for k, seq in o["deps"].items():
                    if k.startswith("E_"):
                        pe = k[2:]
                        idx = sig_idx[pe]
                        p = bisect.bisect_left(idx, seq)
                        assert p < len(idx), ("no signalable op after", pe, seq)
                        s2 = idx[p]
                        needed[pe].add(s2)
                        nd[k] = s2
                    else:
                        nd[k] = seq
                o["deps"] = nd
        cnt_at = {}
        for e in ENGS:
            c = 0
            m = {}
            for i in range(len(self.ops[e])):
                if i in needed[e]:
                    c += 1
                    m[i] = c
            cnt_at[e] = m
        keys = ["E_" + e for e in ENGS if needed[e]] + sorted(self.dcnt.keys())
        with ExitStack() as es:
            sems = {k: es.enter_context(nc.semaphore(k)) for k in keys}
            block = es.enter_context(nc.Block())

            def mk(ename):
                def body(eng):
                    for i, o in enumerate(self.ops[ename]):
                        for k, v in o["deps"].items():
                            if k.startswith("E_"):
                                eng.wait_ge(sems[k], cnt_at[k[2:]][v])
                            else:
                                eng.wait_ge(sems[k], 16 * v)
                        if o["fn"] is None:
                            continue
                        ins = o["fn"](eng)
                        if o["dma"] is not None:
                            ins.then_inc(sems[o["dma"]], 16)
                        elif i in needed[ename]:
                            ins.then_inc(sems["E_" + ename], 1)
                return body

            for e in ENGS:
                if self.ops[e]:
                    getattr(block, e)(mk(e))
        return len(keys)


class Region:
    def __init__(self, t, nbytes):
        self.t = t
        self.nbytes = nbytes
        self.off = 0

    def reset(self):
        self.off = 0

    def take(self, dtype, shape):
        esz = 4 if dtype == F32 else 2
        n = 1
        for s in shape[1:]:
            n *= s
        nb = (n * esz + 31) // 32 * 32
        assert self.off + nb <= self.nbytes, (self.off, nb, self.nbytes)
        a = self.t[0:shape[0], self.off // 2:(self.off + n * esz) // 2]
        self.off += nb
        if dtype == F32:
            a = a.bitcast(F32)
        if len(shape) == 3:
            a = a.rearrange("p (a b) -> p a b", a=shape[1])
        elif len(shape) == 4:
            a = a.rearrange("p (a b c) -> p a b c", a=shape[1], b=shape[2])
        return a


def build(stage="full"):
    nc = bass.Bass("TRN2", target_bir_lowering=False)
    P = Prog()

    def din(name, shape):
        return nc.dram_tensor(name, list(shape), F32, kind="ExternalInput").ap()

    x = din("x", [NCH, 128, D])
    a_norm_g = din("a_norm_g", [D])
    a_w_in = din("a_w_in", [D, 3 * AW])
    a_ln_g = din("a_ln_g", [AW])
    a_ln_b = din("a_ln_b", [AW])
    a_ws = din("a_ws", [16, 128, 128])
    a_bs = din("a_bs", [AW])
    a_w_out = din("a_w_out", [AW, D])
    kv_norm_g = din("kv_norm_g", [D])
    w_kv = din("w_kv", [D, 256])
    b_kv = din("b_kv", [256])
    b_norm_g = din("b_norm_g", [D])
    b_w_in = din("b_w_in", [D, 2048])
    b_bq = din("b_bq", [D])
    b_sinks = din("b_sinks", [16])
    b_w_out = din("b_w_out", [D, D])
    final_norm_g = din("final_norm_g", [D])
    ident_d = din("ident", [128, 128])
    maskc_d = din("maskc", [128, 128])
    negc_d = din("negc", [128, 4, 128])
    negp_d = din("negp", [128, 4, 128])
    negp0_d = din("negp0", [128, 4, 128])
    cos_d = din("cos_t", [128, NCH, 32])
    sin_d = din("sin_t", [128, NCH, 32])
    out = nc.dram_tensor("out", [16, 128, D], F32, kind="ExternalOutput").ap()

    with ExitStack() as es:
        def sbt(name, nbytes):
            return Region(es.enter_context(nc.sbuf_tensor(name, [128, nbytes // 2], BF16)), nbytes)

        K = 1024
        SLOT = sbt("slot", NCH * 4 * K)
        RB = sbt("rb", 34 * K)
        RC = sbt("rc", 32 * K)
        RD = sbt("rd", 16 * K)
        RE = sbt("re", 4 * K)
        RF = sbt("rf", 22 * K)
        RW = sbt("rw", 24 * K)
        RM = sbt("rm", 5 * K + 512)
        PS = es.enter_context(nc.psum_tensor("ps", [128, 8 * 512], F32))
        bank_res = [P.res(f"bank{i}") for i in range(8)]

        def bank(i, n=1):
            return PS[:, i * 512:(i + n) * 512]

        def bank_bf(i):
            return PS[:, i * 512:(i + 1) * 512].bitcast(BF16)

        def mm(outap, lhsT, rhs, start, stop, reads, writes, signal):
            P.op("tensor", lambda e: e.matmul(outap, lhsT, rhs, start=start, stop=stop),
                 reads=reads, writes=writes, signal=signal)

        def tr(outap, inap, idap, reads, writes, signal):
            P.op("tensor", lambda e: e.transpose(outap, inap, idap),
                 reads=reads, writes=writes, signal=signal)

        def act(outap, inap, func, reads, writes, scale=1.0, bias=0.0, accum=None):
            if accum is None:
                P.op("scalar", lambda e: e.activation(outap, inap, func, bias=bias, scale=scale),
                     reads=reads, writes=writes)
            else:
                P.op("scalar", lambda e: e.activation(outap, inap, func, bias=bias, scale=scale, accum_out=accum),
                     reads=reads, writes=writes)

        def tt(eng, outap, a, b, op, reads, writes):
            P.op(eng, lambda e: e.tensor_tensor(outap, a, b, op), reads=reads, writes=writes)

        def stt(outap, a, sc, b, op0, op1, reads, writes):
            P.op("vector", lambda e: e.scalar_tensor_tensor(outap, a, sc, b, op0, op1),
                 reads=reads, writes=writes)

        def ts(eng, outap, a, s1, s2, op0, op1, reads, writes):
            if op1 is None:
                P.op(eng, lambda e: e.tensor_scalar(outap, a, s1, None, op0), reads=reads, writes=writes)
            else:
                P.op(eng, lambda e: e.tensor_scalar(outap, a, s1, s2, op0, op1), reads=reads, writes=writes)

        def cp(eng, outap, inap, reads, writes):
            if eng == "scalar":
                P.op("scalar", lambda e: e.copy(outap, inap), reads=reads, writes=writes)
            else:
                P.op(eng, lambda e: e.tensor_copy(outap, inap), reads=reads, writes=writes)

        def dma(eng, outap, inap, sem, reads=(), writes=(), slow=False):
            if slow:
                P.dma(eng, lambda e: e.dma_start(out=outap, in_=inap, allow_slow_non_contiguous=True),
                      sem, reads=reads, writes=writes)
            else:
                P.dma(eng, lambda e: e.dma_start(out=outap, in_=inap), sem, reads=reads, writes=writes)

        def rstd_from(outap, ssap, n, reads, writes):
            ts("vector", outap, ssap, 1.0 / n, EPS, ALU.mult, ALU.add, reads, writes)
            P.op("gpsimd", lambda e: e.tensor_tensor(outap, outap, negh, ALU.pow),
                 reads=list(writes) + [r_negh], writes=writes)

        ident = RM.take(BF16, [128, 128]); r_ident = P.res()
        maskc = RM.take(BF16, [128, 128]); r_maskc = P.res()
        ones_bf = RM.take(BF16, [128, 128]); r_ones = P.res()
        negc = RM.take(BF16, [128, 4, 128]); r_negc = P.res()
        negp = RM.take(BF16, [128, 4, 128]); r_negp = P.res()
        negp0 = RM.take(BF16, [128, 4, 128]); r_negp0 = P.res()
        lg_pp = RM.take(F32, [128, 16]); r_lg = P.res()
        lb_pp = RM.take(F32, [128, 16]); r_lb = P.res()
        sinkexp = RM.take(F32, [128, 16]); r_sink = P.res()
        stat = RM.take(F32, [128, 128])
        negh = RM.take(F32, [128, 1]); r_negh = P.res()
        vaug = [RM.take(BF16, [128, 2, 65]) for _ in range(3)]
        r_vaug = [P.res() for _ in range(3)]
        wkv = RE.take(BF16, [128, 8, 256]); r_wkv = P.res()

        dma("gpsimd", ident, ident_d, "c0", writes=[r_ident])
        dma("gpsimd", maskc, maskc_d, "c1", writes=[r_maskc])
        dma("sync", sinkexp, b_sinks.partition_broadcast(128), "c6", writes=[r_sink])
        act(sinkexp, sinkexp, AF.Exp, [r_sink], [r_sink])
        ts("vector", sinkexp, sinkexp, 2.0, None, ALU.mult, None, [r_sink], [r_sink])
        P.op("vector", lambda e: e.memset(ones_bf, 1.0), writes=[r_ones])
        P.op("vector", lambda e: e.memset(negh, -0.5), writes=[r_negh])
        for i in range(3):
            P.op("vector", (lambda v: (lambda e: e.memset(v, 1.0)))(vaug[i]), writes=[r_vaug[i]])

        ga_bc = RF.take(F32, [128, D]); r_ga = P.res()
        Cc = RF.take(F32, [128, 16, 128]); r_C = P.res()
        wsT = RF.take(BF16, [128, 16, 128]); r_wsT = P.res()
        dma("sync", ga_bc, a_norm_g.partition_broadcast(128), "c7", writes=[r_ga])
        wug0 = RF.take(BF16, [128, 8, 2, 128])
        RE.reset()
        wug1 = RE.take(BF16, [128, 8, 2, 128])

        RW.reset()
        xs = [RW.take(F32, [128, D]) for _ in range(2)]; r_xs = [P.res() for _ in range(2)]
        sq = RW.take(BF16, [128, D]); r_sq = P.res()
        hn = RW.take(BF16, [128, D]); r_hn = P.res()
        vn = [RW.take(BF16, [128, AW]) for _ in range(2)]; r_vn = [P.res() for _ in range(2)]
        ident_f = RW.take(F32, [128, 128]); r_identf = P.res()
        lgb_st = RW.take(F32, [16, 2, 128]); r_lgbst = [P.res(), P.res()]
        dma("sync", lgb_st[:, 0, :], a_ln_g.rearrange("(g d) -> g d", g=16), "c4", writes=[r_lgbst[0]])
        dma("sync", lgb_st[:, 1, :], a_ln_b.rearrange("(g d) -> g d", g=16), "c5", writes=[r_lgbst[1]])
        RD.reset()
        vraw = [RD.take(F32, [128, AW]) for _ in range(2)]
        r_vraw = [[P.res() for _ in range(4)] for _ in range(2)]
        ws_st = vraw[0].rearrange("p (g t) -> p g t", g=16); r_wsst_l = r_vraw[0]
        bs_bc = vraw[1].rearrange("p (g t) -> p g t", g=16); r_bs_l = r_vraw[1]
        dma("sync", ws_st, a_ws.rearrange("g t s -> t g s"), "c8", writes=r_wsst_l)
        dma("sync", bs_bc, a_bs.partition_broadcast(128).rearrange("p (g t) -> p g t", g=16), "c9", writes=r_bs_l)
        dma("sync", ident_f, ident_d, "c10", writes=[r_identf])

        Wv = RC.take(BF16, [128, 8, 2048])
        r_Wv = [P.res() for _ in range(4)]
        w_in_r = a_w_in.rearrange("(kc p) n -> p kc n", p=128)
        for s in range(4):
            dma("gpsimd", Wv[:, :, s * 512:(s + 1) * 512], w_in_r[:, :, AW + s * 512:AW + (s + 1) * 512],
                f"wv{s}", writes=[r_Wv[s]])


        def setup_compute():
            for k_, (dst_, rdst_) in enumerate(((lg_pp, r_lg), (lb_pp, r_lb))):
                tr(bank(0)[:, k_ * 16:(k_ + 1) * 16], lgb_st[:, k_, :], ident_f[0:16, 0:16],
                   [r_lgbst[k_], r_identf], [bank_res[0]], True)
                cp("vector", dst_, bank(0)[:, k_ * 16:(k_ + 1) * 16], [bank_res[0]], [rdst_])
            for gq in range(4):
                b = gq % 2
                for gi in range(4):
                    g = gq * 4 + gi
                    tr(bank(b)[:, gi * 128:(gi + 1) * 128], ws_st[:, g, :], ident_f,
                       r_wsst_l + [r_identf], [bank_res[b]], gi == 3)
                tt("vector", wsT[:, gq * 4:(gq + 1) * 4, :],
                   bank(b).rearrange("p (a b) -> p a b", a=4),
                   maskc.unsqueeze(1).to_broadcast([128, 4, 128]), ALU.mult,
                   [bank_res[b], r_maskc], [r_wsT])
            for gq in range(4):
                b = (5, 6, 7, 4)[gq]
                mm(bank(b), ones_bf, wsT[:, gq * 4:(gq + 1) * 4, :].rearrange("p a b -> p (a b)"), True, True,
                   [r_ones, r_wsT], [bank_res[b]], True)
                for gi in range(4):
                    g = gq * 4 + gi
                    stt(Cc[:, g, :], bank(b)[:, gi * 128:(gi + 1) * 128], lb_pp[:, g:g + 1], bs_bc[:, g, :],
                        ALU.mult, ALU.add, [bank_res[b], r_lb] + r_bs_l, [r_C])

        hnT = RB.take(BF16, [128, 8, NCH * 128])
        r_hnT = [P.res() for _ in range(NCH)]
        r_slot = [P.res() for _ in range(NCH)]

        def S_view(j):
            return SLOT.t[:, j * 2048:(j + 1) * 2048].rearrange("p (g t) -> p g t", g=16)

        def h1_view(j):
            return SLOT.t[:, j * 2048:(j + 1) * 2048].bitcast(F32)

        ssA = [stat[:, 0:1], stat[:, 1:2]]
        rstdA = [stat[:, 2:3], stat[:, 3:4]]
        r_ssA = [P.res(), P.res()]
        bnst = [stat[:, 8:32], stat[:, 32:56]]
        mv = [stat[:, 56:58], stat[:, 58:60]]
        rstdv = [stat[:, 60:61], stat[:, 61:62]]
        nmr = [stat[:, 62:63], stat[:, 63:64]]
        r_bn = [P.res(), P.res()]

        def av_front(j):
            p = j % 2
            xb = xs[p]; rxb = r_xs[p]
            dma("sync", xb, x[j], f"x{p}", writes=[rxb])
            act(sq, xb, AF.Square, [rxb], [r_sq, r_ssA[p]], accum=ssA[p])
            ts("vector", rstdA[p], ssA[p], 1.0 / D, EPS, ALU.mult, ALU.add, [r_ssA[p]], [r_ssA[p]])
            P.op("gpsimd", lambda e: e.tensor_tensor(rstdA[p], rstdA[p], negh, ALU.pow),
                 reads=[r_ssA[p], r_negh], writes=[r_ssA[p]])

        def av_front_a2(j):
            p = j % 2
            xb = xs[p]; rxb = r_xs[p]
            stt(hn, xb, rstdA[p], ga_bc, ALU.mult, ALU.mult, [rxb, r_ssA[p], r_ga], [r_hn])

        def av_front_b(j):
            tb = bank_bf(4)
            for kc in range(8):
                tr(tb[:, kc * 128:(kc + 1) * 128], hn[:, kc * 128:(kc + 1) * 128], ident,
                   [r_hn, r_ident], [bank_res[4]], kc == 7)
            cp("scalar", hnT[:, :, j * 128:(j + 1) * 128], tb.rearrange("p (a b) -> p a b", a=8),
               [bank_res[4]], [r_hnT[j]])

        def av_vmm(j, slices=range(4)):
            p = j % 2
            for s in slices:
                for kc in range(8):
                    mm(bank(s), hnT[:, kc, j * 128:(j + 1) * 128], Wv[:, kc, s * 512:(s + 1) * 512],
                       kc == 0, kc == 7, [r_hnT[j], r_Wv[s]], [bank_res[s]], kc == 7)
                if AV_VMM_ACT:
                    cp("scalar", vraw[p][:, s * 512:(s + 1) * 512], bank(s), [bank_res[s]], [r_vraw[p][s]])
                if AV_VMM_BN:
                    P.op("vector", (lambda o, i: (lambda e: e.bn_stats(o, i)))(bnst[p][:, s * 6:(s + 1) * 6],
                                                                               vraw[p][:, s * 512:(s + 1) * 512]),
                         reads=[r_vraw[p][s]], writes=[r_bn[p]])

        def av_mid(j):
            p = j % 2
            P.op("vector", lambda e: e.bn_aggr(mv[p], bnst[p]), reads=[r_bn[p]], writes=[r_bn[p]])
            ts("vector", rstdv[p], mv[p][:, 1:2], EPS, None, ALU.add, None, [r_bn[p]], [r_bn[p]])
            P.op("gpsimd", lambda e: e.tensor_tensor(rstdv[p], rstdv[p], negh, ALU.pow),
                 reads=[r_bn[p], r_negh], writes=[r_bn[p]])
            stt(nmr[p], mv[p][:, 0:1], -1.0, rstdv[p], ALU.mult, ALU.mult, [r_bn[p]], [r_bn[p]])
            if AV_NORM == "pool":
                ts("gpsimd", vn[p], vraw[p], rstdv[p], nmr[p], ALU.mult, ALU.add, r_vraw[p] + [r_bn[p]], [r_vn[p]])
            elif AV_NORM == "dve":
                ts("vector", vn[p], vraw[p], rstdv[p], nmr[p], ALU.mult, ALU.add, r_vraw[p] + [r_bn[p]], [r_vn[p]])
            else:
                act(vn[p], vraw[p], AF.Identity, r_vraw[p] + [r_bn[p]], [r_vn[p]], scale=rstdv[p], bias=nmr[p])

        sv_rot = [0]

        def av_back(j, gqs=range(4)):
            p = j % 2
            Sj = S_view(j)
            for gq in gqs:
                b = (5, 6, 7, 3)[gq] if AV_SVBANK == "v" else 5 + sv_rot[0] % 3
                sv_rot[0] += 1
                for gi in range(4):
                    g = gq * 4 + gi
                    mm(bank(b)[:, gi * 128:(gi + 1) * 128], vn[p][:, g * 128:(g + 1) * 128], wsT[:, g, :],
                       True, True, [r_vn[p], r_wsT], [bank_res[b]], gi == 3)
                for gi in range(4):
                    g = gq * 4 + gi
                    stt(Sj[:, g, :], bank(b)[:, gi * 128:(gi + 1) * 128], lg_pp[:, g:g + 1], Cc[:, g, :],
                        ALU.mult, ALU.add, [bank_res[b], r_lg, r_C], [r_slot[j]])

        r_wug = [P.res() for _ in range(4)]
        r_wug2 = [P.res() for _ in range(4)]
        wug_pre = [wug0, wug1]

        def load_wug_pre(g):
            dma("gpsimd", wug_pre[g][:, :, 0, :], w_in_r[:, :, g * 128:(g + 1) * 128], f"wugu{g}",
                writes=[r_wug[g]])
            dma("gpsimd", wug_pre[g][:, :, 1, :], w_in_r[:, :, 2 * AW + g * 128:2 * AW + (g + 1) * 128],
                f"wugg{g}", writes=[r_wug2[g]])

        if AV_ORDER == 1:
            av_front(0)
            av_front_a2(0)
            av_front_b(0)
            av_front(1)
            setup_compute()
            for j in range(NCH):
                if j + 2 < NCH:
                    av_front(j + 2)
                if j + 1 < NCH:
                    av_front_a2(j + 1)
                av_vmm(j)
                if j + 1 < NCH:
                    av_front_b(j + 1)
                if j >= 1:
                    av_back(j - 1)
                av_mid(j)
                if j == NCH - 4:
                    load_wug_pre(0)
                    load_wug_pre(1)
            av_back(NCH - 1)
        else:
            for j in range(NCH):
                av_front(j)
                av_front_a2(j)
                av_front_b(j)
                if AV_PARTS >= 2:
                    av_vmm(j)
                if AV_PARTS >= 3:
                    av_mid(j)
                if AV_PARTS >= 4:
                    av_back(j)
        if stage == "av":
            for j in range(1, NCH):
                dma("sync", out[j - 1], h1_view(j), "o0", reads=[r_slot[j]])
            P.fence()
            P.emit(nc)
            return nc
        AUG_PB = [6, 4, 0, 2]
        for (j0_, nj_), pb_ in zip([(0, 4), (4, 4)], AUG_PB[:2]):
            n_ = nj_ * 128
            rh_ = [r_hnT[j] for j in range(j0_, j0_ + nj_)]
            for kc in range(8):
                mm(bank(pb_)[:, 0:n_], wug0[:, kc, 0, :], hnT[:, kc, j0_ * 128:j0_ * 128 + n_],
                   kc == 0, kc == 7, rh_ + [r_wug[0]], [bank_res[pb_]], kc == 7)
            for kc in range(8):
                mm(bank(pb_ + 1)[:, 0:n_], wug0[:, kc, 1, :], hnT[:, kc, j0_ * 128:j0_ * 128 + n_],
                   kc == 0, kc == 7, rh_ + [r_wug2[0]], [bank_res[pb_ + 1]], kc == 7)
        P.fence()

        RC.reset()
        Wout = RC.take(BF16, [128, 16, D]); r_Wout = [P.res() for _ in range(2)]
        w_out_r = a_w_out.rearrange("(g p) n -> p g n", p=128)
        RW.reset()
        sgs = [RW.take(F32, [128, 512]) for _ in range(3)]; r_sgs = [P.res() for _ in range(3)]
        tus = [RW.take(F32, [128, 512]) for _ in range(3)]; r_tus = [P.res() for _ in range(3)]
        xs_o = [RW.take(F32, [128, D]) for _ in range(2)]; r_xso = [P.res() for _ in range(2)]
        RD.reset()
        wug = [wug0, wug1] + [RD.take(BF16, [128, 8, 2, 128]) for _ in range(2)]
        batches = [(0, 4), (4, 4), (8, 3), (11, 3), (14, 3)]
        S4 = SLOT.t[:, :].rearrange("p (j g t) -> p j g t", j=NCH, g=16)
        w_in_4 = a_w_in.rearrange("(kc p) (th n) -> p kc th n", p=128, th=3)

        def load_wug(g):
            dma("gpsimd", wug[g % 4][:, :, 0, :], w_in_r[:, :, g * 128:(g + 1) * 128], f"wugu{g % 4}",
                writes=[r_wug[g % 4]])
            dma("gpsimd", wug[g % 4][:, :, 1, :], w_in_r[:, :, 2 * AW + g * 128:2 * AW + (g + 1) * 128],
                f"wugg{g % 4}", writes=[r_wug2[g % 4]])

        it = 0
        for g in range(16):
            if g + 2 < 16:
                load_wug(g + 2)
            if g == 1:
                for hf in range(2):
                    dma("gpsimd", Wout[:, hf * 8:(hf + 1) * 8, :], w_out_r[:, hf * 8:(hf + 1) * 8, :],
                        f"wout{hf}", writes=[r_Wout[hf]])
            if g == 14:
                for jj in range(2):
                    dma("sync", xs_o[jj], x[jj], f"x{jj}", writes=[r_xso[jj]])
            wb = wug[g % 4]; rwb = r_wug[g % 4]; rwb2 = r_wug2[g % 4]
            for (j0, nj) in batches:
                n = nj * 128
                pb = AUG_PB[it % 4]
                pre_issued = it < 2
                sg = sgs[it % 3]; rsg = r_sgs[it % 3]
                tu = tus[it % 3]; rtu = r_tus[it % 3]
                it += 1
                rh = [r_hnT[j] for j in range(j0, j0 + nj)]
                rs = [r_slot[j] for j in range(j0, j0 + nj)]
                for kc in range(8):
                    if pre_issued:
                        break
                    mm(bank(pb)[:, 0:n], wb[:, kc, 0, :], hnT[:, kc, j0 * 128:j0 * 128 + n],
                       kc == 0, kc == 7, rh + [rwb], [bank_res[pb]], kc == 7)
                for kc in range(8):
                    if pre_issued:
                        break
                    mm(bank(pb + 1)[:, 0:n], wb[:, kc, 1, :], hnT[:, kc, j0 * 128:j0 * 128 + n],
                       kc == 0, kc == 7, rh + [rwb2], [bank_res[pb + 1]], kc == 7)
                act(sg[:, 0:n], bank(pb + 1)[:, 0:n], AF.Silu, [bank_res[pb + 1]], [rsg])
                tt("vector", tu[:, 0:n], bank(pb)[:, 0:n], sg[:, 0:n], ALU.mult, [bank_res[pb], rsg], [rtu])
                Sv = S4[:, j0:j0 + nj, g, :]
                tt("gpsimd", Sv, Sv, tu[:, 0:n].rearrange("p (j t) -> p j t", j=nj), ALU.mult,
                   rs + [rtu], rs)
        if stage == "aug":
            for j in range(1, NCH):
                dma("sync", out[j - 1], h1_view(j), "o0", reads=[r_slot[j]])
            P.fence()
            P.emit(nc)
            return nc
        P.fence()

        RF.reset()
        kvg_pp = RF.take(F32, [128, 8]); r_kvg = P.res()
        bg_pp = RF.take(F32, [128, 8]); r_bg = P.res()
        fg_bc = RF.take(F32, [128, D]); r_fg = P.res()
        bq_bc = RF.take(F32, [128, D]); r_bq = P.res()
        bkv_bc = RF.take(F32, [128, 256]); r_bkv = P.res()
        cos_t = RF.take(F32, [128, NCH, 32]); r_cos = P.res()
        sin_t = RF.take(F32, [128, NCH, 32]); r_sin = P.res()
        dma("sync", kvg_pp, kv_norm_g.rearrange("(kc p) -> p kc", p=128), "c4", writes=[r_kvg], slow=True)
        dma("sync", bg_pp, b_norm_g.rearrange("(kc p) -> p kc", p=128), "c5", writes=[r_bg], slow=True)
        dma("sync", fg_bc, final_norm_g.partition_broadcast(128), "c6", writes=[r_fg])
        dma("sync", bq_bc, b_bq.partition_broadcast(128), "c7", writes=[r_bq])
        dma("sync", bkv_bc, b_kv.partition_broadcast(128), "c8", writes=[r_bkv])
        dma("sync", cos_t, cos_d, "c9", writes=[r_cos])
        dma("sync", sin_t, sin_d, "c10", writes=[r_sin])

        RE.reset()
        wkv = RE.take(BF16, [128, 8, 256])
        dma("gpsimd", wkv, w_kv.rearrange("(kc p) n -> p kc n", p=128), "wkv", writes=[r_wkv])
        dma("gpsimd", negc, negc_d, "c2", writes=[r_negc])
        dma("gpsimd", negp, negp_d, "c3", writes=[r_negp])
        dma("gpsimd", negp0, negp0_d, "c11", writes=[r_negp0])
        RB.reset()
        bwin = RB.take(BF16, [128, 8, 2048]); r_bwin = [P.res() for _ in range(4)]
        RD.reset()
        bwout = RD.take(BF16, [128, 8, D]); r_bwout = P.res()
        b_w_in_r = b_w_in.rearrange("(kc p) n -> p kc n", p=128)
        for s in range(4):
            dma("gpsimd", bwin[:, :, s * 512:(s + 1) * 512], b_w_in_r[:, :, s * 512:(s + 1) * 512],
                f"wv{s}", writes=[r_bwin[s]])
        dma("gpsimd", bwout, b_w_out.rearrange("(kc p) n -> p kc n", p=128), "wout0", writes=[r_bwout])
        def fold_gain(step):
            kc = step % 8
            if step < 8:
                ts("vector", wkv[:, kc, :], wkv[:, kc, :], kvg_pp[:, kc:kc + 1], None, ALU.mult, None,
                   [r_wkv, r_kvg], [r_wkv])
            else:
                ts("vector", bwin[:, kc, :], bwin[:, kc, :], bg_pp[:, kc:kc + 1], None, ALU.mult, None,
                   list(r_bwin) + [r_bg], list(r_bwin))
        for j in range(NCH):
            xb = xs_o[j % 2]; rxb = r_xso[j % 2]
            if j >= 2:
                dma("sync", xb, x[j], f"x{j % 2}", writes=[rxb])
            Sj = S_view(j)
            pb = 4 * (j % 2)
            for hf in range(2):
                b = pb + hf
                for g in range(16):
                    mm(bank(b), Sj[:, g, :], Wout[:, g, hf * 512:(hf + 1) * 512], g == 0, g == 15,
                       [r_slot[j], r_Wout[g // 8]], [bank_res[b]], g == 15)
            h1 = h1_view(j)
            for hf in range(2):
                tt("vector", h1[:, hf * 512:(hf + 1) * 512], bank(pb + hf), xb[:, hf * 512:(hf + 1) * 512],
                   ALU.add, [bank_res[pb + hf], rxb], [r_slot[j]])
            if j >= 1:
                fold_gain(j - 1)

        if stage == "h1":
            for j in range(1, NCH):
                dma("sync", out[j - 1], h1_view(j), "o0", reads=[r_slot[j]])
            P.fence()
            P.emit(nc)
            return nc
        P.fence()

        RC.reset(); RW.reset()
        hnkv = RC.take(BF16, [128, D]); r_hnkv = P.res()
        hnb = RC.take(BF16, [128, D]); r_hnb = P.res()
        hnkvT = RC.take(BF16, [128, 8, 128]); r_hnkvT = P.res()
        hnbT = RC.take(BF16, [128, 8, 128]); r_hnbT = P.res()
        qf = RC.take(F32, [128, D]); r_qf = P.res()
        qrot = RC.take(BF16, [128, D]); r_qrot = P.res()
        qT = RC.take(BF16, [128, 8, 128]); r_qT = P.res()
        sgB = [RC.take(F32, [128, D]) for _ in range(2)]; r_sgB = [P.res(), P.res()]
        PT = RC.take(BF16, [128, 2, 16, 128]); r_PT = [[P.res(), P.res()], [P.res(), P.res()]]
        on = qf; r_on = r_qf
        kvf = RW.take(F32, [128, 256]); r_kvf = P.res()
        kz = RW.take(BF16, [128, 2, 2, 128]); r_kz = P.res()
        kTz = [RW.take(BF16, [128, 2, 2, 128]) for _ in range(3)]; r_kTz = [P.res() for _ in range(3)]
        rt = [RW.take(F32, [128, 16, 32]) for _ in range(2)]; r_rt = [P.res() for _ in range(2)]
        rtk = [RW.take(F32, [128, 2, 32]) for _ in range(2)]; r_rtk = [P.res() for _ in range(2)]
        yb = RW.take(BF16, [128, D]); r_yb = P.res()
        yT = RW.take(BF16, [128, 8, 128]); r_yT = P.res()
        sqB = RW.take(BF16, [128, D]); r_sqB = P.res()
        rt2 = [RW.take(F32, [128, 16, 32]) for _ in range(2)]; r_rt2 = [P.res() for _ in range(2)]
        r_qrot_hi = P.res()
        ot = RW.take(F32, [128, D]); r_ot = P.res()
        ssB = [stat[:, 64:65], stat[:, 65:66]]
        rstdB = [stat[:, 66:67], stat[:, 67:68]]
        r_ssB = [P.res(), P.res()]
        ss2 = stat[:, 68:69]
        rstd2 = stat[:, 69:70]
        r_ss2 = P.res()
        den = stat[:, 72:88]; r_den = [P.res(), P.res()]
        r_on2 = [P.res(), P.res()]
        r_yb2 = [[P.res(), P.res()], [P.res(), P.res()]]
        ybs = [yb, hnb]
        P.op("vector", lambda e: e.memset(kz.rearrange("p a b c -> p (a b c)"), 0.0), writes=[r_kz])

        def rotary(eng, src, dst_lo, dst_hi, nh, j, rsrc, rdst, rt, r_rt):
            cb = cos_t[:, j, :].unsqueeze(1).to_broadcast([128, nh, 32])
            sb_ = sin_t[:, j, :].unsqueeze(1).to_broadcast([128, nh, 32])
            x1 = src[:, :, 0:32]
            x2 = src[:, :, 32:64]
            a, b = (rt[i][:, 0:nh, :] for i in range(2))
            tt(eng, a, x1, cb, ALU.mult, rsrc + [r_cos], [r_rt[0]])
            tt(eng, b, x2, sb_, ALU.mult, rsrc + [r_sin], [r_rt[1]])
            tt(eng, dst_lo, a, b, ALU.subtract, [r_rt[0], r_rt[1]], rdst)
            tt(eng, a, x2, cb, ALU.mult, rsrc + [r_cos], [r_rt[0]])
            tt(eng, b, x1, sb_, ALU.mult, rsrc + [r_sin], [r_rt[1]])
            tt(eng, dst_hi, a, b, ALU.add, [r_rt[0], r_rt[1]], rdst)

        def rotary_q(j, src, dst_lo, dst_hi, rsrc):
            nh = 16
            cb = cos_t[:, j, :].unsqueeze(1).to_broadcast([128, nh, 32])
            sb_ = sin_t[:, j, :].unsqueeze(1).to_broadcast([128, nh, 32])
            x1 = src[:, :, 0:32]
            x2 = src[:, :, 32:64]
            a, b = rt[0], rt[1]
            c, d_ = rt2[0], rt2[1]
            tt("gpsimd", a, x1, cb, ALU.mult, rsrc + [r_cos], [r_rt[0]])
            tt("vector", c, x2, cb, ALU.mult, rsrc + [r_cos], [r_rt2[0]])
            tt("gpsimd", b, x2, sb_, ALU.mult, rsrc + [r_sin], [r_rt[1]])
            tt("vector", d_, x1, sb_, ALU.mult, rsrc + [r_sin], [r_rt2[1]])
            tt("gpsimd", dst_lo, a, b, ALU.subtract, [r_rt[0], r_rt[1]], [r_qrot])
            tt("vector", dst_hi, c, d_, ALU.add, [r_rt2[0], r_rt2[1]], [r_qrot_hi])

        def b_a(j):
            p = j % 2
            h1 = h1_view(j)
            act(sqB, h1, AF.Square, [r_slot[j]], [r_sqB, r_ssB[p]], accum=ssB[p])
            ts("vector", rstdB[p], ssB[p], 1.0 / D, EPS, ALU.mult, ALU.add, [r_ssB[p]], [r_ssB[p]])
            P.op("gpsimd", lambda e: e.tensor_tensor(rstdB[p], rstdB[p], negh, ALU.pow),
                 reads=[r_ssB[p], r_negh], writes=[r_ssB[p]])

        def b_a2(j):
            p = j % 2
            h1 = h1_view(j)
            ts("vector", hnkv, h1, rstdB[p], None, ALU.mult, None, [r_slot[j], r_ssB[p]], [r_hnkv])

        def b_b_part(j, part):
            which, half = divmod(part, 4)
            if which == 1:
                return
            src, rsrc, bk, dst, rdst = ((hnkv, r_hnkv, 4, hnkvT, r_hnkvT), (hnb, r_hnb, TRB_HNB, hnbT, r_hnbT))[which]
            tb = bank_bf(bk)
            for kc in range(half * 2, half * 2 + 2):
                tr(tb[:, kc * 128:(kc + 1) * 128], src[:, kc * 128:(kc + 1) * 128], ident,
                   [rsrc, r_ident], [bank_res[bk]], kc == 7)
            if half == 3:
                cp("vector", dst, tb.rearrange("p (a b) -> p a b", a=8), [bank_res[bk]], [rdst])

        def b_b(j):
            for part in range(8):
                b_b_part(j, part)

        def b_c_kv(j):
            for kc in range(8):
                mm(bank(KVB)[:, 0:256], hnkvT[:, kc, :], wkv[:, kc, :], kc == 0, kc == 7,
                   [r_hnkvT, r_wkv], [bank_res[KVB]], kc == 7)
            tt("vector", kvf, bank(KVB)[:, 0:256], bkv_bc, ALU.add, [bank_res[KVB], r_bkv], [r_kvf])
            ksrc = kvf[:, 0:128].rearrange("p (h d) -> p h d", h=2)
            rotary("vector", ksrc, kz[:, :, 0, 0:32], kz[:, :, 0, 32:64], 2, j, [r_kvf], [r_kz], rtk, r_rtk)
            cp("vector", kz[:, :, 1, 64:128], kz[:, :, 0, 0:64], [r_kz], [r_kz])
            va = vaug[j % 3]; rva = r_vaug[j % 3]
            cp("vector", va[:, :, 0:64], kvf[:, 128:256].rearrange("p (h d) -> p h d", h=2), [r_kvf], [rva])

        def b_c_q(j):
            if j >= 1:
                for s in range(4):
                    for kc in range(8):
                        mm(bank(s), hnkvT[:, kc, :], bwin[:, kc, s * 512:(s + 1) * 512], kc == 0, kc == 7,
                           [r_hnkvT, r_bwin[s]], [bank_res[s]], kc == 7)
                sg = sgB[j % 2]; rsg = r_sgB[j % 2]
                for s in range(2):
                    tt("vector", qf[:, s * 512:(s + 1) * 512], bank(s), bq_bc[:, s * 512:(s + 1) * 512], ALU.add,
                       [bank_res[s], r_bq], [r_qf, r_on2[s]])
                for s in range(2):
                    act(sg[:, s * 512:(s + 1) * 512], bank(2 + s), AF.Tanh, [bank_res[2 + s]], [rsg], scale=0.5)
                for s in range(2):
                    stt(sg[:, s * 512:(s + 1) * 512], sg[:, s * 512:(s + 1) * 512], 1.0, bank(2 + s),
                        ALU.add, ALU.mult, [rsg, bank_res[2 + s]], [rsg])
                q3 = qf.rearrange("p (h d) -> p h d", h=16)
                qr3 = qrot.rearrange("p (h d) -> p h d", h=16)
                rotary_q(j, q3, qr3[:, :, 0:32], qr3[:, :, 32:64], [r_qf])

        def b_d(j):
            tb = bank_bf(KZB)
            for hk in range(2):
                for par in range(2):
                    c0 = (hk * 2 + par) * 128
                    tr(tb[:, c0:c0 + 128], kz[:, hk, par, :], ident, [r_kz, r_ident], [bank_res[KZB]],
                       hk == 1 and par == 1)
            cp("scalar", kTz[j % 3], tb[:, 0:512].rearrange("p (a b c) -> p a b c", a=2, b=2),
               [bank_res[KZB]], [r_kTz[j % 3]])
            if j >= 1:
                tb = bank_bf(TRB)
                for kc in range(8):
                    tr(tb[:, kc * 128:(kc + 1) * 128], qrot[:, kc * 128:(kc + 1) * 128], ident,
                       [r_qrot, r_qrot_hi, r_ident], [bank_res[TRB]], kc == 7)
                cp("vector", qT, tb.rearrange("p (a b) -> p a b", a=8), [bank_res[TRB]], [r_qT])

        st_rot = [0]

        def b_f(j, inter=None):
            npair = 0
            for hk in range(2):
                for kb in range(2):
                    jk = j - 1 + kb
                    kTk = kTz[jk % 3]; rkTk = r_kTz[jk % 3]
                    ng = negc if kb == 1 else (negp0 if j == 1 else negp)
                    rng_ = r_negc if kb == 1 else (r_negp0 if j == 1 else r_negp)
                    for par in range(2):
                        b = (8 - ST_RING) + st_rot[0] % ST_RING
                        st_rot[0] += 1
                        ob = bank(b).rearrange("p (a b) -> p a b", a=4)
                        mm(ob, ident, ng, True, False, [r_ident, rng_], [bank_res[b]], False)
                        mm(ob, kTk[:, hk, par, :], qT[:, hk * 4:(hk + 1) * 4, :], False, True,
                           [rkTk, r_qT], [bank_res[b]], True)
                        pv = PT[:, kb, hk * 8 + par:hk * 8 + 8:2, :]
                        act(pv, ob, AF.Exp, [bank_res[b]], [r_PT[kb][hk]], scale=0.125)
                        if inter is not None:
                            inter(npair)
                        npair += 1

        def b_g(j):
            o4 = PS[:, 0:2048].rearrange("p (h c) -> p h c", h=16)
            for h in range(16):
                hk = h // 8
                b = h // 4
                for kb in range(2):
                    jk = j - 1 + kb
                    mm(o4[:, h, 0:65], PT[:, kb, h, :], vaug[jk % 3][:, hk, :], kb == 0, kb == 1,
                       [r_PT[kb][hk], r_vaug[jk % 3]], [bank_res[b]], (kb == 1 and h % 4 == 3))
                if h % 8 == 7:
                    hh = h // 8
                    ro = [bank_res[2 * hh], bank_res[2 * hh + 1]]
                    hs = slice(hh * 8, hh * 8 + 8)
                    cs = slice(hh * 512, hh * 512 + 512)
                    dn = den[:, hs]
                    stt(dn, o4[:, hs, 64], 2.0, sinkexp[:, hs], ALU.mult, ALU.add, ro + [r_sink], [r_den[hh]])
                    P.op("vector", (lambda d_: (lambda e: e.reciprocal(d_, d_)))(dn), reads=[r_den[hh]], writes=[r_den[hh]])
                    tt("vector", on[:, cs].rearrange("p (h d) -> p h d", h=8), o4[:, hs, 0:64],
                       dn.unsqueeze(2).to_broadcast([128, 8, 64]), ALU.mult, ro + [r_den[hh]], [r_on2[hh], r_qf])
                    tt("gpsimd", ybs[j % 2][:, cs], on[:, cs], sgB[j % 2][:, cs], ALU.mult,
                       [r_on2[hh], r_sgB[j % 2]], [r_yb2[j % 2][hh]])

        def b_h(j):
            tb = bank_bf(YTB)
            for kc in range(8):
                tr(tb[:, kc * 128:(kc + 1) * 128], ybs[j % 2][:, kc * 128:(kc + 1) * 128], ident,
                   [r_yb2[j % 2][kc // 4], r_ident], [bank_res[YTB]], kc == 7)
            cp("scalar", yT, tb.rearrange("p (a b) -> p a b", a=8), [bank_res[YTB]], [r_yT])

        def b_i(j):
            h1 = h1_view(j)
            for hf in range(2):
                b = WO_BANKS[hf]
                for kc in range(8):
                    mm(bank(b), yT[:, kc, :], bwout[:, kc, hf * 512:(hf + 1) * 512], kc == 0, kc == 7,
                       [r_yT, r_bwout], [bank_res[b]], kc == 7)
            for hf in range(2):
                b = WO_BANKS[hf]
                tt("vector", h1[:, hf * 512:(hf + 1) * 512], bank(b), h1[:, hf * 512:(hf + 1) * 512], ALU.add,
                   [bank_res[b], r_slot[j]], [r_slot[j]])
            h2 = h1
            r_h2 = r_slot[j]
            act(sqB, h2, AF.Square, [r_h2], [r_sqB, r_ss2], accum=ss2)
            ts("vector", rstd2, ss2, 1.0 / D, EPS, ALU.mult, ALU.add, [r_ss2], [r_ss2])
            P.op("gpsimd", lambda e: e.tensor_tensor(rstd2, rstd2, negh, ALU.pow),
                 reads=[r_ss2, r_negh], writes=[r_ss2])
            stt(ot, h2, rstd2, fg_bc, ALU.mult, ALU.mult, [r_h2, r_ss2, r_fg], [r_ot])
            dma("sync", out[j - 1], ot, "o0", reads=[r_ot])

        b_a(0)
        b_a2(0)
        b_b(0)
        for i in range(NCH + 3):
            if i + 1 < NCH:
                b_a(i + 1)
            if i < NCH:
                b_c_kv(i)
            if i + 1 < NCH:
                b_a2(i + 1)
            if 1 <= i - 3 < NCH:
                b_i(i - 3)
            if i < NCH:
                b_c_q(i)
            if 1 <= i - 1 < NCH:
                if i + 1 < NCH:
                    b_f(i - 1, inter=(lambda jj: (lambda k: b_b_part(jj, k)))(i + 1))
                else:
                    b_f(i - 1)
                b_g(i - 1)
            elif i + 1 < NCH:
                b_b(i + 1)
            if 1 <= i - 2 < NCH:
                b_h(i - 2)
            if i < NCH:
                b_d(i)
        P.fence()
        P.emit(nc)
    return nc


def _host_inputs(inputs):
    x = np.ascontiguousarray(np.asarray(inputs["x"], dtype=np.float32))
    sq = lambda k: np.ascontiguousarray(np.asarray(inputs[k], dtype=np.float32))
    shared = {
        "a_norm_g": sq("a_norm_g")[0], "a_w_in": sq("a_w_in")[0], "a_ln_g": sq("a_ln_g")[0],
        "a_ln_b": sq("a_ln_b")[0], "a_ws": sq("a_ws")[0], "a_bs": sq("a_bs")[0].reshape(-1),
        "a_w_out": sq("a_w_out")[0], "kv_norm_g": sq("kv_norm_g"), "w_kv": sq("w_kv"), "b_kv": sq("b_kv"),
        "b_norm_g": sq("b_norm_g")[0], "b_w_in": sq("b_w_in")[0], "b_bq": sq("b_bq")[0],
        "b_sinks": sq("b_sinks")[0], "b_w_out": sq("b_w_out")[0], "final_norm_g": sq("final_norm_g"),
    }
    shared = {k: np.ascontiguousarray(v) for k, v in shared.items()}
    k_i = np.arange(128)[:, None]
    t_i = np.arange(128)[None, :]
    shared["ident"] = np.eye(128, dtype=np.float32)
    shared["maskc"] = (k_i <= t_i).astype(np.float32)
    NEG = np.float32(-30000.0)
    negc = np.where(k_i <= t_i, np.float32(0), NEG).astype(np.float32)
    negp = np.where(k_i > t_i, np.float32(0), NEG).astype(np.float32)
    rep4 = lambda m: np.ascontiguousarray(np.repeat(m[:, None, :], 4, axis=1))
    shared["negc"] = rep4(negc)
    shared["negp"] = rep4(negp)
    inv_freq = (10000.0 ** (-np.arange(0, 64, 2, dtype=np.float32) / 64)).astype(np.float32)
    in_maps = []
    for c in range(NCORES):
        b, hf = divmod(c, 2)
        xc = np.zeros((NCH, 128, D), np.float32)
        xc[1:] = x[b, hf * 2048:(hf + 1) * 2048].reshape(16, 128, D)
        if hf == 1:
            xc[0] = x[b, 2048 - 128:2048]
        pos = (hf * 2048 - 128 + np.arange(NCH * 128)).astype(np.float32)
        ang = pos[:, None] * inv_freq[None, :]
        cos_t = np.cos(ang).astype(np.float32).reshape(NCH, 128, 32).transpose(1, 0, 2)
        sin_t = np.sin(ang).astype(np.float32).reshape(NCH, 128, 32).transpose(1, 0, 2)
        m = dict(shared)
        m["x"] = xc
        m["cos_t"] = np.ascontiguousarray(cos_t)
        m["sin_t"] = np.ascontiguousarray(sin_t)
        m["negp0"] = shared["negp"] if hf == 1 else np.full((128, 4, 128), NEG, np.float32)
        in_maps.append(m)
    return in_maps


def run(inputs, stage="full"):
    in_maps = _host_inputs(inputs)
    nc = build(stage)
    res = run_bass_kernel_spmd(nc, in_maps, core_ids=list(range(NCORES)))
    outs = [np.asarray(r["out"]).reshape(2048, D) for r in res.results]
    full = np.stack([np.concatenate(outs[2 * b:2 * b + 2], axis=0) for b in range(4)], axis=0)
    return full.astype(np.float32)


def kernel(**inputs):
    return run(inputs, "full")
```
